# Optimizing a Trainium2 kernel written in Bass

```python
import jax, jax.numpy as jnp
from jax import lax
import numpy as np

D_MODEL = 4096
BATCH = 4
SEQ = 2048
DEPTH = 2
DEC_BATCH = 8
DEC_SEQ = 1
PAST_LEN = 16384
PAGE_SIZE = 128

H_A = 4
DK_A = 256
DV_A = 512
GATE_RANK = 16
GATE_TAU = 16.0
CHUNK_A = 16
H_B = 8
DK_B = 128
DV_B = 256
CHUNK_B = 64
ROPE_BASE = 10000.0
WINDOWS = (128, 512, 2048)
DILATIONS = (1, 4, 16)
N_GROUPS_C = 3
G_C = 4
HD_C = 128
NH_C = N_GROUPS_C * G_C
BAND = 128
QK_A = H_A * DK_A
V_A = H_A * DV_A
QK_B = H_B * DK_B
V_B = H_B * DV_B
V_C = G_C * HD_C
C_COLS = N_GROUPS_C * 3 * V_C
IN_SIZES = (QK_A, QK_A, V_A, V_A, GATE_RANK, QK_B, QK_B, V_B, V_B, C_COLS)
N_IN = 2 * QK_A + 2 * V_A + GATE_RANK + 2 * QK_B + 2 * V_B + C_COLS
D_FF = -(-8 * D_MODEL // (3 * 256)) * 256
EPS = 1e-6
NEG = -1e30

kernel_name = 'hybrid_gla_retention_dilated_step'


def rms(x):
    x32 = x.astype(jnp.float32)
    return x32 * lax.rsqrt(jnp.mean(x32 * x32, axis=-1, keepdims=True) + EPS)


def rmsnorm(x, g):
    return (rms(x) * g.astype(jnp.float32)).astype(x.dtype)


def split_columns(h):
    parts, start = [], 0
    for n in IN_SIZES:
        parts.append(h[..., start:start + n])
        start += n
    return parts


def rotary(x, pos):
    half = x.shape[-1] // 2
    inv = ROPE_BASE ** (-jnp.arange(half, dtype=jnp.float32) / half)
    ang = pos.astype(jnp.float32)[:, None] * inv[None, :]
    cos = jnp.cos(ang)[None, :, None, :]
    sin = jnp.sin(ang)[None, :, None, :]
    x32 = x.astype(jnp.float32)
    x1, x2 = x32[..., :half], x32[..., half:]
    return jnp.concatenate([x1 * cos - x2 * sin, x1 * sin + x2 * cos], axis=-1).astype(x.dtype)


def alibi_slopes():
    h = jnp.arange(1, NH_C + 1, dtype=jnp.float32)
    return (2.0 ** (-8.0 * h / NH_C)).reshape(N_GROUPS_C, G_C)


def chunk_gated_recurrence(q, k, v, log_decay, state0, chunk):
    B, T, H, DK = q.shape
    DV = v.shape[-1]
    X = log_decay.shape[-1]
    c = min(chunk, T)
    pad = (-T) % c
    f32 = jnp.float32
    arrs = [a.astype(f32) for a in (q, k, v, log_decay)]
    if pad:
        arrs = [jnp.pad(a, ((0, 0), (0, pad), (0, 0), (0, 0))) for a in arrs]
    n = (T + pad) // c
    qc, kc, vc, gc = [a.reshape(B, n, c, H, a.shape[-1]).transpose(1, 0, 2, 3, 4) for a in arrs]
    causal = jnp.tril(jnp.ones((c, c), dtype=bool))[None, :, :, None, None]

    def step(S, inp):
        qi, ki, vi, gi = inp
        b = jnp.cumsum(gi, axis=1)
        b_last = b[:, -1]
        rel = jnp.exp(jnp.where(causal, b[:, :, None] - b[:, None, :], NEG))
        if X == 1:
            scores = jnp.einsum('bthk,bshk->btsh', qi, ki) * rel[..., 0]
        else:
            scores = jnp.einsum('bthk,bshk,btshk->btsh', qi, ki, rel)
        o = jnp.einsum('btsh,bshv->bthv', scores, vi) + jnp.einsum('bthk,bhkv->bthv', qi * jnp.exp(b), S)
        S = jnp.exp(b_last)[..., None] * S + jnp.einsum('bshk,bshv->bhkv', ki * jnp.exp(b_last[:, None] - b), vi)
        return S, o

    S, o = lax.scan(step, state0.astype(f32), (qc, kc, vc, gc))
    o = o.transpose(1, 0, 2, 3, 4).reshape(B, n * c, H, DV)[:, :T]
    return o.astype(v.dtype), S.astype(state0.dtype)


def band_attention(q, k, v, dil, slopes):
    N, L, G, HD = q.shape
    QB = BAND
    nb = -(-L // QB)
    qpad = nb * QB - L
    kpad = qpad + QB
    qp = jnp.pad(q, ((0, 0), (qpad, 0), (0, 0), (0, 0))).reshape(N, nb, QB, G, HD)

    def key_blocks(a):
        ap = jnp.pad(a, ((0, 0), (kpad, 0), (0, 0), (0, 0))).reshape(N, nb + 1, QB, G, HD)
        return jnp.concatenate([ap[:, :-1], ap[:, 1:]], axis=2)

    kb, vb = key_blocks(k), key_blocks(v)
    u = jnp.arange(QB)[:, None]
    c = jnp.arange(2 * QB)[None, :]
    j = QB + u - c
    src = jnp.arange(nb)[:, None, None] * QB + c[None] - kpad
    valid = (j >= 0) & (j <= BAND) & (src >= 0)
    s = jnp.einsum('nbqgd,nbkgd->nbgqk', qp, kb).astype(jnp.float32) * (HD ** -0.5)
    s = s - slopes[:, None, None] * (dil * j).astype(jnp.float32)
    s = jnp.where(valid[None, :, None], s, NEG)
    m = jnp.max(s, axis=-1, keepdims=True)
    p = jnp.exp(s - m)
    l = jnp.sum(p, axis=-1, keepdims=True)
    o = jnp.einsum('nbgqk,nbkgd->nbqgd', (p / l).astype(v.dtype), vb)
    lse = (m + jnp.log(l))[..., 0]
    o = o.reshape(N, nb * QB, G, HD)[:, qpad:]
    lse = lse.transpose(0, 1, 3, 2).reshape(N, nb * QB, G)[:, qpad:]
    return o, lse


def dilated_prefill(q, k, v, dil, slopes):
    B, S, G, HD = q.shape
    L = S // dil

    def sub(a):
        return a.reshape(B, L, dil, G, HD).transpose(0, 2, 1, 3, 4).reshape(B * dil, L, G, HD)

    o, lse = band_attention(sub(q), sub(k), sub(v), dil, slopes)
    o = o.reshape(B, dil, L, G, HD).transpose(0, 2, 1, 3, 4).reshape(B, S, G, HD)
    lse = lse.reshape(B, dil, L, G).transpose(0, 2, 1, 3).reshape(B, S, G)
    return o, lse


def dilated_step(q, k, v, buf, dil, window, slopes):
    B, T, G, HD = q.shape
    L = buf.shape[2]
    k_ext = jnp.concatenate([buf[:, 0].astype(k.dtype), k], axis=1)
    v_ext = jnp.concatenate([buf[:, 1].astype(v.dtype), v], axis=1)
    j = jnp.arange(BAND + 1)
    idx = L + jnp.arange(T)[:, None] - dil * j[None, :]
    valid = idx >= 0
    idx = jnp.maximum(idx, 0)
    kg = k_ext[:, idx]
    vg = v_ext[:, idx]
    s = jnp.einsum('btgd,btjgd->btgj', q, kg).astype(jnp.float32) * (HD ** -0.5)
    s = s - slopes[:, None] * (dil * j).astype(jnp.float32)[None, :]
    s = jnp.where(valid[None, :, None, :], s, NEG)
    m = jnp.max(s, axis=-1, keepdims=True)
    p = jnp.exp(s - m)
    l = jnp.sum(p, axis=-1, keepdims=True)
    o = jnp.einsum('btgj,btjgd->btgd', (p / l).astype(v.dtype), vg)
    lse = (m + jnp.log(l))[..., 0]
    new_len = min(window, L + T)
    new_buf = jnp.stack([k_ext[:, L + T - new_len:], v_ext[:, L + T - new_len:]], axis=1)
    return o, lse, new_buf


def dilated_branch(hc, win_bufs):
    B, T, _ = hc.shape
    hc = hc.reshape(B, T, N_GROUPS_C, 3, G_C, HD_C)
    slopes = alibi_slopes()
    outs, lses, bufs = [], [], []
    for g in range(N_GROUPS_C):
        q, k, v = hc[:, :, g, 0], hc[:, :, g, 1], hc[:, :, g, 2]
        if win_bufs is None:
            o, lse = dilated_prefill(q, k, v, DILATIONS[g], slopes[g])
            keep = min(WINDOWS[g], T)
            buf = jnp.stack([k[:, T - keep:], v[:, T - keep:]], axis=1)
        else:
            o, lse, buf = dilated_step(q, k, v, win_bufs[g], DILATIONS[g], WINDOWS[g], slopes[g])
        outs.append(o)
        lses.append(lse)
        bufs.append(buf)
    w = jax.nn.softmax(jnp.stack(lses, axis=-1), axis=-1)
    o = jnp.einsum('btgdn,btgn->btgd', jnp.stack(outs, axis=-1).astype(jnp.float32), w).astype(hc.dtype)
    return o.reshape(B, T, V_C), bufs


def trunk_layer(x, pos0, gla_state0, ret_state0, win_bufs, g_mix, w_in, w_alpha2, b_alpha, g_gla,
                w_merge, w_up_a, w_up_b, w_up_c, w_out, g_ffn, w_ffn_gate, w_ffn_up, w_ffn_down):
    B, T, _ = x.shape
    xn = rmsnorm(x, g_mix)
    qa, ka, va, ra, za, qb, kb, vb, gb, hc = split_columns(xn @ w_in)
    qa = qa.reshape(B, T, H_A, DK_A) * (DK_A ** -0.5)
    ka = ka.reshape(B, T, H_A, DK_A)
    va = va.reshape(B, T, H_A, DV_A)
    log_alpha = jax.nn.log_sigmoid((za @ w_alpha2 + b_alpha).astype(jnp.float32)) / GATE_TAU
    log_alpha = log_alpha.reshape(B, T, H_A, DK_A)
    oa, gla_state = chunk_gated_recurrence(qa, ka, va, log_alpha, gla_state0, CHUNK_A)
    oa = jax.nn.silu(ra) * rmsnorm(oa, g_gla).reshape(B, T, V_A)
    pos = pos0 + jnp.arange(T)
    qb = rotary(qb.reshape(B, T, H_B, DK_B), pos)
    kb = rotary(kb.reshape(B, T, H_B, DK_B), pos) * (DK_B ** -0.5)
    vb = vb.reshape(B, T, H_B, DV_B)
    log_gamma = jnp.log1p(-(2.0 ** (-5.0 - jnp.arange(H_B, dtype=jnp.float32))))
    log_decay = jnp.broadcast_to(log_gamma[None, None, :, None], (B, T, H_B, 1))
    ob, ret_state = chunk_gated_recurrence(qb, kb, vb, log_decay, ret_state0, CHUNK_B)
    ob = jax.nn.silu(gb) * rms(ob).astype(x.dtype).reshape(B, T, V_B)
    oc, new_bufs = dilated_branch(hc, win_bufs)
    gate_a, gate_b, gate_c = jnp.split(jax.nn.sigmoid(xn @ w_merge), 3, axis=-1)
    merged = gate_a * (oa @ w_up_a) + gate_b * (ob @ w_up_b) + gate_c * (oc @ w_up_c)
    x = x + merged @ w_out
    xf = rmsnorm(x, g_ffn)
    x = x + (jax.nn.silu(xf @ w_ffn_gate) * (xf @ w_ffn_up)) @ w_ffn_down
    return x, gla_state, ret_state, new_bufs


def setup_inputs(seed: int = 0) -> dict:
    key = jax.random.key(seed)
    ks = jax.random.split(key, 24)
    f32 = jnp.float32

    def nrm(k, shape, scale):
        return jax.random.normal(k, shape, f32) * scale

    def gain(k, shape):
        return 1.0 + 0.01 * jax.random.normal(k, shape, f32)

    return {
        'x_prompt': nrm(ks[0], (BATCH, SEQ, D_MODEL), 1.0),
        'x_sample': nrm(ks[1], (DEC_BATCH, DEC_SEQ, D_MODEL), 1.0),
        'state_gla': nrm(ks[2], (DEPTH, DEC_BATCH, H_A, DK_A, DV_A), 0.5),
        'state_ret': nrm(ks[3], (DEPTH, DEC_BATCH, H_B, DK_B, DV_B), 0.5),
        'cache_win0': nrm(ks[4], (DEPTH, DEC_BATCH, 2, min(WINDOWS[0], PAST_LEN), G_C, HD_C), 1.0),
        'cache_win1': nrm(ks[5], (DEPTH, DEC_BATCH, 2, min(WINDOWS[1], PAST_LEN), G_C, HD_C), 1.0),
        'cache_win2': nrm(ks[6], (DEPTH, DEC_BATCH, 2, min(WINDOWS[2], PAST_LEN), G_C, HD_C), 1.0),
        'g_mix': gain(ks[7], (DEPTH, D_MODEL)),
        'w_in': nrm(ks[8], (DEPTH, D_MODEL, N_IN), D_MODEL ** -0.5),
        'w_alpha2': nrm(ks[9], (DEPTH, GATE_RANK, QK_A), GATE_RANK ** -0.5),
        'b_alpha': nrm(ks[10], (DEPTH, QK_A), 0.1),
        'g_gla': gain(ks[11], (DEPTH, DV_A)),
        'w_merge': nrm(ks[12], (DEPTH, D_MODEL, 3 * D_MODEL), D_MODEL ** -0.5),
        'w_up_a': nrm(ks[13], (DEPTH, V_A, D_MODEL), V_A ** -0.5),
        'w_up_b': nrm(ks[14], (DEPTH, V_B, D_MODEL), V_B ** -0.5),
        'w_up_c': nrm(ks[15], (DEPTH, V_C, D_MODEL), V_C ** -0.5),
        'w_out': nrm(ks[16], (DEPTH, D_MODEL, D_MODEL), D_MODEL ** -0.5),
        'g_ffn': gain(ks[17], (DEPTH, D_MODEL)),
        'w_ffn_gate': nrm(ks[18], (DEPTH, D_MODEL, D_FF), D_MODEL ** -0.5),
        'w_ffn_up': nrm(ks[19], (DEPTH, D_MODEL, D_FF), D_MODEL ** -0.5),
        'w_ffn_down': nrm(ks[20], (DEPTH, D_FF, D_MODEL), D_FF ** -0.5),
        'g_final': gain(ks[21], (D_MODEL,)),
    }


def reference(x_prompt, x_sample, state_gla, state_ret, cache_win0, cache_win1, cache_win2,
              g_mix, w_in, w_alpha2, b_alpha, g_gla, w_merge, w_up_a, w_up_b, w_up_c, w_out,
              g_ffn, w_ffn_gate, w_ffn_up, w_ffn_down, g_final):
    xp, xs = x_prompt, x_sample
    Bp = x_prompt.shape[0]
    gla_p, gla_s, ret_p, ret_s = [], [], [], []
    win_p = [[], [], []]
    win_s = [[], [], []]
    for l in range(DEPTH):
        weights = (g_mix[l], w_in[l], w_alpha2[l], b_alpha[l], g_gla[l], w_merge[l], w_up_a[l], w_up_b[l],
                   w_up_c[l], w_out[l], g_ffn[l], w_ffn_gate[l], w_ffn_up[l], w_ffn_down[l])
        zeros_a = jnp.zeros((Bp, H_A, DK_A, DV_A), xp.dtype)
        zeros_b = jnp.zeros((Bp, H_B, DK_B, DV_B), xp.dtype)
        xp, sa, sb, bufs_p = trunk_layer(xp, 0, zeros_a, zeros_b, None, *weights)
        xs, sa2, sb2, bufs_s = trunk_layer(xs, PAST_LEN, state_gla[l], state_ret[l],
                                           (cache_win0[l], cache_win1[l], cache_win2[l]), *weights)
        gla_p.append(sa)
        gla_s.append(sa2)
        ret_p.append(sb)
        ret_s.append(sb2)
        for g in range(N_GROUPS_C):
            win_p[g].append(bufs_p[g])
            win_s[g].append(bufs_s[g])
    y_prompt = rmsnorm(xp, g_final)
    y_sample = rmsnorm(xs, g_final)
    return (y_prompt, y_sample, jnp.stack(gla_p), jnp.stack(gla_s), jnp.stack(ret_p), jnp.stack(ret_s),
            jnp.stack(win_p[0]), jnp.stack(win_s[0]), jnp.stack(win_p[1]), jnp.stack(win_s[1]),
            jnp.stack(win_p[2]), jnp.stack(win_s[2]))
```

```python
import numpy as np
import ml_dtypes
import concourse.bass as bass
import concourse.mybir as mybir
from concourse.bass_utils import run_bass_kernel_spmd

F32 = mybir.dt.float32
BF16 = mybir.dt.bfloat16
AF = mybir.ActivationFunctionType
ALU = mybir.AluOpType
AX = mybir.AxisListType

D = 4096
SEQ = 2048
DEPTH = 2
NT = 512
NTILES = SEQ // NT
KC = D // 128
H_A, DK_A, DV_A = 4, 256, 512
H_B, DK_B, DV_B = 8, 128, 256
WINDOWS = (128, 512, 2048)
DILS = (1, 4, 16)
N_IN = 16912
D_FF = 11008
FKC = D_FF // 128
PAST = 16384
EPS = 1e-6
O_QA, O_KA, O_VA, O_RA, O_ZA, O_QB, O_KB, O_VB, O_GB, O_HC = 0, 1024, 2048, 4096, 6144, 6160, 7184, 8208, 10256, 12304
TJ_OFF = (0, 256, 896)
NS = 4
SLOT = 4096


def _esz(dt):
    return 4 if dt == F32 else 2


class Buf:
    def __init__(self, t, name, tdt, dt, base_bytes, n, space="sb"):
        self.t, self.name, self.tdt, self.dt, self.base, self.n, self.space = t, name, tdt, dt, base_bytes, n, space
        self.esz = _esz(dt)

    def ap(self, lo=0, hi=None, p0=0, p1=128):
        hi = self.n if hi is None else hi
        b0 = self.base + lo * self.esz
        b1 = self.base + hi * self.esz
        te = _esz(self.tdt)
        assert b0 % te == 0 and b1 % te == 0
        v = self.t[p0:p1, b0 // te:b1 // te]
        if self.dt != self.tdt:
            v = v.bitcast(self.dt)
        return v

    def reg(self, lo=0, hi=None):
        hi = self.n if hi is None else hi
        return (self.name, self.base + lo * self.esz, self.base + hi * self.esz)


class Sched:
    def __init__(self, nc, sems):
        self.nc = nc
        self.ops = {e: [] for e in ("pe", "dve", "act", "pool", "sp")}
        self.sem = sems
        self.cnt = {e: 0 for e in ("pe", "dve", "act")}
        self.waited = {}
        self.track = {}
        self.sp_ch = 0
        self.sp_cnt = [0] * 8
        self.pool_cnt = [0] * NS

    def _deps(self, regs_r, regs_w):
        deps = set()
        for (name, lo, hi) in regs_r:
            for rec in self.track.get(name, []):
                if rec[2] == "w" and rec[0] < hi and lo < rec[1]:
                    deps.add(rec[3])
        for (name, lo, hi) in regs_w:
            for rec in self.track.get(name, []):
                if rec[0] < hi and lo < rec[1]:
                    deps.add(rec[3])
        return deps

    def _record(self, regs_r, regs_w, tag):
        for (name, lo, hi) in regs_w:
            lst = self.track.setdefault(name, [])
            new = []
            for rec in lst:
                if rec[0] >= lo and rec[1] <= hi:
                    continue
                new.append(rec)
            new.append([lo, hi, "w", tag])
            self.track[name] = new
        for (name, lo, hi) in regs_r:
            lst = self.track.setdefault(name, [])
            new = []
            for rec in lst:
                if rec[2] == "r" and rec[0] >= lo and rec[1] <= hi and rec[3][0] == tag[0]:
                    continue
                new.append(rec)
            new.append([lo, hi, "r", tag])
            self.track[name] = new

    def _waits(self, eng, deps):
        best = {}
        for (sn, val) in deps:
            if val > best.get(sn, 0):
                best[sn] = val
        out = []
        for sn, val in best.items():
            if self.waited.get((eng, sn), 0) < val:
                self.waited[(eng, sn)] = val
                out.append((sn, val))
        return out

    def op(self, eng, emit, r=(), w=()):
        rr = [b.reg(lo, hi) for (b, lo, hi) in r]
        ww = [b.reg(lo, hi) for (b, lo, hi) in w]
        waits = self._waits(eng, self._deps(rr, ww))
        self.cnt[eng] += 1
        tag = ("s_" + eng, self.cnt[eng])
        self.ops[eng].append((waits, emit, ("s_" + eng, 1)))
        self._record(rr, ww, tag)

    def pe(self, emit, r=(), w=()):
        self.op("pe", emit, r, w)

    def dve(self, emit, r=(), w=()):
        self.op("dve", emit, r, w)

    def act(self, emit, r=(), w=()):
        self.op("act", emit, r, w)

    def dma(self, q, out, in_, r=(), w=(), slot=None):
        rr = [x if isinstance(x[0], str) else x[0].reg(x[1], x[2]) for x in r]
        ww = [x if isinstance(x[0], str) else x[0].reg(x[1], x[2]) for x in w]
        deps = self._deps(rr, ww)
        if q == "pool":
            ch = slot
            sn = "s_w%d" % ch
            prev = self.pool_cnt[ch]
            self.pool_cnt[ch] += 1
            val = 16 * self.pool_cnt[ch]
        else:
            ch = self.sp_ch
            self.sp_ch = (self.sp_ch + 1) % 8
            sn = "s_d%d" % ch
            prev = self.sp_cnt[ch]
            self.sp_cnt[ch] += 1
            val = 16 * self.sp_cnt[ch]
        if prev:
            deps.add((sn, 16 * prev))
        waits = self._waits(q, deps)
        self.ops[q].append((waits, (lambda e, out=out, in_=in_: e.dma_start(out=out, in_=in_)), (sn, 16)))
        self._record(rr, ww, (sn, val))

    def final_waits(self):
        out = []
        for ch in range(8):
            if self.sp_cnt[ch]:
                out.append(("s_d%d" % ch, 16 * self.sp_cnt[ch]))
        for ch in range(NS):
            if self.pool_cnt[ch]:
                out.append(("s_w%d" % ch, 16 * self.pool_cnt[ch]))
        return out

    def replay(self, block):
        nc = self.nc
        sem = self.sem

        def run(e, lst, extra=None):
            for (waits, emit, inc) in lst:
                for (sn, val) in waits:
                    e.wait_ge(sem[sn], val)
                ins = emit(e)
                ins.then_inc(sem[inc[0]], inc[1])
            if extra:
                for (sn, val) in extra:
                    e.wait_ge(sem[sn], val)

        fw = self.final_waits()

        @block.tensor
        def _(e):
            run(e, self.ops["pe"])

        @block.vector
        def _(e):
            run(e, self.ops["dve"])

        @block.scalar
        def _(e):
            run(e, self.ops["act"])

        @block.gpsimd
        def _(e):
            run(e, self.ops["pool"])

        @block.sync
        def _(e):
            run(e, self.ops["sp"], extra=fw)


class Arena:
    def __init__(self, t, name, tdt, nbytes, base=0):
        self.t, self.name, self.tdt, self.nbytes, self.base = t, name, tdt, nbytes, base
        self.off = 0

    def reset(self, off=0):
        self.off = off

    def alloc(self, dt, n):
        e = _esz(dt)
        self.off = (self.off + 3) // 4 * 4
        b = Buf(self.t, self.name, self.tdt, dt, self.base + self.off, n)
        self.off += n * e
        assert self.off <= self.nbytes, (self.name, self.off, self.nbytes)
        return b


def alibi_slope(g, i):
    return 2.0 ** (-8.0 * (g * 4 + i + 1) / 12.0)


def make_consts():
    c = {}
    c["identf"] = np.eye(128, dtype=np.float32)
    s = np.arange(128)
    c["mask01"] = (s[:, None] <= s[None, :]).astype(np.float32)
    perm = np.zeros((128, 128), np.float32)
    for m in range(128):
        perm[(m + 64) % 128, m] = 1.0
    c["permf"] = perm
    half = 64
    inv = (10000.0 ** (-np.arange(half, dtype=np.float32) / half)).astype(np.float32)
    pos = np.concatenate([np.arange(SEQ), [PAST]]).astype(np.float32)
    ang = (pos[None, :] * inv[:, None]).astype(np.float32).astype(np.float64)
    cos = np.cos(ang)
    sin = np.sin(ang)
    c["cost"] = np.concatenate([cos, cos], 0).astype(np.float32)
    c["sint"] = np.concatenate([-sin, sin], 0).astype(np.float32)
    lg = np.log1p(-(2.0 ** (-5.0 - np.arange(H_B, dtype=np.float64))))
    t = np.arange(128, dtype=np.float64)
    ep = np.exp(lg[:, None] * (t[None, :] + 1))
    em = np.exp(-lg[:, None] * (t[None, :] + 1)) * (DK_B ** -0.5)
    c["rEp"] = np.broadcast_to(ep[None], (128, 8, 128)).reshape(128, 1024).astype(np.float32).copy()
    c["rEm"] = np.broadcast_to(em[None], (128, 8, 128)).reshape(128, 1024).astype(np.float32).copy()
    g128 = np.exp(lg * 128)
    g1 = np.exp(lg)
    c["gam"] = np.broadcast_to(np.concatenate([g128, g1])[None], (128, 16)).astype(np.float32).copy()
    tj = np.full((128, 3072), 1e30, np.float64)
    p = np.arange(128)[:, None]
    for g in range(3):
        W, d = WINDOWS[g], DILS[g]
        cc = np.arange(W + 128)[None, :]
        delta = p + W - cc
        valid = (delta >= 0) & (delta % d == 0) & (delta // d <= 128)
        tj[:, TJ_OFF[g]:TJ_OFF[g] + W + 128] = np.where(valid, delta // d, 1e30)
    c["tj"] = tj.astype(np.float32)
    c["onesf"] = np.ones((128, 128), np.float32)
    c["csrow"] = np.concatenate([cos[:, SEQ], sin[:, SEQ]])[None, :].astype(np.float32)
    sb_ = np.zeros((128, 12), np.float64)
    r = np.arange(128)
    for g in range(3):
        for i in range(4):
            sb_[:, g * 4 + i] = -alibi_slope(g, i) * DILS[g] * (128 - r)
    c["sbias"] = sb_.astype(np.float32)
    return c


class _Stop(Exception):
    pass


def build_program(n_layers=DEPTH, n_tiles=NTILES, stop=None, skip_sample=False):
    def ck(name):
        if stop == name:
            raise _Stop()

    nc = bass.Bass("TRN2", target_bir_lowering=False)

    def din(name, shape):
        return nc.dram_tensor(name, list(shape), F32, kind="ExternalInput").ap()

    def dout(name, shape):
        return nc.dram_tensor(name, list(shape), F32, kind="ExternalOutput").ap()

    xp = din("xp", (SEQ, D))
    xs = din("xs", (1, D))
    sgla = din("sgla", (DEPTH, H_A, DK_A, DV_A))
    sret = din("sret", (DEPTH, H_B, DK_B, DV_B))
    cw = [din("cw%d" % g, (DEPTH, 2, WINDOWS[g], 512)) for g in range(3)]
    g_mix = din("g_mix_t", (DEPTH, 128, KC))
    w_in = din("w_in", (DEPTH, D, N_IN))
    w_alpha2 = din("w_alpha2", (DEPTH, 16, 1024))
    b_alpha = din("b_alpha_t", (DEPTH, 128, 8))
    g_gla = din("g_gla", (DEPTH, 512))
    w_merge = din("w_merge", (DEPTH, D, 3 * D))
    w_up_a = din("w_up_a", (DEPTH, 2048, D))
    w_up_b = din("w_up_b", (DEPTH, 2048, D))
    w_up_c = din("w_up_c", (DEPTH, 512, D))
    w_out = din("w_out", (DEPTH, D, D))
    g_ffn = din("g_ffn_t", (DEPTH, 128, KC))
    w_fg = din("w_ffn_gate", (DEPTH, D, D_FF))
    w_fu = din("w_ffn_up", (DEPTH, D, D_FF))
    w_fd = din("w_ffn_down", (DEPTH, D_FF, D))
    g_final = din("g_final_t", (128, KC))
    g_final_row = din("g_final_row", (D,))
    g_mix_r = din("g_mix_r", (DEPTH, D))
    g_ffn_r = din("g_ffn_r", (DEPTH, D))
    b_alpha_r = din("b_alpha_r", (DEPTH, 1024))
    cn = {k: din("c_" + k, v.shape) for k, v in make_consts().items()}

    yp = dout("yp", (SEQ, D))
    ys = dout("ys", (1, D))
    gla_p = dout("gla_p", (DEPTH, H_A, DK_A, DV_A))
    gla_s = dout("gla_s", (DEPTH, H_A, DK_A, DV_A))
    ret_p = dout("ret_p", (DEPTH, H_B, DK_B, DV_B))
    ret_s = dout("ret_s", (DEPTH, H_B, DK_B, DV_B))
    wp = [dout("w%dp" % g, (DEPTH, 2, WINDOWS[g], 512)) for g in range(3)]
    ws = [dout("w%ds" % g, (DEPTH, 2, WINDOWS[g], 512)) for g in range(3)]

    xa = nc.dram_tensor("xa", [SEQ, D], F32).ap()
    xb = nc.dram_tensor("xb", [SEQ, D], F32).ap()
    kts = [nc.dram_tensor("kts%d" % g, [128, 4 * SEQ], BF16).ap() for g in range(3)]
    vsc = [nc.dram_tensor("vsc%d" % g, [SEQ, 512], BF16).ap() for g in range(3)]
    xsa = nc.dram_tensor("xsa", [1, D], F32).ap()
    xsb = nc.dram_tensor("xsb", [1, D], F32).ap()

    sem_names = ["s_pe", "s_dve", "s_act"] + ["s_d%d" % i for i in range(8)] + ["s_w%d" % i for i in range(NS)]
    import contextlib
    with contextlib.ExitStack() as es:
        def sb(name, n, dt):
            return es.enter_context(nc.sbuf_tensor(name, [128, n], dt))

        t_xnT = sb("xnT", KC * NT, BF16)
        t_wsl = sb("wsl", NS * SLOT, BF16)
        ARENA_B = FKC * NT * 2
        t_arena = sb("arena", ARENA_B // 2, BF16)
        t_SA = sb("SA", 8 * 512, F32)
        t_SB = sb("SB", 8 * 256, F32)
        t_cf = sb("cf", 4864, F32)
        t_tj = sb("tj", 3072, BF16)
        t_misc = sb("misc", 1792, F32)
        t_ps = es.enter_context(nc.psum_tensor("ps", [128, 8 * 512], F32))
        sems = {n: es.enter_context(nc.semaphore(n)) for n in sem_names}
        block = es.enter_context(nc.Block())
        S = Sched(nc, sems)

        xnT = Buf(t_xnT, "xnT", BF16, BF16, 0, KC * NT)
        wsl = Buf(t_wsl, "wsl", BF16, BF16, 0, NS * SLOT)
        SA = Buf(t_SA, "SA", F32, F32, 0, 4096)
        SB_ = Buf(t_SB, "SB", F32, F32, 0, 2048)
        tjb = Buf(t_tj, "tj", BF16, BF16, 0, 3072)
        cfA = Arena(t_cf, "cf", F32, 4864 * 4)
        identf = cfA.alloc(F32, 128)
        mask01 = cfA.alloc(F32, 128)
        permf = cfA.alloc(F32, 128)
        onesf = cfA.alloc(F32, 128)
        cosb = cfA.alloc(F32, 512)
        sinb = cfA.alloc(F32, 512)
        rEp = cfA.alloc(F32, 1024)
        rEm = cfA.alloc(F32, 1024)
        gam = cfA.alloc(F32, 16)
        gmix = cfA.alloc(F32, 32)
        gffn = cfA.alloc(F32, 32)
        gfin = cfA.alloc(F32, 32)
        ggla = cfA.alloc(F32, 512)
        negb = cfA.alloc(F32, 8)
        identb = cfA.alloc(BF16, 128)
        wa2 = cfA.alloc(BF16, 1024)
        mA = Arena(t_misc, "misc", F32, 1792 * 4)
        stat = mA.alloc(F32, 64)
        arena = Arena(t_arena, "arena", BF16, ARENA_B)
        psb = [Buf(t_ps, "ps", F32, F32, i * 2048, 512, space="ps") for i in range(8)]

        wslot_ptr = [0]

        def PS(i, lo=0, hi=512, p0=0, p1=128):
            return psb[i].ap(lo, hi, p0, p1)

        def PSr(i, lo=0, hi=512):
            return (psb[i], 0, 512)

        try:
            def cload(buf, src, n=None, q="sp"):
                S.dma(q, buf.ap(0, n), src, w=[(buf, 0, n if n else buf.n)])

            cload(identf, cn["identf"])
            cload(mask01, cn["mask01"])
            cload(permf, cn["permf"])
            cload(onesf, cn["onesf"])
            cload(rEp, cn["rEp"])
            cload(rEm, cn["rEm"])
            cload(gam, cn["gam"])
            cload(gfin, g_final)
            S.act(lambda e: e.activation(out=identb.ap(), in_=identf.ap(), func=AF.Copy), r=[(identf, 0, 128)], w=[(identb, 0, 128)])
            S.dma("pool", tjb.ap().rearrange("p (a b) -> p a b", b=512), cn["tj"].rearrange("p (a b) -> p a b", b=512), w=[(tjb, 0, 3072)], slot=0)

            def wload(src3, k, m):
                s = wslot_ptr[0]
                wslot_ptr[0] = (s + 1) % NS
                lo, hi = s * SLOT, s * SLOT + k * m
                dst = wsl.ap(lo, hi).rearrange("p (k c) -> p k c", c=m)
                S.dma("pool", dst, src3, w=[(wsl, lo, hi)], slot=s)
                return dst, (wsl, lo, hi)

            def wview(wmat):
                return wmat.rearrange("(k p) c -> p k c", p=128)

            def mm_group(out_ap, lhs_fn, rhs_fn, nk, r, w, first=True, last=True):
                def emit(e):
                    ins = None
                    for k in range(nk):
                        ins = e.matmul(out_ap, lhsT=lhs_fn(k), rhs=rhs_fn(k), start=(first and k == 0), stop=(last and k == nk - 1))
                    return ins
                S.pe(emit, r=r, w=w)

            def fm_dense(wv, c0, m, kcn, act_fn, act_reg, out_bank, n=NT, mp=None):
                sv, sreg = wload(wv[:, 0:kcn, c0:c0 + m], kcn, m)
                mm_group(PS(out_bank, 0, n, 0, m), lambda k: sv[:, k, :], act_fn, kcn, r=[sreg, act_reg], w=[PSr(out_bank, 0, n)])

            def xn_fn(k):
                return xnT.ap(k * NT, (k + 1) * NT)
            xn_reg = (xnT, 0, KC * NT)

            def tm_dense(wv, c0, kcn, lhs_fn, lhs_reg, banks, ncols=512):
                nparts = (kcn + 7) // 8
                for kp in range(nparts):
                    k0 = kp * 8
                    kn = min(8, kcn - k0)
                    sv, sreg = wload(wv[:, k0:k0 + kn, c0:c0 + ncols], kn, ncols)
                    for tc in range(4):
                        mm_group(PS(banks[tc], 0, ncols), (lambda k, tc=tc, k0=k0: lhs_fn(k0 + k, tc)), (lambda k, sv=sv: sv[:, k, :]), kn,
                                 r=[sreg, lhs_reg], w=[PSr(banks[tc], 0, ncols)], first=(kp == 0), last=(kp == nparts - 1))

            def xn_lhs(k, tc):
                return xnT.ap(k * NT + tc * 128, k * NT + (tc + 1) * 128)

            def norm_tile(src, t0, gtab, rowbuf):
                for tc in range(4):
                    r0 = t0 + tc * 128
                    S.dma("sp", rowbuf.ap(), src[r0:r0 + 128, :], r=[("dram_" + src.tensor.name, r0, r0 + 128)], w=[(rowbuf, 0, D)])
                    ssq = (stat, tc * 4, tc * 4 + 1)
                    sd = (stat, tc * 4 + 1, tc * 4 + 2)
                    rs = (stat, tc * 4 + 2, tc * 4 + 3)
                    junk = xnjunk
                    S.act(lambda e, tc=tc: e.activation(out=junk.ap(), in_=rowbuf.ap(), func=AF.Square, accum_out=stat.ap(tc * 4, tc * 4 + 1)),
                          r=[(rowbuf, 0, D)], w=[(junk, 0, D), ssq])
                    S.act(lambda e, tc=tc: e.activation(out=stat.ap(tc * 4 + 1, tc * 4 + 2), in_=stat.ap(tc * 4, tc * 4 + 1), func=AF.Sqrt, bias=EPS, scale=1.0 / D),
                          r=[ssq], w=[sd])
                    S.dve(lambda e, tc=tc: e.reciprocal(out=stat.ap(tc * 4 + 2, tc * 4 + 3), in_=stat.ap(tc * 4 + 1, tc * 4 + 2)), r=[sd], w=[rs])
                    S.dve(lambda e, tc=tc: e.tensor_scalar(out=rowbuf.ap(), in0=rowbuf.ap(), scalar1=stat.ap(tc * 4 + 2, tc * 4 + 3), scalar2=None, op0=ALU.mult),
                          r=[(rowbuf, 0, D), rs], w=[(rowbuf, 0, D)])
                    for k4 in range(8):
                        bank = k4 % 2
                        def emit(e, k4=k4, bank=bank):
                            ins = None
                            for j in range(4):
                                k = k4 * 4 + j
                                ins = e.transpose(out=PS(bank, j * 128, (j + 1) * 128), in_=rowbuf.ap(k * 128, (k + 1) * 128), identity=identf.ap())
                            return ins
                        S.pe(emit, r=[(rowbuf, k4 * 512, (k4 + 1) * 512), (identf, 0, 128)], w=[PSr(bank)])
                        for j in range(4):
                            k = k4 * 4 + j
                            eng = S.dve if (j % 2 == 0) else S.act
                            if False:
                                S.dve(lambda e, k=k, j=j, bank=bank, tc=tc: e.tensor_scalar(out=xnT.ap(k * NT + tc * 128, k * NT + (tc + 1) * 128), in0=PS(bank, j * 128, (j + 1) * 128),
                                                                                         scalar1=gtab.ap(k, k + 1), scalar2=None, op0=ALU.mult),
                                      r=[PSr(bank, j * 128, (j + 1) * 128), (gtab, k, k + 1)], w=[(xnT, k * NT + tc * 128, k * NT + (tc + 1) * 128)])
                            else:
                                S.act(lambda e, k=k, j=j, bank=bank, tc=tc: e.activation(out=xnT.ap(k * NT + tc * 128, k * NT + (tc + 1) * 128), in_=PS(bank, j * 128, (j + 1) * 128),
                                                                                      func=AF.Copy, scale=gtab.ap(k, k + 1)),
                                      r=[PSr(bank, j * 128, (j + 1) * 128), (gtab, k, k + 1)], w=[(xnT, k * NT + tc * 128, k * NT + (tc + 1) * 128)])

            sgf = mA.alloc(F32, 512)
            xpc = [mA.alloc(F32, 512) for _ in range(2)]
            xpc_i = [0]

            def resid_phase(wv, kcn, lhs_fn, lhs_reg, banks, src, dst, t0):
                sname, dname = "dram_" + src.tensor.name, "dram_" + dst.tensor.name
                for cb in range(D // 512):
                    tm_dense(wv, cb * 512, kcn, lhs_fn, lhs_reg, banks)
                    for tc in range(4):
                        r0 = t0 + tc * 128
                        xb_ = xpc[xpc_i[0] % 2]
                        xpc_i[0] += 1
                        S.dma("sp", xb_.ap(), src[r0:r0 + 128, cb * 512:(cb + 1) * 512], r=[(sname, r0, r0 + 128)], w=[(xb_, 0, 512)])
                        S.dve(lambda e, xb_=xb_, bk=banks[tc]: e.tensor_tensor(out=xb_.ap(), in0=PS(bk), in1=xb_.ap(), op=ALU.add), r=[PSr(banks[tc]), (xb_, 0, 512)], w=[(xb_, 0, 512)])
                        S.dma("sp", dst[r0:r0 + 128, cb * 512:(cb + 1) * 512], xb_.ap(), r=[(xb_, 0, 512)], w=[(dname, r0, r0 + 128)])


            colA = Arena(t_xnT, "xnT", BF16, KC * NT * 2)

            def row_to_cols(row, off, nchunks, emit_copy):
                for c0 in range(0, nchunks, 8):
                    cn_ = min(8, nchunks - c0)
                    def emit(e, c0=c0, cn_=cn_):
                        ins = None
                        for c in range(c0, c0 + cn_):
                            ins = e.transpose(out=PS(7, c, c + 1), in_=row.ap(off + c * 128, off + (c + 1) * 128, 0, 1), identity=identf.ap(0, 1, 0, 1))
                        return ins
                    S.pe(emit, r=[(row, off + c0 * 128, off + (c0 + cn_) * 128), (identf, 0, 128)], w=[PSr(7)])
                emit_copy()

            def srow_dense(wv, c0, ncols, kcn, lcols, bank):
                nparts = (kcn + 7) // 8
                for kp in range(nparts):
                    k0 = kp * 8
                    kn = min(8, kcn - k0)
                    sv, sreg = wload(wv[:, k0:k0 + kn, c0:c0 + ncols], kn, ncols)
                    mm_group(PS(bank, 0, ncols, 0, 1), (lambda k, k0=k0: lcols.ap(k0 + k, k0 + k + 1)), (lambda k, sv=sv: sv[:, k, :]), kn,
                             r=[sreg, (lcols, 0, kcn)], w=[PSr(bank)], first=(kp == 0), last=(kp == nparts - 1))

            def rms_row(row, n, s0, inv_n, sjunk):
                S.act(lambda e: e.activation(out=sjunk.ap(0, n, 0, 1), in_=row.ap(0, n, 0, 1), func=AF.Square, accum_out=stat.ap(s0, s0 + 1, 0, 1)),
                      r=[(row, 0, n)], w=[(sjunk, 0, n), (stat, s0, s0 + 1)])
                S.act(lambda e: e.activation(out=stat.ap(s0 + 1, s0 + 2, 0, 1), in_=stat.ap(s0, s0 + 1, 0, 1), func=AF.Sqrt, bias=EPS, scale=inv_n),
                      r=[(stat, s0, s0 + 1)], w=[(stat, s0 + 1, s0 + 2)])
                S.dve(lambda e: e.reciprocal(out=stat.ap(s0 + 2, s0 + 3, 0, 1), in_=stat.ap(s0 + 1, s0 + 2, 0, 1)), r=[(stat, s0 + 1, s0 + 2)], w=[(stat, s0 + 2, s0 + 3)])

            def sample_norm_cols(src_dram, gtab, xsT):
                srow = Buf(t_arena, "arena", BF16, F32, 0, D)
                sjunk = Buf(t_arena, "arena", BF16, F32, D * 4, D)
                S.dma("sp", srow.ap(0, D, 0, 1), src_dram, r=[("dram_" + src_dram.tensor.name, 0, 1)], w=[(srow, 0, D)])
                rms_row(srow, D, 52, 1.0 / D, sjunk)
                S.dve(lambda e: e.tensor_scalar(out=srow.ap(0, D, 0, 1), in0=srow.ap(0, D, 0, 1), scalar1=stat.ap(54, 55, 0, 1), scalar2=None, op0=ALU.mult),
                      r=[(srow, 0, D), (stat, 54, 55)], w=[(srow, 0, D)])
                row_to_cols(srow, 0, KC, lambda: S.dve(lambda e: e.tensor_tensor(out=xsT.ap(), in0=PS(7, 0, KC), in1=gtab.ap(), op=ALU.mult),
                                                       r=[PSr(7), (gtab, 0, KC)], w=[(xsT, 0, KC)]))

            def sample_resid(wv, kcn, lcols, src_dram, dst_dram):
                sname, dname = "dram_" + src_dram.tensor.name, "dram_" + dst_dram.tensor.name
                for cb in range(D // 512):
                    bank = cb % 4
                    srow_dense(wv, cb * 512, 512, kcn, lcols, bank)
                    xb_ = xpc[xpc_i[0] % 2]
                    xpc_i[0] += 1
                    S.dma("sp", xb_.ap(0, 512, 0, 1), src_dram[:, cb * 512:(cb + 1) * 512], r=[(sname, 0, 1)], w=[(xb_, 0, 512)])
                    S.dve(lambda e, xb_=xb_, bank=bank: e.tensor_tensor(out=xb_.ap(0, 512, 0, 1), in0=PS(bank, 0, 512, 0, 1), in1=xb_.ap(0, 512, 0, 1), op=ALU.add),
                          r=[PSr(bank), (xb_, 0, 512)], w=[(xb_, 0, 512)])
                    S.dma("sp", dst_dram[:, cb * 512:(cb + 1) * 512], xb_.ap(0, 512, 0, 1), r=[(xb_, 0, 512)], w=[(dname, 0, 1)])

            def sample_layer(l, src_dram):
                global_names = None
                wv_in = wview(w_in[l])
                arena.reset()
                hrow = arena.alloc(F32, N_IN)
                rbase = arena.off
                colA.reset()
                xsT = colA.alloc(BF16, KC)
                oTs = colA.alloc(BF16, 36)
                mTs = colA.alloc(BF16, KC)
                hTs = colA.alloc(BF16, FKC)
                qcol = colA.alloc(F32, 16)
                acol = colA.alloc(F32, 8)
                zc = colA.alloc(F32, 1)
                pcol = colA.alloc(F32, 16)
                wa2f = colA.alloc(F32, 1024)
                sTt = colA.alloc(F32, 400)
                ocs = colA.alloc(F32, 512)
                sample_norm_cols(src_dram, gmix, xsT)
                for cb in range(0, N_IN, 512):
                    ncols = min(512, N_IN - cb)
                    bank = (cb // 512) % 4
                    srow_dense(wv_in, cb, ncols, KC, xsT, bank)
                    S.act(lambda e, cb=cb, ncols=ncols, bank=bank: e.activation(out=hrow.ap(cb, cb + ncols, 0, 1), in_=PS(bank, 0, ncols, 0, 1), func=AF.Copy),
                          r=[PSr(bank)], w=[(hrow, cb, cb + ncols)])
                arena.reset(rbase)
                arow = arena.alloc(F32, 1024)
                brow = arena.alloc(F32, 1024)
                t512 = arena.alloc(F32, 512)
                u512 = arena.alloc(F32, 512)
                sj = arena.alloc(F32, 512)
                S.dma("sp", wa2f.ap(0, 1024, 0, 16), w_alpha2[l], w=[(wa2f, 0, 1024)])
                S.dma("sp", brow.ap(0, 1024, 0, 1), b_alpha_r[l:l + 1, :], w=[(brow, 0, 1024)])
                S.dma("sp", SA.ap().rearrange("p (c v) -> p c v", v=512), sgla[l].rearrange("h (c p) v -> p (h c) v", p=128), w=[(SA, 0, 4096)])
                S.pe(lambda e: e.transpose(out=PS(7, 0, 1, 0, 16), in_=hrow.ap(O_ZA, O_ZA + 16, 0, 1), identity=identf.ap(0, 1, 0, 1)), r=[(hrow, O_ZA, O_ZA + 16), (identf, 0, 128)], w=[PSr(7)])
                S.dve(lambda e: e.tensor_copy(out=zc.ap(0, 1, 0, 16), in_=PS(7, 0, 1, 0, 16)), r=[PSr(7)], w=[(zc, 0, 1)])
                for hf in range(2):
                    mm_group(PS(hf, 0, 512, 0, 1), lambda k: zc.ap(0, 1, 0, 16), lambda k, hf=hf: wa2f.ap(hf * 512, (hf + 1) * 512, 0, 16), 1,
                             r=[(zc, 0, 1), (wa2f, hf * 512, (hf + 1) * 512)], w=[PSr(hf)])
                    sl = (hf * 512, (hf + 1) * 512)
                    S.dve(lambda e, hf=hf, sl=sl: e.tensor_tensor(out=arow.ap(sl[0], sl[1], 0, 1), in0=PS(hf, 0, 512, 0, 1), in1=brow.ap(sl[0], sl[1], 0, 1), op=ALU.add),
                          r=[PSr(hf), (brow, sl[0], sl[1])], w=[(arow, sl[0], sl[1])])
                S.act(lambda e: e.activation(out=arow.ap(0, 1024, 0, 1), in_=arow.ap(0, 1024, 0, 1), func=AF.Exp, scale=-1.0), r=[(arow, 0, 1024)], w=[(arow, 0, 1024)])
                S.act(lambda e: e.activation(out=arow.ap(0, 1024, 0, 1), in_=arow.ap(0, 1024, 0, 1), func=AF.Ln, bias=1.0), r=[(arow, 0, 1024)], w=[(arow, 0, 1024)])
                S.act(lambda e: e.activation(out=arow.ap(0, 1024, 0, 1), in_=arow.ap(0, 1024, 0, 1), func=AF.Exp, scale=-1.0 / 16), r=[(arow, 0, 1024)], w=[(arow, 0, 1024)])
                row_to_cols(arow, 0, 8, lambda: S.dve(lambda e: e.tensor_copy(out=acol.ap(), in_=PS(7, 0, 8)), r=[PSr(7)], w=[(acol, 0, 8)]))
                row_to_cols(hrow, O_QA, 8, lambda: S.dve(lambda e: e.tensor_scalar(out=qcol.ap(0, 8), in0=PS(7, 0, 8), scalar1=DK_A ** -0.5, scalar2=None, op0=ALU.mult),
                                                          r=[PSr(7)], w=[(qcol, 0, 8)]))
                for h in range(H_A):
                    for dkc in range(2):
                        c8 = h * 2 + dkc
                        sl = (c8 * 512, (c8 + 1) * 512)
                        ko = O_KA + c8 * 128
                        vo = O_VA + h * 512
                        mm_group(PS(dkc), lambda k, ko=ko: hrow.ap(ko, ko + 128, 0, 1), lambda k, vo=vo: hrow.ap(vo, vo + 512, 0, 1), 1,
                                 r=[(hrow, ko, ko + 128), (hrow, vo, vo + 512)], w=[PSr(dkc)])
                        S.dve(lambda e, sl=sl, c8=c8, dkc=dkc: e.scalar_tensor_tensor(out=SA.ap(*sl), in0=SA.ap(*sl), scalar=acol.ap(c8, c8 + 1), in1=PS(dkc), op0=ALU.mult, op1=ALU.add),
                              r=[(SA, sl[0], sl[1]), (acol, c8, c8 + 1), PSr(dkc)], w=[(SA, sl[0], sl[1])])
                    def emit_os(e, h=h):
                        e.matmul(PS(2, 0, 512, 0, 1), lhsT=qcol.ap(h * 2, h * 2 + 1), rhs=SA.ap(h * 1024, h * 1024 + 512), start=True, stop=False)
                        return e.matmul(PS(2, 0, 512, 0, 1), lhsT=qcol.ap(h * 2 + 1, h * 2 + 2), rhs=SA.ap(h * 1024 + 512, h * 1024 + 1024), start=False, stop=True)
                    S.pe(emit_os, r=[(qcol, 0, 8), (SA, h * 1024, (h + 1) * 1024)], w=[PSr(2)])
                    S.act(lambda e: e.activation(out=t512.ap(0, 512, 0, 1), in_=PS(2, 0, 512, 0, 1), func=AF.Copy), r=[PSr(2)], w=[(t512, 0, 512)])
                    rms_row(t512, 512, 56, 1.0 / DV_A, sj)
                    ro = O_RA + h * 512
                    S.act(lambda e, ro=ro: e.activation(out=u512.ap(0, 512, 0, 1), in_=hrow.ap(ro, ro + 512, 0, 1), func=AF.Silu), r=[(hrow, ro, ro + 512)], w=[(u512, 0, 512)])
                    S.dve(lambda e: e.tensor_tensor(out=u512.ap(0, 512, 0, 1), in0=u512.ap(0, 512, 0, 1), in1=ggla.ap(0, 512, 0, 1), op=ALU.mult), r=[(u512, 0, 512), (ggla, 0, 512)], w=[(u512, 0, 512)])
                    S.dve(lambda e: e.scalar_tensor_tensor(out=t512.ap(0, 512, 0, 1), in0=t512.ap(0, 512, 0, 1), scalar=stat.ap(58, 59, 0, 1), in1=u512.ap(0, 512, 0, 1), op0=ALU.mult, op1=ALU.mult),
                          r=[(t512, 0, 512), (stat, 58, 59), (u512, 0, 512)], w=[(t512, 0, 512)])
                    row_to_cols(t512, 0, 4, lambda h=h: S.dve(lambda e, h=h: e.tensor_copy(out=oTs.ap(h * 4, h * 4 + 4), in_=PS(7, 0, 4)), r=[PSr(7)], w=[(oTs, h * 4, h * 4 + 4)]))
                S.dma("sp", gla_s[l].rearrange("h (c p) v -> p (h c) v", p=128), SA.ap().rearrange("p (c v) -> p c v", v=512), r=[(SA, 0, 4096)], w=[("dram_gla_s", l, l + 1)])
                arena.reset(rbase)
                qk = arena.alloc(F32, 2048)
                tq = arena.alloc(F32, 1024)
                t256 = arena.alloc(F32, 256)
                u256 = arena.alloc(F32, 256)
                sj = arena.alloc(F32, 512)
                csr = arena.alloc(F32, 128)
                S.dma("sp", csr.ap(0, 128, 0, 1), cn["csrow"], w=[(csr, 0, 128)])
                S.dma("sp", SB_.ap().rearrange("p (h v) -> p h v", v=256), sret[l].rearrange("h p v -> p h v"), w=[(SB_, 0, 2048)])
                cosr = csr.ap(0, 64, 0, 1).unsqueeze(1).to_broadcast([1, 8, 64])
                sinr = csr.ap(64, 128, 0, 1).unsqueeze(1).to_broadcast([1, 8, 64])
                for wi, (ho, scl) in enumerate(((O_QB, 1.0), (O_KB, DK_B ** -0.5))):
                    x3 = hrow.ap(ho, ho + 1024, 0, 1).rearrange("p (h two n) -> p h two n", two=2, n=64)
                    o3 = qk.ap(wi * 1024, (wi + 1) * 1024, 0, 1).rearrange("p (h two n) -> p h two n", two=2, n=64)
                    t3 = tq.ap(0, 1024, 0, 1).rearrange("p (h two n) -> p h two n", two=2, n=64)
                    rr_ = [(hrow, ho, ho + 1024), (csr, 0, 128)]
                    S.dve(lambda e, x3=x3, o3=o3: e.tensor_tensor(out=o3[:, :, 0, :], in0=x3[:, :, 0, :], in1=cosr, op=ALU.mult), r=rr_, w=[(qk, wi * 1024, (wi + 1) * 1024)])
                    S.dve(lambda e, x3=x3, t3=t3: e.tensor_tensor(out=t3[:, :, 0, :], in0=x3[:, :, 1, :], in1=sinr, op=ALU.mult), r=rr_, w=[(tq, 0, 1024)])
                    S.dve(lambda e, o3=o3, t3=t3: e.tensor_tensor(out=o3[:, :, 0, :], in0=o3[:, :, 0, :], in1=t3[:, :, 0, :], op=ALU.subtract),
                          r=[(qk, wi * 1024, (wi + 1) * 1024), (tq, 0, 1024)], w=[(qk, wi * 1024, (wi + 1) * 1024)])
                    S.dve(lambda e, x3=x3, o3=o3: e.tensor_tensor(out=o3[:, :, 1, :], in0=x3[:, :, 0, :], in1=sinr, op=ALU.mult), r=rr_, w=[(qk, wi * 1024, (wi + 1) * 1024)])
                    S.dve(lambda e, x3=x3, t3=t3: e.tensor_tensor(out=t3[:, :, 1, :], in0=x3[:, :, 1, :], in1=cosr, op=ALU.mult), r=rr_, w=[(tq, 0, 1024)])
                    S.dve(lambda e, o3=o3, t3=t3: e.tensor_tensor(out=o3[:, :, 1, :], in0=o3[:, :, 1, :], in1=t3[:, :, 1, :], op=ALU.add),
                          r=[(qk, wi * 1024, (wi + 1) * 1024), (tq, 0, 1024)], w=[(qk, wi * 1024, (wi + 1) * 1024)])
                    if scl != 1.0:
                        S.dve(lambda e, wi=wi, scl=scl: e.tensor_scalar(out=qk.ap(wi * 1024, (wi + 1) * 1024, 0, 1), in0=qk.ap(wi * 1024, (wi + 1) * 1024, 0, 1), scalar1=scl, scalar2=None, op0=ALU.mult),
                              r=[(qk, wi * 1024, (wi + 1) * 1024)], w=[(qk, wi * 1024, (wi + 1) * 1024)])
                row_to_cols(qk, 0, 8, lambda: S.dve(lambda e: e.tensor_copy(out=qcol.ap(8, 16), in_=PS(7, 0, 8)), r=[PSr(7)], w=[(qcol, 8, 16)]))
                for h in range(H_B):
                    ssl = (h * 256, (h + 1) * 256)
                    ko = 1024 + h * 128
                    vo = O_VB + h * 256
                    mm_group(PS(0, 0, 256), lambda k, ko=ko: qk.ap(ko, ko + 128, 0, 1), lambda k, vo=vo: hrow.ap(vo, vo + 256, 0, 1), 1,
                             r=[(qk, ko, ko + 128), (hrow, vo, vo + 256)], w=[PSr(0)])
                    S.dve(lambda e, ssl=ssl, h=h: e.scalar_tensor_tensor(out=SB_.ap(*ssl), in0=SB_.ap(*ssl), scalar=gam.ap(8 + h, 9 + h), in1=PS(0, 0, 256), op0=ALU.mult, op1=ALU.add),
                          r=[(SB_, ssl[0], ssl[1]), (gam, 8 + h, 9 + h), PSr(0)], w=[(SB_, ssl[0], ssl[1])])
                    mm_group(PS(2, 0, 256, 0, 1), lambda k, h=h: qcol.ap(8 + h, 9 + h), lambda k, ssl=ssl: SB_.ap(*ssl), 1, r=[(qcol, 8, 16), (SB_, ssl[0], ssl[1])], w=[PSr(2)])
                    S.act(lambda e: e.activation(out=t256.ap(0, 256, 0, 1), in_=PS(2, 0, 256, 0, 1), func=AF.Copy), r=[PSr(2)], w=[(t256, 0, 256)])
                    rms_row(t256, 256, 56, 1.0 / DV_B, sj)
                    go = O_GB + h * 256
                    S.act(lambda e, go=go: e.activation(out=u256.ap(0, 256, 0, 1), in_=hrow.ap(go, go + 256, 0, 1), func=AF.Silu), r=[(hrow, go, go + 256)], w=[(u256, 0, 256)])
                    S.dve(lambda e: e.scalar_tensor_tensor(out=t256.ap(0, 256, 0, 1), in0=t256.ap(0, 256, 0, 1), scalar=stat.ap(58, 59, 0, 1), in1=u256.ap(0, 256, 0, 1), op0=ALU.mult, op1=ALU.mult),
                          r=[(t256, 0, 256), (stat, 58, 59), (u256, 0, 256)], w=[(t256, 0, 256)])
                    row_to_cols(t256, 0, 2, lambda h=h: S.dve(lambda e, h=h: e.tensor_copy(out=oTs.ap(16 + h * 2, 18 + h * 2), in_=PS(7, 0, 2)), r=[PSr(7)], w=[(oTs, 16 + h * 2, 18 + h * 2)]))
                S.dma("sp", ret_s[l].rearrange("h p v -> p h v"), SB_.ap().rearrange("p (h v) -> p h v", v=256), r=[(SB_, 0, 2048)], w=[("dram_ret_s", l, l + 1)])
                arena.reset(rbase)
                Kc = arena.alloc(F32, 512)
                Vc = [arena.alloc(F32, 512) for _ in range(3)]
                prod = arena.alloc(F32, 512)
                sc12 = arena.alloc(F32, 16)
                s0row = arena.alloc(F32, 16)
                sbt = arena.alloc(F32, 12)
                S.dma("sp", sbt.ap(), cn["sbias"], w=[(sbt, 0, 12)])
                for g in range(3):
                    W, d = WINDOWS[g], DILS[g]
                    qo, ko, vo = O_HC + g * 1536, O_HC + g * 1536 + 512, O_HC + g * 1536 + 1024
                    S.dma("sp", Kc.ap(), cw[g][l, 0].rearrange("(j d) c -> j d c", d=d)[:, 0, :], w=[(Kc, 0, 512)])
                    S.dma("sp", Vc[g].ap(), cw[g][l, 1].rearrange("(j d) c -> j d c", d=d)[:, 0, :], w=[(Vc[g], 0, 512)])
                    for which, oo in ((0, ko), (1, vo)):
                        S.dma("sp", ws[g][l, which, 0:W - 1, :], cw[g][l, which, 1:W, :], w=[("dram_ws%d" % g, (l * 2 + which) * 4096, (l * 2 + which) * 4096 + W - 1)])
                        S.dma("sp", ws[g][l, which, W - 1:W, :], hrow.ap(oo, oo + 512, 0, 1), r=[(hrow, oo, oo + 512)], w=[("dram_ws%d" % g, (l * 2 + which) * 4096 + W - 1, (l * 2 + which) * 4096 + W)])
                    mm_group(PS(0), lambda k: onesf.ap(0, 128, 0, 1), lambda k, qo=qo: hrow.ap(qo, qo + 512, 0, 1), 1, r=[(onesf, 0, 128), (hrow, qo, qo + 512)], w=[PSr(0)])
                    S.dve(lambda e: e.tensor_tensor(out=prod.ap(), in0=Kc.ap(), in1=PS(0), op=ALU.mult), r=[(Kc, 0, 512), PSr(0)], w=[(prod, 0, 512)])
                    S.dve(lambda e, g=g: e.tensor_reduce(out=sc12.ap(g * 4, g * 4 + 4), in_=prod.ap().rearrange("p (i d) -> p i d", d=128), axis=AX.X, op=ALU.add),
                          r=[(prod, 0, 512)], w=[(sc12, g * 4, g * 4 + 4)])
                    S.dve(lambda e, g=g: e.scalar_tensor_tensor(out=sc12.ap(g * 4, g * 4 + 4), in0=sc12.ap(g * 4, g * 4 + 4), scalar=128 ** -0.5, in1=sbt.ap(g * 4, g * 4 + 4), op0=ALU.mult, op1=ALU.add),
                          r=[(sc12, g * 4, g * 4 + 4), (sbt, g * 4, g * 4 + 4)], w=[(sc12, g * 4, g * 4 + 4)])
                    S.dve(lambda e, qo=qo, ko=ko: e.tensor_tensor(out=prod.ap(0, 512, 0, 1), in0=hrow.ap(qo, qo + 512, 0, 1), in1=hrow.ap(ko, ko + 512, 0, 1), op=ALU.mult),
                          r=[(hrow, qo, qo + 512), (hrow, ko, ko + 512), (prod, 0, 512)], w=[(prod, 0, 512)])
                    S.dve(lambda e, g=g: e.tensor_reduce(out=s0row.ap(g * 4, g * 4 + 4, 0, 1), in_=prod.ap(0, 512, 0, 1).rearrange("p (i d) -> p i d", d=128), axis=AX.X, op=ALU.add),
                          r=[(prod, 0, 512)], w=[(s0row, g * 4, g * 4 + 4)])
                    S.pe(lambda e, g=g: e.transpose(out=PS(3, g * 128, (g + 1) * 128, 0, 4), in_=sc12.ap(g * 4, g * 4 + 4), identity=identf.ap()),
                         r=[(sc12, g * 4, g * 4 + 4), (identf, 0, 128)], w=[PSr(3)])
                    S.pe(lambda e, g=g: e.transpose(out=PS(3, 384 + g, 385 + g, 0, 4), in_=s0row.ap(g * 4, g * 4 + 4, 0, 1), identity=identf.ap(0, 1, 0, 1)),
                         r=[(s0row, g * 4, g * 4 + 4), (identf, 0, 128)], w=[PSr(3)])
                S.dve(lambda e: e.tensor_copy(out=sTt.ap(0, 387, 0, 4), in_=PS(3, 0, 387, 0, 4)), r=[PSr(3)], w=[(sTt, 0, 387)])
                S.dve(lambda e: e.tensor_scalar(out=sTt.ap(384, 387, 0, 4), in0=sTt.ap(384, 387, 0, 4), scalar1=128 ** -0.5, scalar2=None, op0=ALU.mult), r=[(sTt, 384, 387)], w=[(sTt, 384, 387)])
                S.dve(lambda e: e.tensor_reduce(out=stat.ap(60, 61, 0, 4), in_=sTt.ap(0, 387, 0, 4), axis=AX.X, op=ALU.max, negate=True), r=[(sTt, 0, 387)], w=[(stat, 60, 61)])
                S.act(lambda e: e.activation(out=sTt.ap(0, 387, 0, 4), in_=sTt.ap(0, 387, 0, 4), func=AF.Exp, bias=stat.ap(60, 61, 0, 4), scale=1.0, accum_out=stat.ap(61, 62, 0, 4)),
                      r=[(sTt, 0, 387), (stat, 60, 61)], w=[(sTt, 0, 387), (stat, 61, 62)])
                S.dve(lambda e: e.reciprocal(out=stat.ap(62, 63, 0, 4), in_=stat.ap(61, 62, 0, 4)), r=[(stat, 61, 62)], w=[(stat, 62, 63)])
                for g in range(3):
                    S.pe(lambda e, g=g: e.transpose(out=PS(4, g * 4, g * 4 + 4), in_=sTt.ap(g * 128, (g + 1) * 128, 0, 4), identity=identf.ap(0, 4, 0, 4)),
                         r=[(sTt, g * 128, (g + 1) * 128), (identf, 0, 128)], w=[PSr(4)])
                    S.pe(lambda e, g=g: e.transpose(out=PS(4, 16 + g * 4, 20 + g * 4, 0, 1), in_=sTt.ap(384 + g, 385 + g, 0, 4), identity=identf.ap(0, 4, 0, 4)),
                         r=[(sTt, 384 + g, 385 + g), (identf, 0, 128)], w=[PSr(4)])
                S.dve(lambda e: e.tensor_copy(out=pcol.ap(0, 12), in_=PS(4, 0, 12)), r=[PSr(4)], w=[(pcol, 0, 12)])
                S.dve(lambda e: e.tensor_copy(out=s0row.ap(0, 12, 0, 1), in_=PS(4, 16, 28, 0, 1)), r=[PSr(4)], w=[(s0row, 0, 12)])
                def emit_pvs(e):
                    ins = None
                    for g in range(3):
                        vo = O_HC + g * 1536 + 1024
                        e.matmul(PS(5, 0, 512, 0, 4), lhsT=pcol.ap(g * 4, g * 4 + 4), rhs=Vc[g].ap(), start=(g == 0), stop=False)
                        ins = e.matmul(PS(5, 0, 512, 0, 4), lhsT=s0row.ap(g * 4, g * 4 + 4, 0, 1), rhs=hrow.ap(vo, vo + 512, 0, 1), start=False, stop=(g == 2))
                    return ins
                S.pe(emit_pvs, r=[(pcol, 0, 12), (s0row, 0, 12), (Vc[0], 0, 512), (Vc[1], 0, 512), (Vc[2], 0, 512), (hrow, O_HC, N_IN)], w=[PSr(5)])
                S.act(lambda e: e.activation(out=ocs.ap(0, 512, 0, 4), in_=PS(5, 0, 512, 0, 4), func=AF.Copy, scale=stat.ap(62, 63, 0, 4)), r=[PSr(5), (stat, 62, 63)], w=[(ocs, 0, 512)])
                for i in range(4):
                    S.pe(lambda e, i=i: e.transpose(out=PS(6, i * 4, i * 4 + 4), in_=ocs.ap(i * 128, (i + 1) * 128, 0, 4), identity=identf.ap(0, 4, 0, 4)),
                         r=[(ocs, i * 128, (i + 1) * 128), (identf, 0, 128)], w=[PSr(6)])
                    S.dve(lambda e, i=i: e.tensor_copy(out=oTs.ap(32 + i, 33 + i), in_=PS(6, i * 4 + i, i * 4 + i + 1)), r=[PSr(6)], w=[(oTs, 32 + i, 33 + i)])
                arena.reset()
                mrow = arena.alloc(F32, D)
                sgr = [arena.alloc(F32, 512) for _ in range(3)]
                wv_mg = wview(w_merge[l])
                wv_ups = [(wview(w_up_a[l]), 16, 0), (wview(w_up_b[l]), 16, 16), (wview(w_up_c[l]), 4, 32)]
                for cb in range(D // 512):
                    for b in range(3):
                        srow_dense(wv_mg, b * D + cb * 512, 512, KC, xsT, b)
                        S.act(lambda e, b=b: e.activation(out=sgr[b].ap(0, 512, 0, 1), in_=PS(b, 0, 512, 0, 1), func=AF.Sigmoid), r=[PSr(b)], w=[(sgr[b], 0, 512)])
                    for b, (wvu, kn, k0) in enumerate(wv_ups):
                        ocols = Buf(oTs.t, oTs.name, oTs.tdt, BF16, oTs.base + k0 * 2, kn)
                        srow_dense(wvu, cb * 512, 512, kn, ocols, 3 + b)
                        S.dve(lambda e, b=b: e.tensor_tensor(out=sgr[b].ap(0, 512, 0, 1), in0=PS(3 + b, 0, 512, 0, 1), in1=sgr[b].ap(0, 512, 0, 1), op=ALU.mult),
                              r=[PSr(3 + b), (sgr[b], 0, 512)], w=[(sgr[b], 0, 512)])
                    S.dve(lambda e: e.tensor_tensor(out=sgr[0].ap(0, 512, 0, 1), in0=sgr[0].ap(0, 512, 0, 1), in1=sgr[1].ap(0, 512, 0, 1), op=ALU.add),
                          r=[(sgr[0], 0, 512), (sgr[1], 0, 512)], w=[(sgr[0], 0, 512)])
                    S.dve(lambda e, cb=cb: e.tensor_tensor(out=mrow.ap(cb * 512, (cb + 1) * 512, 0, 1), in0=sgr[0].ap(0, 512, 0, 1), in1=sgr[2].ap(0, 512, 0, 1), op=ALU.add),
                          r=[(sgr[0], 0, 512), (sgr[2], 0, 512)], w=[(mrow, cb * 512, (cb + 1) * 512)])
                row_to_cols(mrow, 0, KC, lambda: S.dve(lambda e: e.tensor_copy(out=mTs.ap(), in_=PS(7, 0, KC)), r=[PSr(7)], w=[(mTs, 0, KC)]))
                sample_resid(wview(w_out[l]), KC, mTs, src_dram, xsa)
                sample_norm_cols(xsa, gffn, xsT)
                arena.reset()
                frow = arena.alloc(F32, D_FF)
                grow_ = arena.alloc(F32, 512)
                wv_g, wv_u = wview(w_fg[l]), wview(w_fu[l])
                for cb in range(0, D_FF, 512):
                    ncols = min(512, D_FF - cb)
                    srow_dense(wv_g, cb, ncols, KC, xsT, 0)
                    srow_dense(wv_u, cb, ncols, KC, xsT, 1)
                    S.act(lambda e, ncols=ncols: e.activation(out=grow_.ap(0, ncols, 0, 1), in_=PS(0, 0, ncols, 0, 1), func=AF.Silu), r=[PSr(0)], w=[(grow_, 0, ncols)])
                    S.dve(lambda e, cb=cb, ncols=ncols: e.tensor_tensor(out=frow.ap(cb, cb + ncols, 0, 1), in0=PS(1, 0, ncols, 0, 1), in1=grow_.ap(0, ncols, 0, 1), op=ALU.mult),
                          r=[PSr(1), (grow_, 0, ncols)], w=[(frow, cb, cb + ncols)])
                row_to_cols(frow, 0, FKC, lambda: S.dve(lambda e: e.tensor_copy(out=hTs.ap(), in_=PS(7, 0, FKC)), r=[PSr(7)], w=[(hTs, 0, FKC)]))
                sample_resid(wview(w_fd[l]), FKC, hTs, xsa, xsb)

            for l in range(n_layers):
                x_src = xp if l == 0 else xb
                wv_in = wview(w_in[l])
                cload(gmix, g_mix[l])
                cload(gffn, g_ffn[l])
                cload(ggla, g_gla[l].partition_broadcast(128))
                cload(negb, b_alpha[l])
                S.dve(lambda e: e.tensor_scalar(out=negb.ap(), in0=negb.ap(), scalar1=-1.0, scalar2=None, op0=ALU.mult), r=[(negb, 0, 8)], w=[(negb, 0, 8)])
                S.dma("pool", wa2.ap(0, 1024, 0, 16), w_alpha2[l], w=[(wa2, 0, 1024)], slot=0)
                ck('params')
                if not skip_sample:
                    sample_layer(l, xs if l == 0 else xsb)
                ck('sample')
                S.dve(lambda e: e.memset(SA.ap(), 0.0), w=[(SA, 0, 4096)])
                S.dve(lambda e: e.memset(SB_.ap(), 0.0), w=[(SB_, 0, 2048)])

                for ti in range(n_tiles):
                    t0 = ti * NT
                    arena.reset()
                    rowbuf = arena.alloc(F32, D)
                    xnjunk = arena.alloc(BF16, D)
                    norm_tile(x_src, t0, gmix, rowbuf)
                    ck('norm')
                    cload(cosb, cn["cost"][:, t0:t0 + NT])
                    cload(sinb, cn["sint"][:, t0:t0 + NT])

                    arena.reset()
                    oT = arena.alloc(BF16, 36 * NT)
                    mT = arena.alloc(BF16, 32 * NT)
                    tmp_base = arena.off - 32 * NT * 2
                    arena.reset(tmp_base)
                    zT = arena.alloc(BF16, 512)
                    qt = arena.alloc(BF16, 1024)
                    kt = arena.alloc(BF16, 1024)
                    Ep = arena.alloc(F32, 1024)
                    Em = arena.alloc(F32, 512)
                    tmp1 = arena.alloc(F32, 512)
                    tmp2 = arena.alloc(F32, 512)
                    cum = arena.alloc(F32, 512)
                    junkf = arena.alloc(F32, 512)
                    vtok = arena.alloc(BF16, 2048)
                    gs = arena.alloc(BF16, 2048)
                    Pm = arena.alloc(BF16, 128)
                    ktok = arena.alloc(BF16, 256)
                    otok = arena.alloc(BF16, 512)
                    Sbf = arena.alloc(BF16, 1024)
                    fm_dense(wv_in, O_ZA, 16, KC, xn_fn, xn_reg, 0)
                    S.act(lambda e: e.activation(out=zT.ap(0, 512, 0, 16), in_=PS(0, 0, 512, 0, 16), func=AF.Copy), r=[PSr(0)], w=[(zT, 0, 512)])
                    for h in range(H_A):
                        for dkc in range(2):
                            c8 = h * 2 + dkc
                            seg = (dkc * 512, (dkc + 1) * 512)
                            mm_group(PS(1), lambda k, c8=c8: wa2.ap(c8 * 128, (c8 + 1) * 128, 0, 16), lambda k: zT.ap(0, 512, 0, 16), 1,
                                     r=[(wa2, c8 * 128, (c8 + 1) * 128), (zT, 0, 512)], w=[PSr(1)])
                            S.act(lambda e, c8=c8: e.activation(out=tmp1.ap(), in_=PS(1), func=AF.Exp, bias=negb.ap(c8, c8 + 1), scale=-1.0),
                                  r=[PSr(1), (negb, c8, c8 + 1)], w=[(tmp1, 0, 512)])
                            S.act(lambda e: e.activation(out=tmp2.ap(), in_=tmp1.ap(), func=AF.Ln, bias=1.0, scale=1.0), r=[(tmp1, 0, 512)], w=[(tmp2, 0, 512)])
                            for tc in range(4):
                                S.dve(lambda e, tc=tc: e.tensor_tensor_scan(out=cum.ap(tc * 128, (tc + 1) * 128), data0=onesf.ap(), data1=tmp2.ap(tc * 128, (tc + 1) * 128),
                                                                            initial=0.0, op0=ALU.mult, op1=ALU.add),
                                      r=[(tmp2, tc * 128, (tc + 1) * 128), (onesf, 0, 128)], w=[(cum, tc * 128, (tc + 1) * 128)])
                            S.act(lambda e, seg=seg: e.activation(out=Ep.ap(*seg), in_=cum.ap(), func=AF.Exp, scale=-1.0 / 16), r=[(cum, 0, 512)], w=[(Ep, seg[0], seg[1])])
                            S.act(lambda e: e.activation(out=Em.ap(), in_=cum.ap(), func=AF.Exp, scale=1.0 / 16), r=[(cum, 0, 512)], w=[(Em, 0, 512)])
                            fm_dense(wv_in, O_QA + c8 * 128, 128, KC, xn_fn, xn_reg, 2)
                            S.dve(lambda e, seg=seg: e.scalar_tensor_tensor(out=qt.ap(*seg), in0=PS(2), scalar=DK_A ** -0.5, in1=Ep.ap(*seg), op0=ALU.mult, op1=ALU.mult),
                                  r=[PSr(2), (Ep, seg[0], seg[1])], w=[(qt, seg[0], seg[1])])
                            fm_dense(wv_in, O_KA + c8 * 128, 128, KC, xn_fn, xn_reg, 3)
                            S.dve(lambda e, seg=seg: e.tensor_tensor(out=kt.ap(*seg), in0=PS(3), in1=Em.ap(), op=ALU.mult),
                                  r=[PSr(3), (Em, 0, 512)], w=[(kt, seg[0], seg[1])])
                        tm_dense(wv_in, O_VA + h * 512, KC, xn_lhs, xn_reg, [0, 1, 2, 3])
                        for tc in range(4):
                            S.act(lambda e, tc=tc: e.activation(out=vtok.ap(tc * 512, (tc + 1) * 512), in_=PS(tc), func=AF.Copy), r=[PSr(tc)], w=[(vtok, tc * 512, (tc + 1) * 512)])
                        tm_dense(wv_in, O_RA + h * 512, KC, xn_lhs, xn_reg, [0, 1, 2, 3])
                        for tc in range(4):
                            S.act(lambda e, tc=tc: e.activation(out=junkf.ap(), in_=PS(tc), func=AF.Silu), r=[PSr(tc)], w=[(junkf, 0, 512)])
                            S.dve(lambda e, tc=tc: e.tensor_tensor(out=gs.ap(tc * 512, (tc + 1) * 512), in0=junkf.ap(), in1=ggla.ap(), op=ALU.mult),
                                  r=[(junkf, 0, 512), (ggla, 0, 512)], w=[(gs, tc * 512, (tc + 1) * 512)])
                        S.act(lambda e, h=h: e.activation(out=Sbf.ap(), in_=SA.ap(h * 1024, (h + 1) * 1024), func=AF.Copy), r=[(SA, h * 1024, (h + 1) * 1024)], w=[(Sbf, 0, 1024)])
                        for tc in range(4):
                            def ts(dkc, tc=tc):
                                return (dkc * 512 + tc * 128, dkc * 512 + (tc + 1) * 128)
                            mm_group(PS(4, 0, 128), lambda k, tc=tc: kt.ap(*ts(k, tc)), lambda k, tc=tc: qt.ap(*ts(k, tc)), 2,
                                     r=[(kt, 0, 1024), (qt, 0, 1024)], w=[PSr(4, 0, 128)])
                            S.dve(lambda e: e.tensor_tensor(out=Pm.ap(), in0=PS(4, 0, 128), in1=mask01.ap(), op=ALU.mult), r=[PSr(4, 0, 128), (mask01, 0, 128)], w=[(Pm, 0, 128)])
                            def emit_kt(e, tc=tc):
                                ins = None
                                for dkc in range(2):
                                    ins = e.transpose(out=PS(7, 0, 256).bitcast(BF16)[:, dkc * 128:(dkc + 1) * 128], in_=kt.ap(*ts(dkc, tc)), identity=identb.ap())
                                return ins
                            S.pe(emit_kt, r=[(kt, 0, 1024), (identb, 0, 128)], w=[PSr(7, 0, 256)])
                            S.act(lambda e: e.activation(out=ktok.ap(), in_=PS(7, 0, 256).bitcast(BF16)[:, 0:256], func=AF.Copy), r=[PSr(7, 0, 256)], w=[(ktok, 0, 256)])
                            def emit_o(e, tc=tc):
                                e.matmul(PS(5), lhsT=Pm.ap(), rhs=vtok.ap(tc * 512, (tc + 1) * 512), start=True, stop=False)
                                e.matmul(PS(5), lhsT=qt.ap(*ts(0, tc)), rhs=Sbf.ap(0, 512), start=False, stop=False)
                                return e.matmul(PS(5), lhsT=qt.ap(*ts(1, tc)), rhs=Sbf.ap(512, 1024), start=False, stop=True)
                            S.pe(emit_o, r=[(Pm, 0, 128), (vtok, tc * 512, (tc + 1) * 512), (qt, 0, 1024), (Sbf, 0, 1024)], w=[PSr(5)])
                            st0 = 16 + tc * 4
                            S.act(lambda e, st0=st0: e.activation(out=junkf.ap(), in_=PS(5), func=AF.Square, accum_out=stat.ap(st0, st0 + 1)),
                                  r=[PSr(5)], w=[(junkf, 0, 512), (stat, st0, st0 + 1)])
                            S.act(lambda e, st0=st0: e.activation(out=stat.ap(st0 + 1, st0 + 2), in_=stat.ap(st0, st0 + 1), func=AF.Sqrt, bias=EPS, scale=1.0 / DV_A),
                                  r=[(stat, st0, st0 + 1)], w=[(stat, st0 + 1, st0 + 2)])
                            S.dve(lambda e, st0=st0: e.reciprocal(out=stat.ap(st0 + 2, st0 + 3), in_=stat.ap(st0 + 1, st0 + 2)), r=[(stat, st0 + 1, st0 + 2)], w=[(stat, st0 + 2, st0 + 3)])
                            S.dve(lambda e, st0=st0, tc=tc: e.scalar_tensor_tensor(out=otok.ap(), in0=PS(5), scalar=stat.ap(st0 + 2, st0 + 3), in1=gs.ap(tc * 512, (tc + 1) * 512),
                                                                                  op0=ALU.mult, op1=ALU.mult),
                                  r=[PSr(5), (stat, st0 + 2, st0 + 3), (gs, tc * 512, (tc + 1) * 512)], w=[(otok, 0, 512)])
                            def emit_ot(e):
                                ins = None
                                for j in range(4):
                                    ins = e.transpose(out=PS(7, 256, 512).bitcast(BF16)[:, j * 128:(j + 1) * 128], in_=otok.ap(j * 128, (j + 1) * 128), identity=identb.ap())
                                return ins
                            S.pe(emit_ot, r=[(otok, 0, 512), (identb, 0, 128)], w=[PSr(7, 256, 512)])
                            oc0 = h * 4
                            S.act(lambda e, oc0=oc0, tc=tc: e.activation(out=oT.ap(oc0 * NT, (oc0 + 4) * NT).rearrange("p (j t) -> p j t", t=NT)[:, :, tc * 128:(tc + 1) * 128],
                                                                        in_=PS(7, 256, 512).bitcast(BF16).rearrange("p (j t) -> p j t", t=128), func=AF.Copy),
                                  r=[PSr(7, 256, 512)], w=[(oT, oc0 * NT, (oc0 + 4) * NT)])
                            for dkc in range(2):
                                c8 = h * 2 + dkc
                                ecol = dkc * 512 + tc * 128 + 127
                                mm_group(PS(6), lambda k, dkc=dkc: ktok.ap(dkc * 128, (dkc + 1) * 128), lambda k, tc=tc: vtok.ap(tc * 512, (tc + 1) * 512), 1,
                                         r=[(ktok, 0, 256), (vtok, tc * 512, (tc + 1) * 512)], w=[PSr(6)])
                                sl = (c8 * 512, (c8 + 1) * 512)
                                S.dve(lambda e, sl=sl, ecol=ecol: e.tensor_scalar(out=SA.ap(*sl), in0=SA.ap(*sl), scalar1=Ep.ap(ecol, ecol + 1), scalar2=None, op0=ALU.mult),
                                      r=[(SA, sl[0], sl[1]), (Ep, ecol, ecol + 1)], w=[(SA, sl[0], sl[1])])
                                S.dve(lambda e, sl=sl, ecol=ecol: e.scalar_tensor_tensor(out=SA.ap(*sl), in0=PS(6), scalar=Ep.ap(ecol, ecol + 1), in1=SA.ap(*sl), op0=ALU.mult, op1=ALU.add),
                                      r=[PSr(6), (SA, sl[0], sl[1]), (Ep, ecol, ecol + 1)], w=[(SA, sl[0], sl[1])])
                                S.act(lambda e, sl=sl, dkc=dkc: e.activation(out=Sbf.ap(dkc * 512, (dkc + 1) * 512), in_=SA.ap(*sl), func=AF.Copy),
                                      r=[(SA, sl[0], sl[1])], w=[(Sbf, dkc * 512, (dkc + 1) * 512)])
                    if ti == n_tiles - 1:
                        S.dma("sp", gla_p[l].rearrange("h (c p) v -> p (h c) v", p=128), SA.ap().rearrange("p (c v) -> p c v", v=512), r=[(SA, 0, 4096)], w=[("dram_gla_p", l, l + 1)])


                    ck('mixA')
                    arena.reset(tmp_base)
                    qf = arena.alloc(F32, 512)
                    t1 = arena.alloc(F32, 512)
                    t2 = arena.alloc(F32, 512)
                    junkfB = arena.alloc(F32, 512)
                    qtb = arena.alloc(BF16, 1024)
                    ktb = arena.alloc(BF16, 1024)
                    vtokb = arena.alloc(BF16, 2048)
                    gsb = arena.alloc(BF16, 2048)
                    PmB = arena.alloc(BF16, 128)
                    ktokb = arena.alloc(BF16, 128)
                    otokb = arena.alloc(BF16, 256)
                    Sbfb = arena.alloc(BF16, 512)
                    for hp in range(4):
                        for hh in range(2):
                            h = hp * 2 + hh
                            for (coff, dst, etab) in ((O_QB, qtb, rEp), (O_KB, ktb, rEm)):
                                fm_dense(wv_in, coff + h * 128, 128, KC, xn_fn, xn_reg, 2)
                                S.act(lambda e: e.activation(out=qf.ap(), in_=PS(2), func=AF.Copy), r=[PSr(2)], w=[(qf, 0, 512)])
                                mm_group(PS(3), lambda k: permf.ap(), lambda k: qf.ap(), 1, r=[(permf, 0, 128), (qf, 0, 512)], w=[PSr(3)])
                                S.dve(lambda e: e.tensor_tensor(out=t1.ap(), in0=qf.ap(), in1=cosb.ap(), op=ALU.mult), r=[(qf, 0, 512), (cosb, 0, 512)], w=[(t1, 0, 512)])
                                S.dve(lambda e: e.tensor_tensor(out=t2.ap(), in0=PS(3), in1=sinb.ap(), op=ALU.mult), r=[PSr(3), (sinb, 0, 512)], w=[(t2, 0, 512)])
                                S.dve(lambda e: e.tensor_tensor(out=t1.ap(), in0=t1.ap(), in1=t2.ap(), op=ALU.add), r=[(t1, 0, 512), (t2, 0, 512)], w=[(t1, 0, 512)])
                                S.dve(lambda e, dst=dst, etab=etab, h=h, hh=hh: e.tensor_tensor(
                                    out=dst.ap(hh * 512, (hh + 1) * 512).rearrange("p (c t) -> p c t", t=128),
                                    in0=t1.ap().rearrange("p (c t) -> p c t", t=128),
                                    in1=etab.ap(h * 128, (h + 1) * 128).unsqueeze(1).to_broadcast([128, 4, 128]), op=ALU.mult),
                                    r=[(t1, 0, 512), (etab, h * 128, (h + 1) * 128)], w=[(dst, hh * 512, (hh + 1) * 512)])
                        tm_dense(wv_in, O_VB + hp * 512, KC, xn_lhs, xn_reg, [0, 1, 2, 3])
                        for tc in range(4):
                            S.act(lambda e, tc=tc: e.activation(out=vtokb.ap(tc * 512, (tc + 1) * 512), in_=PS(tc), func=AF.Copy), r=[PSr(tc)], w=[(vtokb, tc * 512, (tc + 1) * 512)])
                        tm_dense(wv_in, O_GB + hp * 512, KC, xn_lhs, xn_reg, [0, 1, 2, 3])
                        for tc in range(4):
                            S.act(lambda e, tc=tc: e.activation(out=gsb.ap(tc * 512, (tc + 1) * 512), in_=PS(tc), func=AF.Silu), r=[PSr(tc)], w=[(gsb, tc * 512, (tc + 1) * 512)])
                        for hh in range(2):
                            h = hp * 2 + hh
                            ssl = (h * 256, (h + 1) * 256)
                            S.act(lambda e, ssl=ssl, hh=hh: e.activation(out=Sbfb.ap(hh * 256, (hh + 1) * 256), in_=SB_.ap(*ssl), func=AF.Copy),
                                  r=[(SB_, ssl[0], ssl[1])], w=[(Sbfb, hh * 256, (hh + 1) * 256)])
                            for tc in range(4):
                                sg_ = (hh * 512 + tc * 128, hh * 512 + (tc + 1) * 128)
                                vs = (tc * 512 + hh * 256, tc * 512 + (hh + 1) * 256)
                                mm_group(PS(4, 0, 128), lambda k, sg_=sg_: ktb.ap(*sg_), lambda k, sg_=sg_: qtb.ap(*sg_), 1,
                                         r=[(ktb, sg_[0], sg_[1]), (qtb, sg_[0], sg_[1])], w=[PSr(4, 0, 128)])
                                S.dve(lambda e: e.tensor_tensor(out=PmB.ap(), in0=PS(4, 0, 128), in1=mask01.ap(), op=ALU.mult), r=[PSr(4, 0, 128), (mask01, 0, 128)], w=[(PmB, 0, 128)])
                                S.pe(lambda e, sg_=sg_: e.transpose(out=PS(7, 0, 64).bitcast(BF16), in_=ktb.ap(*sg_), identity=identb.ap()),
                                     r=[(ktb, sg_[0], sg_[1]), (identb, 0, 128)], w=[PSr(7, 0, 64)])
                                S.act(lambda e: e.activation(out=ktokb.ap(), in_=PS(7, 0, 64).bitcast(BF16), func=AF.Copy), r=[PSr(7, 0, 64)], w=[(ktokb, 0, 128)])
                                def emit_ob(e, sg_=sg_, vs=vs, hh=hh):
                                    e.matmul(PS(5, 0, 256), lhsT=PmB.ap(), rhs=vtokb.ap(*vs), start=True, stop=False)
                                    return e.matmul(PS(5, 0, 256), lhsT=qtb.ap(*sg_), rhs=Sbfb.ap(hh * 256, (hh + 1) * 256), start=False, stop=True)
                                S.pe(emit_ob, r=[(PmB, 0, 128), (vtokb, vs[0], vs[1]), (qtb, sg_[0], sg_[1]), (Sbfb, hh * 256, (hh + 1) * 256)], w=[PSr(5, 0, 256)])
                                st0 = 32 + tc * 4
                                S.act(lambda e, st0=st0: e.activation(out=junkfB.ap(0, 256), in_=PS(5, 0, 256), func=AF.Square, accum_out=stat.ap(st0, st0 + 1)),
                                      r=[PSr(5, 0, 256)], w=[(junkfB, 0, 256), (stat, st0, st0 + 1)])
                                S.act(lambda e, st0=st0: e.activation(out=stat.ap(st0 + 1, st0 + 2), in_=stat.ap(st0, st0 + 1), func=AF.Sqrt, bias=EPS, scale=1.0 / DV_B),
                                      r=[(stat, st0, st0 + 1)], w=[(stat, st0 + 1, st0 + 2)])
                                S.dve(lambda e, st0=st0: e.reciprocal(out=stat.ap(st0 + 2, st0 + 3), in_=stat.ap(st0 + 1, st0 + 2)), r=[(stat, st0 + 1, st0 + 2)], w=[(stat, st0 + 2, st0 + 3)])
                                S.dve(lambda e, st0=st0, vs=vs: e.scalar_tensor_tensor(out=otokb.ap(), in0=PS(5, 0, 256), scalar=stat.ap(st0 + 2, st0 + 3), in1=gsb.ap(*vs),
                                                                                      op0=ALU.mult, op1=ALU.mult),
                                      r=[PSr(5, 0, 256), (stat, st0 + 2, st0 + 3), (gsb, vs[0], vs[1])], w=[(otokb, 0, 256)])
                                def emit_otb(e):
                                    ins = None
                                    for j in range(2):
                                        ins = e.transpose(out=PS(7, 256, 384).bitcast(BF16)[:, j * 128:(j + 1) * 128], in_=otokb.ap(j * 128, (j + 1) * 128), identity=identb.ap())
                                    return ins
                                S.pe(emit_otb, r=[(otokb, 0, 256), (identb, 0, 128)], w=[PSr(7, 256, 384)])
                                oc0 = 16 + h * 2
                                S.act(lambda e, oc0=oc0, tc=tc: e.activation(out=oT.ap(oc0 * NT, (oc0 + 2) * NT).rearrange("p (j t) -> p j t", t=NT)[:, :, tc * 128:(tc + 1) * 128],
                                                                            in_=PS(7, 256, 384).bitcast(BF16).rearrange("p (j t) -> p j t", t=128), func=AF.Copy),
                                      r=[PSr(7, 256, 384)], w=[(oT, oc0 * NT, (oc0 + 2) * NT)])
                                mm_group(PS(6, 0, 256), lambda k: ktokb.ap(), lambda k, vs=vs: vtokb.ap(*vs), 1, r=[(ktokb, 0, 128), (vtokb, vs[0], vs[1])], w=[PSr(6, 0, 256)])
                                S.dve(lambda e, ssl=ssl, h=h: e.tensor_scalar(out=SB_.ap(*ssl), in0=SB_.ap(*ssl), scalar1=gam.ap(h, h + 1), scalar2=None, op0=ALU.mult),
                                      r=[(SB_, ssl[0], ssl[1]), (gam, h, h + 1)], w=[(SB_, ssl[0], ssl[1])])
                                S.dve(lambda e, ssl=ssl, h=h: e.scalar_tensor_tensor(out=SB_.ap(*ssl), in0=PS(6, 0, 256), scalar=gam.ap(h, h + 1), in1=SB_.ap(*ssl), op0=ALU.mult, op1=ALU.add),
                                      r=[PSr(6, 0, 256), (SB_, ssl[0], ssl[1]), (gam, h, h + 1)], w=[(SB_, ssl[0], ssl[1])])
                                S.act(lambda e, ssl=ssl, hh=hh: e.activation(out=Sbfb.ap(hh * 256, (hh + 1) * 256), in_=SB_.ap(*ssl), func=AF.Copy),
                                      r=[(SB_, ssl[0], ssl[1])], w=[(Sbfb, hh * 256, (hh + 1) * 256)])
                    if ti == n_tiles - 1:
                        S.dma("sp", ret_p[l].rearrange("h p v -> p h v"), SB_.ap().rearrange("p (h v) -> p h v", v=256), r=[(SB_, 0, 2048)], w=[("dram_ret_p", l, l + 1)])

                    ck('mixB')
                    arena.reset(tmp_base)
                    qTc = arena.alloc(BF16, 12 * 512)
                    c2_base = arena.off
                    kf = arena.alloc(F32, 1024)
                    k16 = arena.alloc(BF16, 1024)
                    kTs = arena.alloc(BF16, 1024)
                    for g in range(3):
                        W = WINDOWS[g]
                        keep_lo = SEQ - W
                        for i in range(4):
                            bk = 4 + (i % 2)
                            fm_dense(wv_in, O_HC + g * 1536 + i * 128, 128, KC, xn_fn, xn_reg, bk)
                            qs = ((g * 4 + i) * 512, (g * 4 + i + 1) * 512)
                            S.act(lambda e, qs=qs, bk=bk: e.activation(out=qTc.ap(*qs), in_=PS(bk), func=AF.Copy, scale=128 ** -0.5), r=[PSr(bk)], w=[(qTc, qs[0], qs[1])])
                        for which in (0, 1):
                            tm_dense(wv_in, O_HC + g * 1536 + 512 * (which + 1), KC, xn_lhs, xn_reg, [0, 1, 2, 3])
                            for tc in range(4):
                                tok0 = t0 + tc * 128
                                rb = (tc % 2) * 512
                                S.act(lambda e, tc=tc, rb=rb: e.activation(out=kf.ap(rb, rb + 512), in_=PS(tc), func=AF.Copy), r=[PSr(tc)], w=[(kf, rb, rb + 512)])
                                if tok0 >= keep_lo:
                                    S.dma("sp", wp[g][l, which, tok0 - keep_lo:tok0 - keep_lo + 128, :], kf.ap(rb, rb + 512), r=[(kf, rb, rb + 512)],
                                          w=[("dram_wp%d" % g, (l * 2 + which) * SEQ + tok0, (l * 2 + which) * SEQ + tok0 + 128)])
                                S.dve(lambda e, rb=rb: e.tensor_copy(out=k16.ap(rb, rb + 512), in_=kf.ap(rb, rb + 512)), r=[(kf, rb, rb + 512)], w=[(k16, rb, rb + 512)])
                                if which == 0:
                                    def emit_kT(e, rb=rb):
                                        ins = None
                                        for i in range(4):
                                            ins = e.transpose(out=PS(7, 0, 256).bitcast(BF16)[:, i * 128:(i + 1) * 128], in_=k16.ap(rb + i * 128, rb + (i + 1) * 128), identity=identb.ap())
                                        return ins
                                    S.pe(emit_kT, r=[(k16, rb, rb + 512), (identb, 0, 128)], w=[PSr(7, 0, 256)])
                                    S.act(lambda e, rb=rb: e.activation(out=kTs.ap(rb, rb + 512), in_=PS(7, 0, 256).bitcast(BF16), func=AF.Copy), r=[PSr(7, 0, 256)], w=[(kTs, rb, rb + 512)])
                                    S.dma("sp", kts[g].rearrange("p (i t) -> p i t", t=SEQ)[:, :, tok0:tok0 + 128], kTs.ap(rb, rb + 512).rearrange("p (i t) -> p i t", t=128),
                                          r=[(kTs, rb, rb + 512)], w=[("dram_kts%d" % g, tok0, tok0 + 128)])
                                else:
                                    S.dma("sp", vsc[g][tok0:tok0 + 128, :], k16.ap(rb, rb + 512), r=[(k16, rb, rb + 512)], w=[("dram_vsc%d" % g, tok0, tok0 + 128)])
                    ck('mixC1')
                    arena.reset(c2_base)
                    s_all = arena.alloc(F32, 3072)
                    p_all = arena.alloc(BF16, 3072)
                    pT = arena.alloc(BF16, 2048)
                    KTw = arena.alloc(BF16, 3712)
                    Vw = arena.alloc(BF16, 3712)
                    octok = arena.alloc(BF16, 128)
                    koff = (0, 640, 1664)
                    for i in range(4):
                        los = []
                        for g in range(3):
                            lo_g = max(0, t0 - WINDOWS[g])
                            n_g = t0 + NT - lo_g
                            los.append(lo_g)
                            S.dma("sp", KTw.ap(koff[g], koff[g] + n_g), kts[g][:, i * SEQ + lo_g:i * SEQ + t0 + NT], r=[("dram_kts%d" % g, lo_g, t0 + NT)], w=[(KTw, koff[g], koff[g] + n_g)])
                            S.dma("sp", Vw.ap(koff[g], koff[g] + n_g).rearrange("p (b d) -> p b d", d=128),
                                  vsc[g][lo_g:t0 + NT, i * 128:(i + 1) * 128].rearrange("(b s) d -> s b d", s=128),
                                  r=[("dram_vsc%d" % g, lo_g, t0 + NT)], w=[(Vw, koff[g], koff[g] + n_g)])
                        for qb in range(4):
                            q0 = t0 + qb * 128
                            col = 0
                            blocks = []
                            bankrot = 0
                            for g in range(3):
                                W = WINDOWS[g]
                                klo = max(0, q0 - W)
                                n = q0 + 128 - klo
                                tj_lo = TJ_OFF[g] + (klo - (q0 - W))
                                kbase = koff[g] + (klo - los[g])
                                qs = ((g * 4 + i) * 512 + qb * 128, (g * 4 + i) * 512 + (qb + 1) * 128)
                                coef = -alibi_slope(g, i) * DILS[g]
                                for pc in range(0, n, 512):
                                    pn = min(512, n - pc)
                                    bk = bankrot % 4
                                    bankrot += 1
                                    mm_group(PS(bk, 0, pn), lambda k, qs=qs: qTc.ap(*qs), lambda k, kbase=kbase, pc=pc, pn=pn: KTw.ap(kbase + pc, kbase + pc + pn), 1,
                                             r=[(qTc, qs[0], qs[1]), (KTw, kbase + pc, kbase + pc + pn)], w=[PSr(bk, 0, pn)])
                                    S.dve(lambda e, bk=bk, pn=pn, col=col, pc=pc, tj_lo=tj_lo, coef=coef: e.scalar_tensor_tensor(
                                        out=s_all.ap(col + pc, col + pc + pn), in0=tjb.ap(tj_lo + pc, tj_lo + pc + pn), scalar=coef, in1=PS(bk, 0, pn), op0=ALU.mult, op1=ALU.add),
                                        r=[(tjb, tj_lo + pc, tj_lo + pc + pn), PSr(bk, 0, pn)], w=[(s_all, col + pc, col + pc + pn)])
                                for kb in range(n // 128):
                                    blocks.append((col + kb * 128, kbase + kb * 128))
                                col += n
                            S.dve(lambda e, col=col: e.tensor_reduce(out=stat.ap(48, 49), in_=s_all.ap(0, col), axis=AX.X, op=ALU.max, negate=True), r=[(s_all, 0, col)], w=[(stat, 48, 49)])
                            S.act(lambda e, col=col: e.activation(out=p_all.ap(0, col), in_=s_all.ap(0, col), func=AF.Exp, bias=stat.ap(48, 49), scale=1.0, accum_out=stat.ap(49, 50)),
                                  r=[(s_all, 0, col), (stat, 48, 49)], w=[(p_all, 0, col), (stat, 49, 50)])
                            S.dve(lambda e: e.reciprocal(out=stat.ap(50, 51), in_=stat.ap(49, 50)), r=[(stat, 49, 50)], w=[(stat, 50, 51)])
                            nb = len(blocks)
                            for b8 in range(0, nb, 8):
                                bn = min(8, nb - b8)
                                half = (b8 // 8) % 2
                                def emit_pt(e, b8=b8, bn=bn, blocks=blocks):
                                    ins = None
                                    for j in range(bn):
                                        pc0 = blocks[b8 + j][0]
                                        ins = e.transpose(out=PS(7).bitcast(BF16)[:, j * 128:(j + 1) * 128], in_=p_all.ap(pc0, pc0 + 128), identity=identb.ap())
                                    return ins
                                S.pe(emit_pt, r=[(p_all, blocks[b8][0], blocks[b8 + bn - 1][0] + 128), (identb, 0, 128)], w=[PSr(7, 0, bn * 64)])
                                pts = (half * 1024, half * 1024 + bn * 128)
                                if half == 0:
                                    S.dve(lambda e, pts=pts, bn=bn: e.tensor_copy(out=pT.ap(*pts), in_=PS(7, 0, bn * 64).bitcast(BF16)), r=[PSr(7, 0, bn * 64)], w=[(pT, pts[0], pts[1])])
                                else:
                                    S.act(lambda e, pts=pts, bn=bn: e.activation(out=pT.ap(*pts), in_=PS(7, 0, bn * 64).bitcast(BF16), func=AF.Copy), r=[PSr(7, 0, bn * 64)], w=[(pT, pts[0], pts[1])])
                                def emit_pv(e, b8=b8, bn=bn, half=half, nb=nb, blocks=blocks):
                                    ins = None
                                    for j in range(bn):
                                        vb0 = blocks[b8 + j][1]
                                        ins = e.matmul(PS(6, 0, 128), lhsT=pT.ap(half * 1024 + j * 128, half * 1024 + (j + 1) * 128), rhs=Vw.ap(vb0, vb0 + 128),
                                                       start=(b8 + j == 0), stop=(b8 + j == nb - 1))
                                    return ins
                                S.pe(emit_pv, r=[(pT, pts[0], pts[1]), (Vw, 0, 3712)], w=[PSr(6, 0, 128)])
                            S.act(lambda e: e.activation(out=octok.ap(), in_=PS(6, 0, 128), func=AF.Copy, scale=stat.ap(50, 51)), r=[PSr(6, 0, 128), (stat, 50, 51)], w=[(octok, 0, 128)])
                            S.pe(lambda e: e.transpose(out=PS(5, 256, 320).bitcast(BF16), in_=octok.ap(), identity=identb.ap()), r=[(octok, 0, 128), (identb, 0, 128)], w=[PSr(5, 256, 320)])
                            od = ((32 + i) * NT + qb * 128, (32 + i) * NT + (qb + 1) * 128)
                            S.dve(lambda e, od=od: e.tensor_copy(out=oT.ap(*od), in_=PS(5, 256, 320).bitcast(BF16)), r=[PSr(5, 256, 320)], w=[(oT, od[0], od[1])])

                    ck('mixC')
                    arena.reset(tmp_base + 32 * NT * 2)
                    sg = [arena.alloc(F32, 512) for _ in range(3)]
                    acc = arena.alloc(F32, 512)
                    wv_mg = wview(w_merge[l])
                    wv_ups = [(wview(w_up_a[l]), 16, 0), (wview(w_up_b[l]), 16, 16), (wview(w_up_c[l]), 4, 32)]
                    oT_reg = (oT, 0, 36 * NT)
                    for j in range(KC):
                        for b in range(3):
                            fm_dense(wv_mg, b * D + j * 128, 128, KC, xn_fn, xn_reg, b)
                            S.act(lambda e, b=b: e.activation(out=sg[b].ap(), in_=PS(b), func=AF.Sigmoid), r=[PSr(b)], w=[(sg[b], 0, 512)])
                        for b, (wvu, kn, k0) in enumerate(wv_ups):
                            fm_dense(wvu, j * 128, 128, kn, (lambda k, k0=k0: oT.ap((k0 + k) * NT, (k0 + k + 1) * NT)), oT_reg, 3 + b)
                        S.dve(lambda e: e.tensor_tensor(out=acc.ap(), in0=PS(3), in1=sg[0].ap(), op=ALU.mult), r=[PSr(3), (sg[0], 0, 512)], w=[(acc, 0, 512)])
                        S.dve(lambda e: e.tensor_tensor(out=sg[1].ap(), in0=PS(4), in1=sg[1].ap(), op=ALU.mult), r=[PSr(4), (sg[1], 0, 512)], w=[(sg[1], 0, 512)])
                        S.dve(lambda e: e.tensor_tensor(out=sg[2].ap(), in0=PS(5), in1=sg[2].ap(), op=ALU.mult), r=[PSr(5), (sg[2], 0, 512)], w=[(sg[2], 0, 512)])
                        S.dve(lambda e: e.tensor_tensor(out=acc.ap(), in0=acc.ap(), in1=sg[1].ap(), op=ALU.add), r=[(acc, 0, 512), (sg[1], 0, 512)], w=[(acc, 0, 512)])
                        S.dve(lambda e, j=j: e.tensor_tensor(out=mT.ap(j * NT, (j + 1) * NT), in0=acc.ap(), in1=sg[2].ap(), op=ALU.add),
                              r=[(acc, 0, 512), (sg[2], 0, 512)], w=[(mT, j * NT, (j + 1) * NT)])
                    wv_o = wview(w_out[l])
                    resid_phase(wv_o, KC, (lambda k, tc: mT.ap(k * NT + tc * 128, k * NT + (tc + 1) * 128)), (mT, 0, KC * NT), [0, 1, 2, 3], x_src, xa, t0)

                    ck('merge')
                    arena.reset()
                    rowbuf = arena.alloc(F32, D)
                    xnjunk = arena.alloc(BF16, D)
                    norm_tile(xa, t0, gffn, rowbuf)
                    arena.reset()
                    hT = arena.alloc(BF16, FKC * NT)
                    wv_g, wv_u = wview(w_fg[l]), wview(w_fu[l])
                    for j in range(FKC):
                        bg, bu = (j % 2) * 2, (j % 2) * 2 + 1
                        fm_dense(wv_g, j * 128, 128, KC, xn_fn, xn_reg, bg)
                        fm_dense(wv_u, j * 128, 128, KC, xn_fn, xn_reg, bu)
                        S.act(lambda e, bg=bg: e.activation(out=sgf.ap(), in_=PS(bg), func=AF.Silu), r=[PSr(bg)], w=[(sgf, 0, 512)])
                        S.dve(lambda e, bu=bu, j=j: e.tensor_tensor(out=hT.ap(j * NT, (j + 1) * NT), in0=PS(bu), in1=sgf.ap(), op=ALU.mult),
                              r=[PSr(bu), (sgf, 0, 512)], w=[(hT, j * NT, (j + 1) * NT)])
                    resid_phase(wview(w_fd[l]), FKC, (lambda k, tc: hT.ap(k * NT + tc * 128, k * NT + (tc + 1) * 128)), (hT, 0, FKC * NT), [4, 5, 6, 7], xa, xb, t0)

            ck('ffn')
            arena.reset()
            rowbuf = arena.alloc(F32, D)
            xnjunk = arena.alloc(BF16, D)
            gfb = arena.alloc(F32, D)
            S.dma("sp", gfb.ap(), g_final_row.partition_broadcast(128), w=[(gfb, 0, D)])
            S.dma("sp", rowbuf.ap(0, D, 0, 1), xsb, r=[("dram_xsb", 0, 1)], w=[(rowbuf, 0, D)])
            S.act(lambda e: e.activation(out=xnjunk.ap(0, D, 0, 1), in_=rowbuf.ap(0, D, 0, 1), func=AF.Square, accum_out=stat.ap(0, 1, 0, 1)), r=[(rowbuf, 0, D)], w=[(xnjunk, 0, D), (stat, 0, 1)])
            S.act(lambda e: e.activation(out=stat.ap(1, 2, 0, 1), in_=stat.ap(0, 1, 0, 1), func=AF.Sqrt, bias=EPS, scale=1.0 / D), r=[(stat, 0, 1)], w=[(stat, 1, 2)])
            S.dve(lambda e: e.reciprocal(out=stat.ap(2, 3, 0, 1), in_=stat.ap(1, 2, 0, 1)), r=[(stat, 1, 2)], w=[(stat, 2, 3)])
            S.dve(lambda e: e.scalar_tensor_tensor(out=rowbuf.ap(0, D, 0, 1), in0=rowbuf.ap(0, D, 0, 1), scalar=stat.ap(2, 3, 0, 1), in1=gfb.ap(0, D, 0, 1), op0=ALU.mult, op1=ALU.mult),
                  r=[(rowbuf, 0, D), (stat, 2, 3), (gfb, 0, D)], w=[(rowbuf, 0, D)])
            S.dma("sp", ys, rowbuf.ap(0, D, 0, 1), r=[(rowbuf, 0, D)], w=[("dram_ys", 0, 1)])
            for r0 in range(0, n_tiles * NT, 128):
                S.dma("sp", rowbuf.ap(), xb[r0:r0 + 128, :], r=[("dram_xb", r0, r0 + 128)], w=[(rowbuf, 0, D)])
                S.act(lambda e: e.activation(out=xnjunk.ap(), in_=rowbuf.ap(), func=AF.Square, accum_out=stat.ap(0, 1)), r=[(rowbuf, 0, D)], w=[(xnjunk, 0, D), (stat, 0, 1)])
                S.act(lambda e: e.activation(out=stat.ap(1, 2), in_=stat.ap(0, 1), func=AF.Sqrt, bias=EPS, scale=1.0 / D), r=[(stat, 0, 1)], w=[(stat, 1, 2)])
                S.dve(lambda e: e.reciprocal(out=stat.ap(2, 3), in_=stat.ap(1, 2)), r=[(stat, 1, 2)], w=[(stat, 2, 3)])
                S.dve(lambda e: e.scalar_tensor_tensor(out=rowbuf.ap(), in0=rowbuf.ap(), scalar=stat.ap(2, 3), in1=gfb.ap(), op0=ALU.mult, op1=ALU.mult),
                      r=[(rowbuf, 0, D), (stat, 2, 3), (gfb, 0, D)], w=[(rowbuf, 0, D)])
                S.dma("sp", yp[r0:r0 + 128, :], rowbuf.ap(), r=[(rowbuf, 0, D)], w=[("dram_yp", r0, r0 + 128)])

        except _Stop:
            pass
        S.replay(block)
        nc_stats = (dict(S.cnt), list(S.sp_cnt), list(S.pool_cnt))
    build_program.stats = nc_stats
    return nc


_NC_CACHE = {}


def kernel(**inputs):
    n = 8
    if "nc" not in _NC_CACHE:
        _NC_CACHE["nc"] = build_program()
    nc = _NC_CACHE["nc"]
    consts = make_consts()
    f = lambda a: np.ascontiguousarray(np.asarray(a, dtype=np.float32))
    shared = {k: f(inputs[k]) for k in ("w_in", "w_alpha2", "g_gla", "w_merge", "w_up_a", "w_up_b", "w_up_c", "w_out",
                                       "w_ffn_gate", "w_ffn_up", "w_ffn_down")}
    shared["g_mix_t"] = f(np.asarray(inputs["g_mix"]).reshape(DEPTH, KC, 128).transpose(0, 2, 1))
    shared["g_ffn_t"] = f(np.asarray(inputs["g_ffn"]).reshape(DEPTH, KC, 128).transpose(0, 2, 1))
    shared["g_final_t"] = f(np.asarray(inputs["g_final"]).reshape(KC, 128).T)
    shared["g_final_row"] = f(inputs["g_final"])
    shared["g_mix_r"] = f(inputs["g_mix"])
    shared["g_ffn_r"] = f(inputs["g_ffn"])
    shared["b_alpha_r"] = f(inputs["b_alpha"])
    shared["b_alpha_t"] = f(np.asarray(inputs["b_alpha"]).reshape(DEPTH, 8, 128).transpose(0, 2, 1))
    for k, v in consts.items():
        shared["c_" + k] = v
    in_maps = []
    for c in range(n):
        m = dict(shared)
        m["xp"] = f(inputs["x_prompt"][c % 4])
        m["xs"] = f(inputs["x_sample"][c])
        m["sgla"] = f(inputs["state_gla"][:, c])
        m["sret"] = f(inputs["state_ret"][:, c])
        for g in range(3):
            cwg = np.asarray(inputs["cache_win%d" % g])[:, c]
            m["cw%d" % g] = f(cwg.reshape(DEPTH, 2, WINDOWS[g], 512))
        in_maps.append(m)
    res = run_bass_kernel_spmd(nc, in_maps, core_ids=list(range(n)))
    R = res.results
    y_prompt = np.stack([R[b]["yp"] for b in range(4)])
    y_sample = np.stack([R[c]["ys"] for c in range(8)])
    outs = [y_prompt, y_sample]
    for nm in ("gla", "ret"):
        outs.append(np.stack([R[b][nm + "_p"] for b in range(4)], axis=1))
        outs.append(np.stack([R[c][nm + "_s"] for c in range(8)], axis=1))
    for g in range(3):
        W = WINDOWS[g]
        outs.append(np.stack([R[b]["w%dp" % g] for b in range(4)], axis=1).reshape(DEPTH, 4, 2, W, 4, 128))
        outs.append(np.stack([R[c]["w%ds" % g] for c in range(8)], axis=1).reshape(DEPTH, 8, 2, W, 4, 128))
    return tuple(np.ascontiguousarray(o, dtype=np.float32) for o in outs)
```

```python
import numpy as np
import ml_dtypes
import concourse.bass as bass
import concourse.mybir as mybir
from concourse.bass_utils import run_bass_kernel_spmd

F32 = mybir.dt.float32
BF16 = mybir.dt.bfloat16
AF = mybir.ActivationFunctionType
ALU = mybir.AluOpType
AX = mybir.AxisListType

D = 4096
SEQ = 2048
DEPTH = 2
NT = 512
NTILES = SEQ // NT
KC = D // 128
H_A, DK_A, DV_A = 4, 256, 512
H_B, DK_B, DV_B = 8, 128, 256
WINDOWS = (128, 512, 2048)
DILS = (1, 4, 16)
N_IN = 16912
D_FF = 11008
FKC = D_FF // 128
PAST = 16384
EPS = 1e-6
O_QA, O_KA, O_VA, O_RA, O_ZA, O_QB, O_KB, O_VB, O_GB, O_HC = 0, 1024, 2048, 4096, 6144, 6160, 7184, 8208, 10256, 12304
TJ_OFF = (0, 256, 896)
NS = 4
SLOT = 4096


def _esz(dt):
    return 4 if dt == F32 else 2


class Buf:
    def __init__(self, t, name, tdt, dt, base_bytes, n, space="sb"):
        self.t, self.name, self.tdt, self.dt, self.base, self.n, self.space = t, name, tdt, dt, base_bytes, n, space
        self.esz = _esz(dt)

    def ap(self, lo=0, hi=None, p0=0, p1=128):
        hi = self.n if hi is None else hi
        b0 = self.base + lo * self.esz
        b1 = self.base + hi * self.esz
        te = _esz(self.tdt)
        assert b0 % te == 0 and b1 % te == 0
        v = self.t[p0:p1, b0 // te:b1 // te]
        if self.dt != self.tdt:
            v = v.bitcast(self.dt)
        return v

    def reg(self, lo=0, hi=None):
        hi = self.n if hi is None else hi
        return (self.name, self.base + lo * self.esz, self.base + hi * self.esz)


class Sched:
    def __init__(self, nc, sems):
        self.nc = nc
        self.ops = {e: [] for e in ("pe", "dve", "act", "pool", "sp")}
        self.sem = sems
        self.cnt = {e: 0 for e in ("pe", "dve", "act")}
        self.waited = {}
        self.track = {}
        self.sp_ch = 0
        self.sp_cnt = [0] * 8
        self.pool_cnt = [0] * NS

    def _deps(self, regs_r, regs_w):
        deps = set()
        for (name, lo, hi) in regs_r:
            for rec in self.track.get(name, []):
                if rec[2] == "w" and rec[0] < hi and lo < rec[1]:
                    deps.add(rec[3])
        for (name, lo, hi) in regs_w:
            for rec in self.track.get(name, []):
                if rec[0] < hi and lo < rec[1]:
                    deps.add(rec[3])
        return deps

    def _record(self, regs_r, regs_w, tag):
        for (name, lo, hi) in regs_w:
            lst = self.track.setdefault(name, [])
            new = []
            for rec in lst:
                if rec[0] >= lo and rec[1] <= hi:
                    continue
                new.append(rec)
            new.append([lo, hi, "w", tag])
            self.track[name] = new
        for (name, lo, hi) in regs_r:
            lst = self.track.setdefault(name, [])
            new = []
            for rec in lst:
                if rec[2] == "r" and rec[0] >= lo and rec[1] <= hi and rec[3][0] == tag[0]:
                    continue
                new.append(rec)
            new.append([lo, hi, "r", tag])
            self.track[name] = new

    def _waits(self, eng, deps):
        best = {}
        for (sn, val) in deps:
            if val > best.get(sn, 0):
                best[sn] = val
        out = []
        for sn, val in best.items():
            if self.waited.get((eng, sn), 0) < val:
                self.waited[(eng, sn)] = val
                out.append((sn, val))
        return out

    def op(self, eng, emit, r=(), w=()):
        rr = [b.reg(lo, hi) for (b, lo, hi) in r]
        ww = [b.reg(lo, hi) for (b, lo, hi) in w]
        waits = self._waits(eng, self._deps(rr, ww))
        self.cnt[eng] += 1
        tag = ("s_" + eng, self.cnt[eng])
        self.ops[eng].append((waits, emit, ("s_" + eng, 1)))
        self._record(rr, ww, tag)

    def pe(self, emit, r=(), w=()):
        self.op("pe", emit, r, w)

    def dve(self, emit, r=(), w=()):
        self.op("dve", emit, r, w)

    def act(self, emit, r=(), w=()):
        self.op("act", emit, r, w)

    def dma(self, q, out, in_, r=(), w=(), slot=None):
        rr = [x if isinstance(x[0], str) else x[0].reg(x[1], x[2]) for x in r]
        ww = [x if isinstance(x[0], str) else x[0].reg(x[1], x[2]) for x in w]
        deps = self._deps(rr, ww)
        if q == "pool":
            ch = slot
            sn = "s_w%d" % ch
            prev = self.pool_cnt[ch]
            self.pool_cnt[ch] += 1
            val = 16 * self.pool_cnt[ch]
        else:
            ch = self.sp_ch
            self.sp_ch = (self.sp_ch + 1) % 8
            sn = "s_d%d" % ch
            prev = self.sp_cnt[ch]
            self.sp_cnt[ch] += 1
            val = 16 * self.sp_cnt[ch]
        if prev:
            deps.add((sn, 16 * prev))
        waits = self._waits(q, deps)
        self.ops[q].append((waits, (lambda e, out=out, in_=in_: e.dma_start(out=out, in_=in_)), (sn, 16)))
        self._record(rr, ww, (sn, val))

    def final_waits(self):
        out = []
        for ch in range(8):
            if self.sp_cnt[ch]:
                out.append(("s_d%d" % ch, 16 * self.sp_cnt[ch]))
        for ch in range(NS):
            if self.pool_cnt[ch]:
                out.append(("s_w%d" % ch, 16 * self.pool_cnt[ch]))
        return out

    def replay(self, block):
        nc = self.nc
        sem = self.sem

        def run(e, lst, extra=None):
            for (waits, emit, inc) in lst:
                for (sn, val) in waits:
                    e.wait_ge(sem[sn], val)
                ins = emit(e)
                ins.then_inc(sem[inc[0]], inc[1])
            if extra:
                for (sn, val) in extra:
                    e.wait_ge(sem[sn], val)

        fw = self.final_waits()

        @block.tensor
        def _(e):
            run(e, self.ops["pe"])

        @block.vector
        def _(e):
            run(e, self.ops["dve"])

        @block.scalar
        def _(e):
            run(e, self.ops["act"])

        @block.gpsimd
        def _(e):
            run(e, self.ops["pool"])

        @block.sync
        def _(e):
            run(e, self.ops["sp"], extra=fw)


class Arena:
    def __init__(self, t, name, tdt, nbytes, base=0):
        self.t, self.name, self.tdt, self.nbytes, self.base = t, name, tdt, nbytes, base
        self.off = 0

    def reset(self, off=0):
        self.off = off

    def alloc(self, dt, n):
        e = _esz(dt)
        self.off = (self.off + 3) // 4 * 4
        b = Buf(self.t, self.name, self.tdt, dt, self.base + self.off, n)
        self.off += n * e
        assert self.off <= self.nbytes, (self.name, self.off, self.nbytes)
        return b


def alibi_slope(g, i):
    return 2.0 ** (-8.0 * (g * 4 + i + 1) / 12.0)


def make_consts():
    c = {}
    c["identf"] = np.eye(128, dtype=np.float32)
    s = np.arange(128)
    c["mask01"] = (s[:, None] <= s[None, :]).astype(np.float32)
    perm = np.zeros((128, 128), np.float32)
    for m in range(128):
        perm[(m + 64) % 128, m] = 1.0
    c["permf"] = perm
    half = 64
    inv = (10000.0 ** (-np.arange(half, dtype=np.float32) / half)).astype(np.float32)
    pos = np.concatenate([np.arange(SEQ), [PAST]]).astype(np.float32)
    ang = (pos[None, :] * inv[:, None]).astype(np.float32).astype(np.float64)
    cos = np.cos(ang)
    sin = np.sin(ang)
    c["cost"] = np.concatenate([cos, cos], 0).astype(np.float32)
    c["sint"] = np.concatenate([-sin, sin], 0).astype(np.float32)
    lg = np.log1p(-(2.0 ** (-5.0 - np.arange(H_B, dtype=np.float64))))
    t = np.arange(128, dtype=np.float64)
    ep = np.exp(lg[:, None] * (t[None, :] + 1))
    em = np.exp(-lg[:, None] * (t[None, :] + 1)) * (DK_B ** -0.5)
    c["rEp"] = np.broadcast_to(ep[None], (128, 8, 128)).reshape(128, 1024).astype(np.float32).copy()
    c["rEm"] = np.broadcast_to(em[None], (128, 8, 128)).reshape(128, 1024).astype(np.float32).copy()
    g128 = np.exp(lg * 128)
    g1 = np.exp(lg)
    c["gam"] = np.broadcast_to(np.concatenate([g128, g1])[None], (128, 16)).astype(np.float32).copy()
    tj = np.full((128, 3072), 1e30, np.float64)
    p = np.arange(128)[:, None]
    for g in range(3):
        W, d = WINDOWS[g], DILS[g]
        cc = np.arange(W + 128)[None, :]
        delta = p + W - cc
        valid = (delta >= 0) & (delta % d == 0) & (delta // d <= 128)
        tj[:, TJ_OFF[g]:TJ_OFF[g] + W + 128] = np.where(valid, delta // d, 1e30)
    c["tj"] = tj.astype(np.float32)
    c["onesf"] = np.ones((128, 128), np.float32)
    c["csrow"] = np.concatenate([cos[:, SEQ], sin[:, SEQ]])[None, :].astype(np.float32)
    sb_ = np.zeros((128, 12), np.float64)
    r = np.arange(128)
    for g in range(3):
        for i in range(4):
            sb_[:, g * 4 + i] = -alibi_slope(g, i) * DILS[g] * (128 - r)
    c["sbias"] = sb_.astype(np.float32)
    return c


class _Stop(Exception):
    pass


def build_program(n_layers=DEPTH, n_tiles=NTILES, stop=None, skip_sample=False):
    def ck(name):
        if stop == name:
            raise _Stop()

    nc = bass.Bass("TRN2", target_bir_lowering=False)

    def din(name, shape):
        return nc.dram_tensor(name, list(shape), F32, kind="ExternalInput").ap()

    def dout(name, shape):
        return nc.dram_tensor(name, list(shape), F32, kind="ExternalOutput").ap()

    xp = din("xp", (SEQ, D))
    xs = din("xs", (1, D))
    sgla = din("sgla", (DEPTH, H_A, DK_A, DV_A))
    sret = din("sret", (DEPTH, H_B, DK_B, DV_B))
    cw = [din("cw%d" % g, (DEPTH, 2, WINDOWS[g], 512)) for g in range(3)]
    g_mix = din("g_mix_t", (DEPTH, 128, KC))
    w_in = din("w_in", (DEPTH, D, N_IN))
    w_alpha2 = din("w_alpha2", (DEPTH, 16, 1024))
    b_alpha = din("b_alpha_t", (DEPTH, 128, 8))
    g_gla = din("g_gla", (DEPTH, 512))
    w_merge = din("w_merge", (DEPTH, D, 3 * D))
    w_up_a = din("w_up_a", (DEPTH, 2048, D))
    w_up_b = din("w_up_b", (DEPTH, 2048, D))
    w_up_c = din("w_up_c", (DEPTH, 512, D))
    w_out = din("w_out", (DEPTH, D, D))
    g_ffn = din("g_ffn_t", (DEPTH, 128, KC))
    w_fg = din("w_ffn_gate", (DEPTH, D, D_FF))
    w_fu = din("w_ffn_up", (DEPTH, D, D_FF))
    w_fd = din("w_ffn_down", (DEPTH, D_FF, D))
    g_final = din("g_final_t", (128, KC))
    g_final_row = din("g_final_row", (D,))
    g_mix_r = din("g_mix_r", (DEPTH, D))
    g_ffn_r = din("g_ffn_r", (DEPTH, D))
    b_alpha_r = din("b_alpha_r", (DEPTH, 1024))
    cn = {k: din("c_" + k, v.shape) for k, v in make_consts().items()}

    yp = dout("yp", (SEQ, D))
    ys = dout("ys", (1, D))
    gla_p = dout("gla_p", (DEPTH, H_A, DK_A, DV_A))
    gla_s = dout("gla_s", (DEPTH, H_A, DK_A, DV_A))
    ret_p = dout("ret_p", (DEPTH, H_B, DK_B, DV_B))
    ret_s = dout("ret_s", (DEPTH, H_B, DK_B, DV_B))
    wp = [dout("w%dp" % g, (DEPTH, 2, WINDOWS[g], 512)) for g in range(3)]
    ws = [dout("w%ds" % g, (DEPTH, 2, WINDOWS[g], 512)) for g in range(3)]

    xa = nc.dram_tensor("xa", [SEQ, D], F32).ap()
    xb = nc.dram_tensor("xb", [SEQ, D], F32).ap()
    kts = [nc.dram_tensor("kts%d" % g, [128, 4 * SEQ], BF16).ap() for g in range(3)]
    vsc = [nc.dram_tensor("vsc%d" % g, [SEQ, 512], BF16).ap() for g in range(3)]
    xsa = nc.dram_tensor("xsa", [1, D], F32).ap()
    WCN = 214
    wcaches = [nc.dram_tensor("wcache%d" % i, [WCN, 128, SLOT], BF16).ap() for i in range(3)]
    xsb = nc.dram_tensor("xsb", [1, D], F32).ap()

    sem_names = ["s_pe", "s_dve", "s_act"] + ["s_d%d" % i for i in range(8)] + ["s_w%d" % i for i in range(NS)]
    import contextlib
    with contextlib.ExitStack() as es:
        def sb(name, n, dt):
            return es.enter_context(nc.sbuf_tensor(name, [128, n], dt))

        t_xnT = sb("xnT", KC * NT, BF16)
        t_wsl = sb("wsl", NS * SLOT, BF16)
        ARENA_B = FKC * NT * 2
        t_arena = sb("arena", ARENA_B // 2, BF16)
        t_SA = sb("SA", 8 * 512, F32)
        t_SB = sb("SB", 8 * 256, F32)
        t_cf = sb("cf", 4864, F32)
        t_tj = sb("tj", 3072, BF16)
        t_misc = sb("misc", 1792, F32)
        t_ps = es.enter_context(nc.psum_tensor("ps", [128, 8 * 512], F32))
        sems = {n: es.enter_context(nc.semaphore(n)) for n in sem_names}
        block = es.enter_context(nc.Block())
        S = Sched(nc, sems)

        xnT = Buf(t_xnT, "xnT", BF16, BF16, 0, KC * NT)
        wsl = Buf(t_wsl, "wsl", BF16, BF16, 0, NS * SLOT)
        SA = Buf(t_SA, "SA", F32, F32, 0, 4096)
        SB_ = Buf(t_SB, "SB", F32, F32, 0, 2048)
        tjb = Buf(t_tj, "tj", BF16, BF16, 0, 3072)
        cfA = Arena(t_cf, "cf", F32, 4864 * 4)
        identf = cfA.alloc(F32, 128)
        mask01 = cfA.alloc(F32, 128)
        permf = cfA.alloc(F32, 128)
        onesf = cfA.alloc(F32, 128)
        cosb = cfA.alloc(F32, 512)
        sinb = cfA.alloc(F32, 512)
        rEp = cfA.alloc(F32, 1024)
        rEm = cfA.alloc(F32, 1024)
        gam = cfA.alloc(F32, 16)
        gmix = cfA.alloc(F32, 32)
        gffn = cfA.alloc(F32, 32)
        gfin = cfA.alloc(F32, 32)
        ggla = cfA.alloc(F32, 512)
        negb = cfA.alloc(F32, 8)
        identb = cfA.alloc(BF16, 128)
        wa2 = cfA.alloc(BF16, 1024)
        mA = Arena(t_misc, "misc", F32, 1792 * 4)
        stat = mA.alloc(F32, 64)
        arena = Arena(t_arena, "arena", BF16, ARENA_B)
        psb = [Buf(t_ps, "ps", F32, F32, i * 2048, 512, space="ps") for i in range(8)]

        wslot_ptr = [0]

        def PS(i, lo=0, hi=512, p0=0, p1=128):
            return psb[i].ap(lo, hi, p0, p1)

        def PSr(i, lo=0, hi=512):
            return (psb[i], 0, 512)

        try:
            def cload(buf, src, n=None, q="sp"):
                S.dma(q, buf.ap(0, n), src, w=[(buf, 0, n if n else buf.n)])

            cload(identf, cn["identf"])
            cload(mask01, cn["mask01"])
            cload(permf, cn["permf"])
            cload(onesf, cn["onesf"])
            cload(rEp, cn["rEp"])
            cload(rEm, cn["rEm"])
            cload(gam, cn["gam"])
            cload(gfin, g_final)
            S.act(lambda e: e.activation(out=identb.ap(), in_=identf.ap(), func=AF.Copy), r=[(identf, 0, 128)], w=[(identb, 0, 128)])
            S.dma("pool", tjb.ap().rearrange("p (a b) -> p a b", b=512), cn["tj"].rearrange("p (a b) -> p a b", b=512), w=[(tjb, 0, 3072)], slot=0)

            wseq = [0]
            wmode = ["stream"]

            def wload(src3, k, m):
                s = wslot_ptr[0]
                wslot_ptr[0] = (s + 1) % NS
                lo, hi = s * SLOT, s * SLOT + k * m
                dst = wsl.ap(lo, hi).rearrange("p (k c) -> p k c", c=m)
                if wmode[0] == "cached":
                    q = wseq[0]
                    wseq[0] += 1
                    S.dma("pool", wsl.ap(lo, hi), wcaches[q // WCN][q % WCN, :, 0:k * m], r=[("dram_wc", q, q + 1)], w=[(wsl, lo, hi)], slot=s)
                else:
                    S.dma("pool", dst, src3, w=[(wsl, lo, hi)], slot=s)
                    if wmode[0] == "fill":
                        q = wseq[0]
                        wseq[0] += 1
                        S.dma("sp", wcaches[q // WCN][q % WCN, :, 0:k * m], wsl.ap(lo, hi), r=[(wsl, lo, hi)], w=[("dram_wc", q, q + 1)])
                return dst, (wsl, lo, hi)

            def wview(wmat):
                return wmat.rearrange("(k p) c -> p k c", p=128)

            def mm_group(out_ap, lhs_fn, rhs_fn, nk, r, w, first=True, last=True):
                def emit(e):
                    ins = None
                    for k in range(nk):
                        ins = e.matmul(out_ap, lhsT=lhs_fn(k), rhs=rhs_fn(k), start=(first and k == 0), stop=(last and k == nk - 1))
                    return ins
                S.pe(emit, r=r, w=w)

            def fm_dense(wv, c0, m, kcn, act_fn, act_reg, out_bank, n=NT, mp=None):
                sv, sreg = wload(wv[:, 0:kcn, c0:c0 + m], kcn, m)
                mm_group(PS(out_bank, 0, n, 0, m), lambda k: sv[:, k, :], act_fn, kcn, r=[sreg, act_reg], w=[PSr(out_bank, 0, n)])

            def xn_fn(k):
                return xnT.ap(k * NT, (k + 1) * NT)
            xn_reg = (xnT, 0, KC * NT)

            def tm_dense(wv, c0, kcn, lhs_fn, lhs_reg, banks, ncols=512):
                nparts = (kcn + 7) // 8
                for kp in range(nparts):
                    k0 = kp * 8
                    kn = min(8, kcn - k0)
                    sv, sreg = wload(wv[:, k0:k0 + kn, c0:c0 + ncols], kn, ncols)
                    for tc in range(4):
                        mm_group(PS(banks[tc], 0, ncols), (lambda k, tc=tc, k0=k0: lhs_fn(k0 + k, tc)), (lambda k, sv=sv: sv[:, k, :]), kn,
                                 r=[sreg, lhs_reg], w=[PSr(banks[tc], 0, ncols)], first=(kp == 0), last=(kp == nparts - 1))

            def xn_lhs(k, tc):
                return xnT.ap(k * NT + tc * 128, k * NT + (tc + 1) * 128)

            def norm_tile(src, t0, gtab, rowbuf):
                for tc in range(4):
                    r0 = t0 + tc * 128
                    S.dma("sp", rowbuf.ap(), src[r0:r0 + 128, :], r=[("dram_" + src.tensor.name, r0, r0 + 128)], w=[(rowbuf, 0, D)])
                    ssq = (stat, tc * 4, tc * 4 + 1)
                    sd = (stat, tc * 4 + 1, tc * 4 + 2)
                    rs = (stat, tc * 4 + 2, tc * 4 + 3)
                    junk = xnjunk
                    S.act(lambda e, tc=tc: e.activation(out=junk.ap(), in_=rowbuf.ap(), func=AF.Square, accum_out=stat.ap(tc * 4, tc * 4 + 1)),
                          r=[(rowbuf, 0, D)], w=[(junk, 0, D), ssq])
                    S.act(lambda e, tc=tc: e.activation(out=stat.ap(tc * 4 + 1, tc * 4 + 2), in_=stat.ap(tc * 4, tc * 4 + 1), func=AF.Sqrt, bias=EPS, scale=1.0 / D),
                          r=[ssq], w=[sd])
                    S.dve(lambda e, tc=tc: e.reciprocal(out=stat.ap(tc * 4 + 2, tc * 4 + 3), in_=stat.ap(tc * 4 + 1, tc * 4 + 2)), r=[sd], w=[rs])
                    S.dve(lambda e, tc=tc: e.tensor_scalar(out=rowbuf.ap(), in0=rowbuf.ap(), scalar1=stat.ap(tc * 4 + 2, tc * 4 + 3), scalar2=None, op0=ALU.mult),
                          r=[(rowbuf, 0, D), rs], w=[(rowbuf, 0, D)])
                    for k4 in range(8):
                        bank = k4 % 2
                        def emit(e, k4=k4, bank=bank):
                            ins = None
                            for j in range(4):
                                k = k4 * 4 + j
                                ins = e.transpose(out=PS(bank, j * 128, (j + 1) * 128), in_=rowbuf.ap(k * 128, (k + 1) * 128), identity=identf.ap())
                            return ins
                        S.pe(emit, r=[(rowbuf, k4 * 512, (k4 + 1) * 512), (identf, 0, 128)], w=[PSr(bank)])
                        for j in range(4):
                            k = k4 * 4 + j
                            eng = S.dve if (j % 2 == 0) else S.act
                            if False:
                                S.dve(lambda e, k=k, j=j, bank=bank, tc=tc: e.tensor_scalar(out=xnT.ap(k * NT + tc * 128, k * NT + (tc + 1) * 128), in0=PS(bank, j * 128, (j + 1) * 128),
                                                                                         scalar1=gtab.ap(k, k + 1), scalar2=None, op0=ALU.mult),
                                      r=[PSr(bank, j * 128, (j + 1) * 128), (gtab, k, k + 1)], w=[(xnT, k * NT + tc * 128, k * NT + (tc + 1) * 128)])
                            else:
                                S.act(lambda e, k=k, j=j, bank=bank, tc=tc: e.activation(out=xnT.ap(k * NT + tc * 128, k * NT + (tc + 1) * 128), in_=PS(bank, j * 128, (j + 1) * 128),
                                                                                      func=AF.Copy, scale=gtab.ap(k, k + 1)),
                                      r=[PSr(bank, j * 128, (j + 1) * 128), (gtab, k, k + 1)], w=[(xnT, k * NT + tc * 128, k * NT + (tc + 1) * 128)])

            sgf = mA.alloc(F32, 512)
            xpc = [mA.alloc(F32, 512) for _ in range(2)]
            xpc_i = [0]

            def resid_phase(wv, kcn, lhs_fn, lhs_reg, banks, src, dst, t0):
                sname, dname = "dram_" + src.tensor.name, "dram_" + dst.tensor.name
                for cb in range(D // 512):
                    tm_dense(wv, cb * 512, kcn, lhs_fn, lhs_reg, banks)
                    for tc in range(4):
                        r0 = t0 + tc * 128
                        xb_ = xpc[xpc_i[0] % 2]
                        xpc_i[0] += 1
                        S.dma("sp", xb_.ap(), src[r0:r0 + 128, cb * 512:(cb + 1) * 512], r=[(sname, r0, r0 + 128)], w=[(xb_, 0, 512)])
                        S.dve(lambda e, xb_=xb_, bk=banks[tc]: e.tensor_tensor(out=xb_.ap(), in0=PS(bk), in1=xb_.ap(), op=ALU.add), r=[PSr(banks[tc]), (xb_, 0, 512)], w=[(xb_, 0, 512)])
                        S.dma("sp", dst[r0:r0 + 128, cb * 512:(cb + 1) * 512], xb_.ap(), r=[(xb_, 0, 512)], w=[(dname, r0, r0 + 128)])


            colA = Arena(t_xnT, "xnT", BF16, KC * NT * 2)

            def row_to_cols(row, off, nchunks, emit_copy):
                for c0 in range(0, nchunks, 8):
                    cn_ = min(8, nchunks - c0)
                    def emit(e, c0=c0, cn_=cn_):
                        ins = None
                        for c in range(c0, c0 + cn_):
                            ins = e.transpose(out=PS(7, c, c + 1), in_=row.ap(off + c * 128, off + (c + 1) * 128, 0, 1), identity=identf.ap(0, 1, 0, 1))
                        return ins
                    S.pe(emit, r=[(row, off + c0 * 128, off + (c0 + cn_) * 128), (identf, 0, 128)], w=[PSr(7)])
                emit_copy()

            def srow_dense(wv, c0, ncols, kcn, lcols, bank):
                nparts = (kcn + 7) // 8
                for kp in range(nparts):
                    k0 = kp * 8
                    kn = min(8, kcn - k0)
                    sv, sreg = wload(wv[:, k0:k0 + kn, c0:c0 + ncols], kn, ncols)
                    mm_group(PS(bank, 0, ncols, 0, 1), (lambda k, k0=k0: lcols.ap(k0 + k, k0 + k + 1)), (lambda k, sv=sv: sv[:, k, :]), kn,
                             r=[sreg, (lcols, 0, kcn)], w=[PSr(bank)], first=(kp == 0), last=(kp == nparts - 1))

            def rms_row(row, n, s0, inv_n, sjunk):
                S.act(lambda e: e.activation(out=sjunk.ap(0, n, 0, 1), in_=row.ap(0, n, 0, 1), func=AF.Square, accum_out=stat.ap(s0, s0 + 1, 0, 1)),
                      r=[(row, 0, n)], w=[(sjunk, 0, n), (stat, s0, s0 + 1)])
                S.act(lambda e: e.activation(out=stat.ap(s0 + 1, s0 + 2, 0, 1), in_=stat.ap(s0, s0 + 1, 0, 1), func=AF.Sqrt, bias=EPS, scale=inv_n),
                      r=[(stat, s0, s0 + 1)], w=[(stat, s0 + 1, s0 + 2)])
                S.dve(lambda e: e.reciprocal(out=stat.ap(s0 + 2, s0 + 3, 0, 1), in_=stat.ap(s0 + 1, s0 + 2, 0, 1)), r=[(stat, s0 + 1, s0 + 2)], w=[(stat, s0 + 2, s0 + 3)])

            def sample_norm_cols(src_dram, gtab, xsT):
                srow = Buf(t_arena, "arena", BF16, F32, 0, D)
                sjunk = Buf(t_arena, "arena", BF16, F32, D * 4, D)
                S.dma("sp", srow.ap(0, D, 0, 1), src_dram, r=[("dram_" + src_dram.tensor.name, 0, 1)], w=[(srow, 0, D)])
                rms_row(srow, D, 52, 1.0 / D, sjunk)
                S.dve(lambda e: e.tensor_scalar(out=srow.ap(0, D, 0, 1), in0=srow.ap(0, D, 0, 1), scalar1=stat.ap(54, 55, 0, 1), scalar2=None, op0=ALU.mult),
                      r=[(srow, 0, D), (stat, 54, 55)], w=[(srow, 0, D)])
                row_to_cols(srow, 0, KC, lambda: S.dve(lambda e: e.tensor_tensor(out=xsT.ap(), in0=PS(7, 0, KC), in1=gtab.ap(), op=ALU.mult),
                                                       r=[PSr(7), (gtab, 0, KC)], w=[(xsT, 0, KC)]))

            def sample_resid(wv, kcn, lcols, src_dram, dst_dram):
                sname, dname = "dram_" + src_dram.tensor.name, "dram_" + dst_dram.tensor.name
                for cb in range(D // 512):
                    bank = cb % 4
                    srow_dense(wv, cb * 512, 512, kcn, lcols, bank)
                    xb_ = xpc[xpc_i[0] % 2]
                    xpc_i[0] += 1
                    S.dma("sp", xb_.ap(0, 512, 0, 1), src_dram[:, cb * 512:(cb + 1) * 512], r=[(sname, 0, 1)], w=[(xb_, 0, 512)])
                    S.dve(lambda e, xb_=xb_, bank=bank: e.tensor_tensor(out=xb_.ap(0, 512, 0, 1), in0=PS(bank, 0, 512, 0, 1), in1=xb_.ap(0, 512, 0, 1), op=ALU.add),
                          r=[PSr(bank), (xb_, 0, 512)], w=[(xb_, 0, 512)])
                    S.dma("sp", dst_dram[:, cb * 512:(cb + 1) * 512], xb_.ap(0, 512, 0, 1), r=[(xb_, 0, 512)], w=[(dname, 0, 1)])

            def sample_layer(l, src_dram):
                global_names = None
                wv_in = wview(w_in[l])
                arena.reset()
                hrow = arena.alloc(F32, N_IN)
                rbase = arena.off
                colA.reset()
                xsT = colA.alloc(BF16, KC)
                oTs = colA.alloc(BF16, 36)
                mTs = colA.alloc(BF16, KC)
                hTs = colA.alloc(BF16, FKC)
                qcol = colA.alloc(F32, 16)
                acol = colA.alloc(F32, 8)
                zc = colA.alloc(F32, 1)
                pcol = colA.alloc(F32, 16)
                wa2f = colA.alloc(F32, 1024)
                sTt = colA.alloc(F32, 400)
                ocs = colA.alloc(F32, 512)
                sample_norm_cols(src_dram, gmix, xsT)
                for cb in range(0, N_IN, 512):
                    ncols = min(512, N_IN - cb)
                    bank = (cb // 512) % 4
                    srow_dense(wv_in, cb, ncols, KC, xsT, bank)
                    S.act(lambda e, cb=cb, ncols=ncols, bank=bank: e.activation(out=hrow.ap(cb, cb + ncols, 0, 1), in_=PS(bank, 0, ncols, 0, 1), func=AF.Copy),
                          r=[PSr(bank)], w=[(hrow, cb, cb + ncols)])
                arena.reset(rbase)
                arow = arena.alloc(F32, 1024)
                brow = arena.alloc(F32, 1024)
                t512 = arena.alloc(F32, 512)
                u512 = arena.alloc(F32, 512)
                sj = arena.alloc(F32, 512)
                S.dma("sp", wa2f.ap(0, 1024, 0, 16), w_alpha2[l], w=[(wa2f, 0, 1024)])
                S.dma("sp", brow.ap(0, 1024, 0, 1), b_alpha_r[l:l + 1, :], w=[(brow, 0, 1024)])
                S.dma("sp", SA.ap().rearrange("p (c v) -> p c v", v=512), sgla[l].rearrange("h (c p) v -> p (h c) v", p=128), w=[(SA, 0, 4096)])
                S.pe(lambda e: e.transpose(out=PS(7, 0, 1, 0, 16), in_=hrow.ap(O_ZA, O_ZA + 16, 0, 1), identity=identf.ap(0, 1, 0, 1)), r=[(hrow, O_ZA, O_ZA + 16), (identf, 0, 128)], w=[PSr(7)])
                S.dve(lambda e: e.tensor_copy(out=zc.ap(0, 1, 0, 16), in_=PS(7, 0, 1, 0, 16)), r=[PSr(7)], w=[(zc, 0, 1)])
                for hf in range(2):
                    mm_group(PS(hf, 0, 512, 0, 1), lambda k: zc.ap(0, 1, 0, 16), lambda k, hf=hf: wa2f.ap(hf * 512, (hf + 1) * 512, 0, 16), 1,
                             r=[(zc, 0, 1), (wa2f, hf * 512, (hf + 1) * 512)], w=[PSr(hf)])
                    sl = (hf * 512, (hf + 1) * 512)
                    S.dve(lambda e, hf=hf, sl=sl: e.tensor_tensor(out=arow.ap(sl[0], sl[1], 0, 1), in0=PS(hf, 0, 512, 0, 1), in1=brow.ap(sl[0], sl[1], 0, 1), op=ALU.add),
                          r=[PSr(hf), (brow, sl[0], sl[1])], w=[(arow, sl[0], sl[1])])
                S.act(lambda e: e.activation(out=arow.ap(0, 1024, 0, 1), in_=arow.ap(0, 1024, 0, 1), func=AF.Exp, scale=-1.0), r=[(arow, 0, 1024)], w=[(arow, 0, 1024)])
                S.act(lambda e: e.activation(out=arow.ap(0, 1024, 0, 1), in_=arow.ap(0, 1024, 0, 1), func=AF.Ln, bias=1.0), r=[(arow, 0, 1024)], w=[(arow, 0, 1024)])
                S.act(lambda e: e.activation(out=arow.ap(0, 1024, 0, 1), in_=arow.ap(0, 1024, 0, 1), func=AF.Exp, scale=-1.0 / 16), r=[(arow, 0, 1024)], w=[(arow, 0, 1024)])
                row_to_cols(arow, 0, 8, lambda: S.dve(lambda e: e.tensor_copy(out=acol.ap(), in_=PS(7, 0, 8)), r=[PSr(7)], w=[(acol, 0, 8)]))
                row_to_cols(hrow, O_QA, 8, lambda: S.dve(lambda e: e.tensor_scalar(out=qcol.ap(0, 8), in0=PS(7, 0, 8), scalar1=DK_A ** -0.5, scalar2=None, op0=ALU.mult),
                                                          r=[PSr(7)], w=[(qcol, 0, 8)]))
                for h in range(H_A):
                    for dkc in range(2):
                        c8 = h * 2 + dkc
                        sl = (c8 * 512, (c8 + 1) * 512)
                        ko = O_KA + c8 * 128
                        vo = O_VA + h * 512
                        mm_group(PS(dkc), lambda k, ko=ko: hrow.ap(ko, ko + 128, 0, 1), lambda k, vo=vo: hrow.ap(vo, vo + 512, 0, 1), 1,
                                 r=[(hrow, ko, ko + 128), (hrow, vo, vo + 512)], w=[PSr(dkc)])
                        S.dve(lambda e, sl=sl, c8=c8, dkc=dkc: e.scalar_tensor_tensor(out=SA.ap(*sl), in0=SA.ap(*sl), scalar=acol.ap(c8, c8 + 1), in1=PS(dkc), op0=ALU.mult, op1=ALU.add),
                              r=[(SA, sl[0], sl[1]), (acol, c8, c8 + 1), PSr(dkc)], w=[(SA, sl[0], sl[1])])
                    def emit_os(e, h=h):
                        e.matmul(PS(2, 0, 512, 0, 1), lhsT=qcol.ap(h * 2, h * 2 + 1), rhs=SA.ap(h * 1024, h * 1024 + 512), start=True, stop=False)
                        return e.matmul(PS(2, 0, 512, 0, 1), lhsT=qcol.ap(h * 2 + 1, h * 2 + 2), rhs=SA.ap(h * 1024 + 512, h * 1024 + 1024), start=False, stop=True)
                    S.pe(emit_os, r=[(qcol, 0, 8), (SA, h * 1024, (h + 1) * 1024)], w=[PSr(2)])
                    S.act(lambda e: e.activation(out=t512.ap(0, 512, 0, 1), in_=PS(2, 0, 512, 0, 1), func=AF.Copy), r=[PSr(2)], w=[(t512, 0, 512)])
                    rms_row(t512, 512, 56, 1.0 / DV_A, sj)
                    ro = O_RA + h * 512
                    S.act(lambda e, ro=ro: e.activation(out=u512.ap(0, 512, 0, 1), in_=hrow.ap(ro, ro + 512, 0, 1), func=AF.Silu), r=[(hrow, ro, ro + 512)], w=[(u512, 0, 512)])
                    S.dve(lambda e: e.tensor_tensor(out=u512.ap(0, 512, 0, 1), in0=u512.ap(0, 512, 0, 1), in1=ggla.ap(0, 512, 0, 1), op=ALU.mult), r=[(u512, 0, 512), (ggla, 0, 512)], w=[(u512, 0, 512)])
                    S.dve(lambda e: e.scalar_tensor_tensor(out=t512.ap(0, 512, 0, 1), in0=t512.ap(0, 512, 0, 1), scalar=stat.ap(58, 59, 0, 1), in1=u512.ap(0, 512, 0, 1), op0=ALU.mult, op1=ALU.mult),
                          r=[(t512, 0, 512), (stat, 58, 59), (u512, 0, 512)], w=[(t512, 0, 512)])
                    row_to_cols(t512, 0, 4, lambda h=h: S.dve(lambda e, h=h: e.tensor_copy(out=oTs.ap(h * 4, h * 4 + 4), in_=PS(7, 0, 4)), r=[PSr(7)], w=[(oTs, h * 4, h * 4 + 4)]))
                S.dma("sp", gla_s[l].rearrange("h (c p) v -> p (h c) v", p=128), SA.ap().rearrange("p (c v) -> p c v", v=512), r=[(SA, 0, 4096)], w=[("dram_gla_s", l, l + 1)])
                arena.reset(rbase)
                qk = arena.alloc(F32, 2048)
                tq = arena.alloc(F32, 1024)
                t256 = arena.alloc(F32, 256)
                u256 = arena.alloc(F32, 256)
                sj = arena.alloc(F32, 512)
                csr = arena.alloc(F32, 128)
                S.dma("sp", csr.ap(0, 128, 0, 1), cn["csrow"], w=[(csr, 0, 128)])
                S.dma("sp", SB_.ap().rearrange("p (h v) -> p h v", v=256), sret[l].rearrange("h p v -> p h v"), w=[(SB_, 0, 2048)])
                cosr = csr.ap(0, 64, 0, 1).unsqueeze(1).to_broadcast([1, 8, 64])
                sinr = csr.ap(64, 128, 0, 1).unsqueeze(1).to_broadcast([1, 8, 64])
                for wi, (ho, scl) in enumerate(((O_QB, 1.0), (O_KB, DK_B ** -0.5))):
                    x3 = hrow.ap(ho, ho + 1024, 0, 1).rearrange("p (h two n) -> p h two n", two=2, n=64)
                    o3 = qk.ap(wi * 1024, (wi + 1) * 1024, 0, 1).rearrange("p (h two n) -> p h two n", two=2, n=64)
                    t3 = tq.ap(0, 1024, 0, 1).rearrange("p (h two n) -> p h two n", two=2, n=64)
                    rr_ = [(hrow, ho, ho + 1024), (csr, 0, 128)]
                    S.dve(lambda e, x3=x3, o3=o3: e.tensor_tensor(out=o3[:, :, 0, :], in0=x3[:, :, 0, :], in1=cosr, op=ALU.mult), r=rr_, w=[(qk, wi * 1024, (wi + 1) * 1024)])
                    S.dve(lambda e, x3=x3, t3=t3: e.tensor_tensor(out=t3[:, :, 0, :], in0=x3[:, :, 1, :], in1=sinr, op=ALU.mult), r=rr_, w=[(tq, 0, 1024)])
                    S.dve(lambda e, o3=o3, t3=t3: e.tensor_tensor(out=o3[:, :, 0, :], in0=o3[:, :, 0, :], in1=t3[:, :, 0, :], op=ALU.subtract),
                          r=[(qk, wi * 1024, (wi + 1) * 1024), (tq, 0, 1024)], w=[(qk, wi * 1024, (wi + 1) * 1024)])
                    S.dve(lambda e, x3=x3, o3=o3: e.tensor_tensor(out=o3[:, :, 1, :], in0=x3[:, :, 0, :], in1=sinr, op=ALU.mult), r=rr_, w=[(qk, wi * 1024, (wi + 1) * 1024)])
                    S.dve(lambda e, x3=x3, t3=t3: e.tensor_tensor(out=t3[:, :, 1, :], in0=x3[:, :, 1, :], in1=cosr, op=ALU.mult), r=rr_, w=[(tq, 0, 1024)])
                    S.dve(lambda e, o3=o3, t3=t3: e.tensor_tensor(out=o3[:, :, 1, :], in0=o3[:, :, 1, :], in1=t3[:, :, 1, :], op=ALU.add),
                          r=[(qk, wi * 1024, (wi + 1) * 1024), (tq, 0, 1024)], w=[(qk, wi * 1024, (wi + 1) * 1024)])
                    if scl != 1.0:
                        S.dve(lambda e, wi=wi, scl=scl: e.tensor_scalar(out=qk.ap(wi * 1024, (wi + 1) * 1024, 0, 1), in0=qk.ap(wi * 1024, (wi + 1) * 1024, 0, 1), scalar1=scl, scalar2=None, op0=ALU.mult),
                              r=[(qk, wi * 1024, (wi + 1) * 1024)], w=[(qk, wi * 1024, (wi + 1) * 1024)])
                row_to_cols(qk, 0, 8, lambda: S.dve(lambda e: e.tensor_copy(out=qcol.ap(8, 16), in_=PS(7, 0, 8)), r=[PSr(7)], w=[(qcol, 8, 16)]))
                for h in range(H_B):
                    ssl = (h * 256, (h + 1) * 256)
                    ko = 1024 + h * 128
                    vo = O_VB + h * 256
                    mm_group(PS(0, 0, 256), lambda k, ko=ko: qk.ap(ko, ko + 128, 0, 1), lambda k, vo=vo: hrow.ap(vo, vo + 256, 0, 1), 1,
                             r=[(qk, ko, ko + 128), (hrow, vo, vo + 256)], w=[PSr(0)])
                    S.dve(lambda e, ssl=ssl, h=h: e.scalar_tensor_tensor(out=SB_.ap(*ssl), in0=SB_.ap(*ssl), scalar=gam.ap(8 + h, 9 + h), in1=PS(0, 0, 256), op0=ALU.mult, op1=ALU.add),
                          r=[(SB_, ssl[0], ssl[1]), (gam, 8 + h, 9 + h), PSr(0)], w=[(SB_, ssl[0], ssl[1])])
                    mm_group(PS(2, 0, 256, 0, 1), lambda k, h=h: qcol.ap(8 + h, 9 + h), lambda k, ssl=ssl: SB_.ap(*ssl), 1, r=[(qcol, 8, 16), (SB_, ssl[0], ssl[1])], w=[PSr(2)])
                    S.act(lambda e: e.activation(out=t256.ap(0, 256, 0, 1), in_=PS(2, 0, 256, 0, 1), func=AF.Copy), r=[PSr(2)], w=[(t256, 0, 256)])
                    rms_row(t256, 256, 56, 1.0 / DV_B, sj)
                    go = O_GB + h * 256
                    S.act(lambda e, go=go: e.activation(out=u256.ap(0, 256, 0, 1), in_=hrow.ap(go, go + 256, 0, 1), func=AF.Silu), r=[(hrow, go, go + 256)], w=[(u256, 0, 256)])
                    S.dve(lambda e: e.scalar_tensor_tensor(out=t256.ap(0, 256, 0, 1), in0=t256.ap(0, 256, 0, 1), scalar=stat.ap(58, 59, 0, 1), in1=u256.ap(0, 256, 0, 1), op0=ALU.mult, op1=ALU.mult),
                          r=[(t256, 0, 256), (stat, 58, 59), (u256, 0, 256)], w=[(t256, 0, 256)])
                    row_to_cols(t256, 0, 2, lambda h=h: S.dve(lambda e, h=h: e.tensor_copy(out=oTs.ap(16 + h * 2, 18 + h * 2), in_=PS(7, 0, 2)), r=[PSr(7)], w=[(oTs, 16 + h * 2, 18 + h * 2)]))
                S.dma("sp", ret_s[l].rearrange("h p v -> p h v"), SB_.ap().rearrange("p (h v) -> p h v", v=256), r=[(SB_, 0, 2048)], w=[("dram_ret_s", l, l + 1)])
                arena.reset(rbase)
                Kc = arena.alloc(F32, 512)
                Vc = [arena.alloc(F32, 512) for _ in range(3)]
                prod = arena.alloc(F32, 512)
                sc12 = arena.alloc(F32, 16)
                s0row = arena.alloc(F32, 16)
                sbt = arena.alloc(F32, 12)
                S.dma("sp", sbt.ap(), cn["sbias"], w=[(sbt, 0, 12)])
                for g in range(3):
                    W, d = WINDOWS[g], DILS[g]
                    qo, ko, vo = O_HC + g * 1536, O_HC + g * 1536 + 512, O_HC + g * 1536 + 1024
                    S.dma("sp", Kc.ap(), cw[g][l, 0].rearrange("(j d) c -> j d c", d=d)[:, 0, :], w=[(Kc, 0, 512)])
                    S.dma("sp", Vc[g].ap(), cw[g][l, 1].rearrange("(j d) c -> j d c", d=d)[:, 0, :], w=[(Vc[g], 0, 512)])
                    for which, oo in ((0, ko), (1, vo)):
                        S.dma("sp", ws[g][l, which, 0:W - 1, :], cw[g][l, which, 1:W, :], w=[("dram_ws%d" % g, (l * 2 + which) * 4096, (l * 2 + which) * 4096 + W - 1)])
                        S.dma("sp", ws[g][l, which, W - 1:W, :], hrow.ap(oo, oo + 512, 0, 1), r=[(hrow, oo, oo + 512)], w=[("dram_ws%d" % g, (l * 2 + which) * 4096 + W - 1, (l * 2 + which) * 4096 + W)])
                    mm_group(PS(0), lambda k: onesf.ap(0, 128, 0, 1), lambda k, qo=qo: hrow.ap(qo, qo + 512, 0, 1), 1, r=[(onesf, 0, 128), (hrow, qo, qo + 512)], w=[PSr(0)])
                    S.dve(lambda e: e.tensor_tensor(out=prod.ap(), in0=Kc.ap(), in1=PS(0), op=ALU.mult), r=[(Kc, 0, 512), PSr(0)], w=[(prod, 0, 512)])
                    S.dve(lambda e, g=g: e.tensor_reduce(out=sc12.ap(g * 4, g * 4 + 4), in_=prod.ap().rearrange("p (i d) -> p i d", d=128), axis=AX.X, op=ALU.add),
                          r=[(prod, 0, 512)], w=[(sc12, g * 4, g * 4 + 4)])
                    S.dve(lambda e, g=g: e.scalar_tensor_tensor(out=sc12.ap(g * 4, g * 4 + 4), in0=sc12.ap(g * 4, g * 4 + 4), scalar=128 ** -0.5, in1=sbt.ap(g * 4, g * 4 + 4), op0=ALU.mult, op1=ALU.add),
                          r=[(sc12, g * 4, g * 4 + 4), (sbt, g * 4, g * 4 + 4)], w=[(sc12, g * 4, g * 4 + 4)])
                    S.dve(lambda e, qo=qo, ko=ko: e.tensor_tensor(out=prod.ap(0, 512, 0, 1), in0=hrow.ap(qo, qo + 512, 0, 1), in1=hrow.ap(ko, ko + 512, 0, 1), op=ALU.mult),
                          r=[(hrow, qo, qo + 512), (hrow, ko, ko + 512), (prod, 0, 512)], w=[(prod, 0, 512)])
                    S.dve(lambda e, g=g: e.tensor_reduce(out=s0row.ap(g * 4, g * 4 + 4, 0, 1), in_=prod.ap(0, 512, 0, 1).rearrange("p (i d) -> p i d", d=128), axis=AX.X, op=ALU.add),
                          r=[(prod, 0, 512)], w=[(s0row, g * 4, g * 4 + 4)])
                    S.pe(lambda e, g=g: e.transpose(out=PS(3, g * 128, (g + 1) * 128, 0, 4), in_=sc12.ap(g * 4, g * 4 + 4), identity=identf.ap()),
                         r=[(sc12, g * 4, g * 4 + 4), (identf, 0, 128)], w=[PSr(3)])
                    S.pe(lambda e, g=g: e.transpose(out=PS(3, 384 + g, 385 + g, 0, 4), in_=s0row.ap(g * 4, g * 4 + 4, 0, 1), identity=identf.ap(0, 1, 0, 1)),
                         r=[(s0row, g * 4, g * 4 + 4), (identf, 0, 128)], w=[PSr(3)])
                S.dve(lambda e: e.tensor_copy(out=sTt.ap(0, 387, 0, 4), in_=PS(3, 0, 387, 0, 4)), r=[PSr(3)], w=[(sTt, 0, 387)])
                S.dve(lambda e: e.tensor_scalar(out=sTt.ap(384, 387, 0, 4), in0=sTt.ap(384, 387, 0, 4), scalar1=128 ** -0.5, scalar2=None, op0=ALU.mult), r=[(sTt, 384, 387)], w=[(sTt, 384, 387)])
                S.dve(lambda e: e.tensor_reduce(out=stat.ap(60, 61, 0, 4), in_=sTt.ap(0, 387, 0, 4), axis=AX.X, op=ALU.max, negate=True), r=[(sTt, 0, 387)], w=[(stat, 60, 61)])
                S.act(lambda e: e.activation(out=sTt.ap(0, 387, 0, 4), in_=sTt.ap(0, 387, 0, 4), func=AF.Exp, bias=stat.ap(60, 61, 0, 4), scale=1.0, accum_out=stat.ap(61, 62, 0, 4)),
                      r=[(sTt, 0, 387), (stat, 60, 61)], w=[(sTt, 0, 387), (stat, 61, 62)])
                S.dve(lambda e: e.reciprocal(out=stat.ap(62, 63, 0, 4), in_=stat.ap(61, 62, 0, 4)), r=[(stat, 61, 62)], w=[(stat, 62, 63)])
                for g in range(3):
                    S.pe(lambda e, g=g: e.transpose(out=PS(4, g * 4, g * 4 + 4), in_=sTt.ap(g * 128, (g + 1) * 128, 0, 4), identity=identf.ap(0, 4, 0, 4)),
                         r=[(sTt, g * 128, (g + 1) * 128), (identf, 0, 128)], w=[PSr(4)])
                    S.pe(lambda e, g=g: e.transpose(out=PS(4, 16 + g * 4, 20 + g * 4, 0, 1), in_=sTt.ap(384 + g, 385 + g, 0, 4), identity=identf.ap(0, 4, 0, 4)),
                         r=[(sTt, 384 + g, 385 + g), (identf, 0, 128)], w=[PSr(4)])
                S.dve(lambda e: e.tensor_copy(out=pcol.ap(0, 12), in_=PS(4, 0, 12)), r=[PSr(4)], w=[(pcol, 0, 12)])
                S.dve(lambda e: e.tensor_copy(out=s0row.ap(0, 12, 0, 1), in_=PS(4, 16, 28, 0, 1)), r=[PSr(4)], w=[(s0row, 0, 12)])
                def emit_pvs(e):
                    ins = None
                    for g in range(3):
                        vo = O_HC + g * 1536 + 1024
                        e.matmul(PS(5, 0, 512, 0, 4), lhsT=pcol.ap(g * 4, g * 4 + 4), rhs=Vc[g].ap(), start=(g == 0), stop=False)
                        ins = e.matmul(PS(5, 0, 512, 0, 4), lhsT=s0row.ap(g * 4, g * 4 + 4, 0, 1), rhs=hrow.ap(vo, vo + 512, 0, 1), start=False, stop=(g == 2))
                    return ins
                S.pe(emit_pvs, r=[(pcol, 0, 12), (s0row, 0, 12), (Vc[0], 0, 512), (Vc[1], 0, 512), (Vc[2], 0, 512), (hrow, O_HC, N_IN)], w=[PSr(5)])
                S.act(lambda e: e.activation(out=ocs.ap(0, 512, 0, 4), in_=PS(5, 0, 512, 0, 4), func=AF.Copy, scale=stat.ap(62, 63, 0, 4)), r=[PSr(5), (stat, 62, 63)], w=[(ocs, 0, 512)])
                for i in range(4):
                    S.pe(lambda e, i=i: e.transpose(out=PS(6, i * 4, i * 4 + 4), in_=ocs.ap(i * 128, (i + 1) * 128, 0, 4), identity=identf.ap(0, 4, 0, 4)),
                         r=[(ocs, i * 128, (i + 1) * 128), (identf, 0, 128)], w=[PSr(6)])
                    S.dve(lambda e, i=i: e.tensor_copy(out=oTs.ap(32 + i, 33 + i), in_=PS(6, i * 4 + i, i * 4 + i + 1)), r=[PSr(6)], w=[(oTs, 32 + i, 33 + i)])
                arena.reset()
                mrow = arena.alloc(F32, D)
                sgr = [arena.alloc(F32, 512) for _ in range(3)]
                wv_mg = wview(w_merge[l])
                wv_ups = [(wview(w_up_a[l]), 16, 0), (wview(w_up_b[l]), 16, 16), (wview(w_up_c[l]), 4, 32)]
                for cb in range(D // 512):
                    for b in range(3):
                        srow_dense(wv_mg, b * D + cb * 512, 512, KC, xsT, b)
                        S.act(lambda e, b=b: e.activation(out=sgr[b].ap(0, 512, 0, 1), in_=PS(b, 0, 512, 0, 1), func=AF.Sigmoid), r=[PSr(b)], w=[(sgr[b], 0, 512)])
                    for b, (wvu, kn, k0) in enumerate(wv_ups):
                        ocols = Buf(oTs.t, oTs.name, oTs.tdt, BF16, oTs.base + k0 * 2, kn)
                        srow_dense(wvu, cb * 512, 512, kn, ocols, 3 + b)
                        S.dve(lambda e, b=b: e.tensor_tensor(out=sgr[b].ap(0, 512, 0, 1), in0=PS(3 + b, 0, 512, 0, 1), in1=sgr[b].ap(0, 512, 0, 1), op=ALU.mult),
                              r=[PSr(3 + b), (sgr[b], 0, 512)], w=[(sgr[b], 0, 512)])
                    S.dve(lambda e: e.tensor_tensor(out=sgr[0].ap(0, 512, 0, 1), in0=sgr[0].ap(0, 512, 0, 1), in1=sgr[1].ap(0, 512, 0, 1), op=ALU.add),
                          r=[(sgr[0], 0, 512), (sgr[1], 0, 512)], w=[(sgr[0], 0, 512)])
                    S.dve(lambda e, cb=cb: e.tensor_tensor(out=mrow.ap(cb * 512, (cb + 1) * 512, 0, 1), in0=sgr[0].ap(0, 512, 0, 1), in1=sgr[2].ap(0, 512, 0, 1), op=ALU.add),
                          r=[(sgr[0], 0, 512), (sgr[2], 0, 512)], w=[(mrow, cb * 512, (cb + 1) * 512)])
                row_to_cols(mrow, 0, KC, lambda: S.dve(lambda e: e.tensor_copy(out=mTs.ap(), in_=PS(7, 0, KC)), r=[PSr(7)], w=[(mTs, 0, KC)]))
                sample_resid(wview(w_out[l]), KC, mTs, src_dram, xsa)
                sample_norm_cols(xsa, gffn, xsT)
                arena.reset()
                frow = arena.alloc(F32, D_FF)
                grow_ = arena.alloc(F32, 512)
                wv_g, wv_u = wview(w_fg[l]), wview(w_fu[l])
                for cb in range(0, D_FF, 512):
                    ncols = min(512, D_FF - cb)
                    srow_dense(wv_g, cb, ncols, KC, xsT, 0)
                    srow_dense(wv_u, cb, ncols, KC, xsT, 1)
                    S.act(lambda e, ncols=ncols: e.activation(out=grow_.ap(0, ncols, 0, 1), in_=PS(0, 0, ncols, 0, 1), func=AF.Silu), r=[PSr(0)], w=[(grow_, 0, ncols)])
                    S.dve(lambda e, cb=cb, ncols=ncols: e.tensor_tensor(out=frow.ap(cb, cb + ncols, 0, 1), in0=PS(1, 0, ncols, 0, 1), in1=grow_.ap(0, ncols, 0, 1), op=ALU.mult),
                          r=[PSr(1), (grow_, 0, ncols)], w=[(frow, cb, cb + ncols)])
                row_to_cols(frow, 0, FKC, lambda: S.dve(lambda e: e.tensor_copy(out=hTs.ap(), in_=PS(7, 0, FKC)), r=[PSr(7)], w=[(hTs, 0, FKC)]))
                sample_resid(wview(w_fd[l]), FKC, hTs, xsa, xsb)

            for l in range(n_layers):
                x_src = xp if l == 0 else xb
                wv_in = wview(w_in[l])
                cload(gmix, g_mix[l])
                cload(gffn, g_ffn[l])
                cload(ggla, g_gla[l].partition_broadcast(128))
                cload(negb, b_alpha[l])
                S.dve(lambda e: e.tensor_scalar(out=negb.ap(), in0=negb.ap(), scalar1=-1.0, scalar2=None, op0=ALU.mult), r=[(negb, 0, 8)], w=[(negb, 0, 8)])
                S.dma("pool", wa2.ap(0, 1024, 0, 16), w_alpha2[l], w=[(wa2, 0, 1024)], slot=0)
                ck('params')
                wmode[0] = "stream"
                if not skip_sample:
                    sample_layer(l, xs if l == 0 else xsb)
                ck('sample')
                S.dve(lambda e: e.memset(SA.ap(), 0.0), w=[(SA, 0, 4096)])
                S.dve(lambda e: e.memset(SB_.ap(), 0.0), w=[(SB_, 0, 2048)])

                for ti in range(n_tiles):
                    t0 = ti * NT
                    wmode[0] = "fill" if (ti == 0 and n_tiles > 1) else ("cached" if ti > 0 else "stream")
                    wseq[0] = 0
                    arena.reset()
                    rowbuf = arena.alloc(F32, D)
                    xnjunk = arena.alloc(BF16, D)
                    norm_tile(x_src, t0, gmix, rowbuf)
                    ck('norm')
                    cload(cosb, cn["cost"][:, t0:t0 + NT])
                    cload(sinb, cn["sint"][:, t0:t0 + NT])

                    arena.reset()
                    oT = arena.alloc(BF16, 36 * NT)
                    mT = arena.alloc(BF16, 32 * NT)
                    tmp_base = arena.off - 32 * NT * 2
                    arena.reset(tmp_base)
                    zT = arena.alloc(BF16, 512)
                    qt = arena.alloc(BF16, 1024)
                    kt = arena.alloc(BF16, 1024)
                    Ep = arena.alloc(F32, 1024)
                    Em = arena.alloc(F32, 512)
                    tmp1 = arena.alloc(F32, 512)
                    tmp2 = arena.alloc(F32, 512)
                    cum = arena.alloc(F32, 512)
                    junkf = arena.alloc(F32, 512)
                    vtok = arena.alloc(BF16, 2048)
                    gs = arena.alloc(BF16, 2048)
                    Pm = arena.alloc(BF16, 128)
                    ktok = arena.alloc(BF16, 256)
                    otok = arena.alloc(BF16, 512)
                    Sbf = arena.alloc(BF16, 1024)
                    fm_dense(wv_in, O_ZA, 16, KC, xn_fn, xn_reg, 0)
                    S.act(lambda e: e.activation(out=zT.ap(0, 512, 0, 16), in_=PS(0, 0, 512, 0, 16), func=AF.Copy), r=[PSr(0)], w=[(zT, 0, 512)])
                    for h in range(H_A):
                        for dkc in range(2):
                            c8 = h * 2 + dkc
                            seg = (dkc * 512, (dkc + 1) * 512)
                            mm_group(PS(1), lambda k, c8=c8: wa2.ap(c8 * 128, (c8 + 1) * 128, 0, 16), lambda k: zT.ap(0, 512, 0, 16), 1,
                                     r=[(wa2, c8 * 128, (c8 + 1) * 128), (zT, 0, 512)], w=[PSr(1)])
                            S.act(lambda e, c8=c8: e.activation(out=tmp1.ap(), in_=PS(1), func=AF.Exp, bias=negb.ap(c8, c8 + 1), scale=-1.0),
                                  r=[PSr(1), (negb, c8, c8 + 1)], w=[(tmp1, 0, 512)])
                            S.act(lambda e: e.activation(out=tmp2.ap(), in_=tmp1.ap(), func=AF.Ln, bias=1.0, scale=1.0), r=[(tmp1, 0, 512)], w=[(tmp2, 0, 512)])
                            for tc in range(4):
                                S.dve(lambda e, tc=tc: e.tensor_tensor_scan(out=cum.ap(tc * 128, (tc + 1) * 128), data0=onesf.ap(), data1=tmp2.ap(tc * 128, (tc + 1) * 128),
                                                                            initial=0.0, op0=ALU.mult, op1=ALU.add),
                                      r=[(tmp2, tc * 128, (tc + 1) * 128), (onesf, 0, 128)], w=[(cum, tc * 128, (tc + 1) * 128)])
                            S.act(lambda e, seg=seg: e.activation(out=Ep.ap(*seg), in_=cum.ap(), func=AF.Exp, scale=-1.0 / 16), r=[(cum, 0, 512)], w=[(Ep, seg[0], seg[1])])
                            S.act(lambda e: e.activation(out=Em.ap(), in_=cum.ap(), func=AF.Exp, scale=1.0 / 16), r=[(cum, 0, 512)], w=[(Em, 0, 512)])
                            fm_dense(wv_in, O_QA + c8 * 128, 128, KC, xn_fn, xn_reg, 2)
                            S.dve(lambda e, seg=seg: e.scalar_tensor_tensor(out=qt.ap(*seg), in0=PS(2), scalar=DK_A ** -0.5, in1=Ep.ap(*seg), op0=ALU.mult, op1=ALU.mult),
                                  r=[PSr(2), (Ep, seg[0], seg[1])], w=[(qt, seg[0], seg[1])])
                            fm_dense(wv_in, O_KA + c8 * 128, 128, KC, xn_fn, xn_reg, 3)
                            S.dve(lambda e, seg=seg: e.tensor_tensor(out=kt.ap(*seg), in0=PS(3), in1=Em.ap(), op=ALU.mult),
                                  r=[PSr(3), (Em, 0, 512)], w=[(kt, seg[0], seg[1])])
                        tm_dense(wv_in, O_VA + h * 512, KC, xn_lhs, xn_reg, [0, 1, 2, 3])
                        for tc in range(4):
                            S.act(lambda e, tc=tc: e.activation(out=vtok.ap(tc * 512, (tc + 1) * 512), in_=PS(tc), func=AF.Copy), r=[PSr(tc)], w=[(vtok, tc * 512, (tc + 1) * 512)])
                        tm_dense(wv_in, O_RA + h * 512, KC, xn_lhs, xn_reg, [0, 1, 2, 3])
                        for tc in range(4):
                            S.act(lambda e, tc=tc: e.activation(out=junkf.ap(), in_=PS(tc), func=AF.Silu), r=[PSr(tc)], w=[(junkf, 0, 512)])
                            S.dve(lambda e, tc=tc: e.tensor_tensor(out=gs.ap(tc * 512, (tc + 1) * 512), in0=junkf.ap(), in1=ggla.ap(), op=ALU.mult),
                                  r=[(junkf, 0, 512), (ggla, 0, 512)], w=[(gs, tc * 512, (tc + 1) * 512)])
                        S.act(lambda e, h=h: e.activation(out=Sbf.ap(), in_=SA.ap(h * 1024, (h + 1) * 1024), func=AF.Copy), r=[(SA, h * 1024, (h + 1) * 1024)], w=[(Sbf, 0, 1024)])
                        for tc in range(4):
                            def ts(dkc, tc=tc):
                                return (dkc * 512 + tc * 128, dkc * 512 + (tc + 1) * 128)
                            mm_group(PS(4, 0, 128), lambda k, tc=tc: kt.ap(*ts(k, tc)), lambda k, tc=tc: qt.ap(*ts(k, tc)), 2,
                                     r=[(kt, 0, 1024), (qt, 0, 1024)], w=[PSr(4, 0, 128)])
                            S.dve(lambda e: e.tensor_tensor(out=Pm.ap(), in0=PS(4, 0, 128), in1=mask01.ap(), op=ALU.mult), r=[PSr(4, 0, 128), (mask01, 0, 128)], w=[(Pm, 0, 128)])
                            def emit_kt(e, tc=tc):
                                ins = None
                                for dkc in range(2):
                                    ins = e.transpose(out=PS(7, 0, 256).bitcast(BF16)[:, dkc * 128:(dkc + 1) * 128], in_=kt.ap(*ts(dkc, tc)), identity=identb.ap())
                                return ins
                            S.pe(emit_kt, r=[(kt, 0, 1024), (identb, 0, 128)], w=[PSr(7, 0, 256)])
                            S.act(lambda e: e.activation(out=ktok.ap(), in_=PS(7, 0, 256).bitcast(BF16)[:, 0:256], func=AF.Copy), r=[PSr(7, 0, 256)], w=[(ktok, 0, 256)])
                            def emit_o(e, tc=tc):
                                e.matmul(PS(5), lhsT=Pm.ap(), rhs=vtok.ap(tc * 512, (tc + 1) * 512), start=True, stop=False)
                                e.matmul(PS(5), lhsT=qt.ap(*ts(0, tc)), rhs=Sbf.ap(0, 512), start=False, stop=False)
                                return e.matmul(PS(5), lhsT=qt.ap(*ts(1, tc)), rhs=Sbf.ap(512, 1024), start=False, stop=True)
                            S.pe(emit_o, r=[(Pm, 0, 128), (vtok, tc * 512, (tc + 1) * 512), (qt, 0, 1024), (Sbf, 0, 1024)], w=[PSr(5)])
                            st0 = 16 + tc * 4
                            S.act(lambda e, st0=st0: e.activation(out=junkf.ap(), in_=PS(5), func=AF.Square, accum_out=stat.ap(st0, st0 + 1)),
                                  r=[PSr(5)], w=[(junkf, 0, 512), (stat, st0, st0 + 1)])
                            S.act(lambda e, st0=st0: e.activation(out=stat.ap(st0 + 1, st0 + 2), in_=stat.ap(st0, st0 + 1), func=AF.Sqrt, bias=EPS, scale=1.0 / DV_A),
                                  r=[(stat, st0, st0 + 1)], w=[(stat, st0 + 1, st0 + 2)])
                            S.dve(lambda e, st0=st0: e.reciprocal(out=stat.ap(st0 + 2, st0 + 3), in_=stat.ap(st0 + 1, st0 + 2)), r=[(stat, st0 + 1, st0 + 2)], w=[(stat, st0 + 2, st0 + 3)])
                            S.dve(lambda e, st0=st0, tc=tc: e.scalar_tensor_tensor(out=otok.ap(), in0=PS(5), scalar=stat.ap(st0 + 2, st0 + 3), in1=gs.ap(tc * 512, (tc + 1) * 512),
                                                                                  op0=ALU.mult, op1=ALU.mult),
                                  r=[PSr(5), (stat, st0 + 2, st0 + 3), (gs, tc * 512, (tc + 1) * 512)], w=[(otok, 0, 512)])
                            def emit_ot(e):
                                ins = None
                                for j in range(4):
                                    ins = e.transpose(out=PS(7, 256, 512).bitcast(BF16)[:, j * 128:(j + 1) * 128], in_=otok.ap(j * 128, (j + 1) * 128), identity=identb.ap())
                                return ins
                            S.pe(emit_ot, r=[(otok, 0, 512), (identb, 0, 128)], w=[PSr(7, 256, 512)])
                            oc0 = h * 4
                            S.act(lambda e, oc0=oc0, tc=tc: e.activation(out=oT.ap(oc0 * NT, (oc0 + 4) * NT).rearrange("p (j t) -> p j t", t=NT)[:, :, tc * 128:(tc + 1) * 128],
                                                                        in_=PS(7, 256, 512).bitcast(BF16).rearrange("p (j t) -> p j t", t=128), func=AF.Copy),
                                  r=[PSr(7, 256, 512)], w=[(oT, oc0 * NT, (oc0 + 4) * NT)])
                            for dkc in range(2):
                                c8 = h * 2 + dkc
                                ecol = dkc * 512 + tc * 128 + 127
                                mm_group(PS(6), lambda k, dkc=dkc: ktok.ap(dkc * 128, (dkc + 1) * 128), lambda k, tc=tc: vtok.ap(tc * 512, (tc + 1) * 512), 1,
                                         r=[(ktok, 0, 256), (vtok, tc * 512, (tc + 1) * 512)], w=[PSr(6)])
                                sl = (c8 * 512, (c8 + 1) * 512)
                                S.dve(lambda e, sl=sl, ecol=ecol: e.tensor_scalar(out=SA.ap(*sl), in0=SA.ap(*sl), scalar1=Ep.ap(ecol, ecol + 1), scalar2=None, op0=ALU.mult),
                                      r=[(SA, sl[0], sl[1]), (Ep, ecol, ecol + 1)], w=[(SA, sl[0], sl[1])])
                                S.dve(lambda e, sl=sl, ecol=ecol: e.scalar_tensor_tensor(out=SA.ap(*sl), in0=PS(6), scalar=Ep.ap(ecol, ecol + 1), in1=SA.ap(*sl), op0=ALU.mult, op1=ALU.add),
                                      r=[PSr(6), (SA, sl[0], sl[1]), (Ep, ecol, ecol + 1)], w=[(SA, sl[0], sl[1])])
                                S.act(lambda e, sl=sl, dkc=dkc: e.activation(out=Sbf.ap(dkc * 512, (dkc + 1) * 512), in_=SA.ap(*sl), func=AF.Copy),
                                      r=[(SA, sl[0], sl[1])], w=[(Sbf, dkc * 512, (dkc + 1) * 512)])
                    if ti == n_tiles - 1:
                        S.dma("sp", gla_p[l].rearrange("h (c p) v -> p (h c) v", p=128), SA.ap().rearrange("p (c v) -> p c v", v=512), r=[(SA, 0, 4096)], w=[("dram_gla_p", l, l + 1)])


                    ck('mixA')
                    arena.reset(tmp_base)
                    qf = arena.alloc(F32, 512)
                    t1 = arena.alloc(F32, 512)
                    t2 = arena.alloc(F32, 512)
                    junkfB = arena.alloc(F32, 512)
                    qtb = arena.alloc(BF16, 1024)
                    ktb = arena.alloc(BF16, 1024)
                    vtokb = arena.alloc(BF16, 2048)
                    gsb = arena.alloc(BF16, 2048)
                    PmB = arena.alloc(BF16, 128)
                    ktokb = arena.alloc(BF16, 128)
                    otokb = arena.alloc(BF16, 256)
                    Sbfb = arena.alloc(BF16, 512)
                    for hp in range(4):
                        for hh in range(2):
                            h = hp * 2 + hh
                            for (coff, dst, etab) in ((O_QB, qtb, rEp), (O_KB, ktb, rEm)):
                                fm_dense(wv_in, coff + h * 128, 128, KC, xn_fn, xn_reg, 2)
                                S.act(lambda e: e.activation(out=qf.ap(), in_=PS(2), func=AF.Copy), r=[PSr(2)], w=[(qf, 0, 512)])
                                mm_group(PS(3), lambda k: permf.ap(), lambda k: qf.ap(), 1, r=[(permf, 0, 128), (qf, 0, 512)], w=[PSr(3)])
                                S.dve(lambda e: e.tensor_tensor(out=t1.ap(), in0=qf.ap(), in1=cosb.ap(), op=ALU.mult), r=[(qf, 0, 512), (cosb, 0, 512)], w=[(t1, 0, 512)])
                                S.dve(lambda e: e.tensor_tensor(out=t2.ap(), in0=PS(3), in1=sinb.ap(), op=ALU.mult), r=[PSr(3), (sinb, 0, 512)], w=[(t2, 0, 512)])
                                S.dve(lambda e: e.tensor_tensor(out=t1.ap(), in0=t1.ap(), in1=t2.ap(), op=ALU.add), r=[(t1, 0, 512), (t2, 0, 512)], w=[(t1, 0, 512)])
                                S.dve(lambda e, dst=dst, etab=etab, h=h, hh=hh: e.tensor_tensor(
                                    out=dst.ap(hh * 512, (hh + 1) * 512).rearrange("p (c t) -> p c t", t=128),
                                    in0=t1.ap().rearrange("p (c t) -> p c t", t=128),
                                    in1=etab.ap(h * 128, (h + 1) * 128).unsqueeze(1).to_broadcast([128, 4, 128]), op=ALU.mult),
                                    r=[(t1, 0, 512), (etab, h * 128, (h + 1) * 128)], w=[(dst, hh * 512, (hh + 1) * 512)])
                        tm_dense(wv_in, O_VB + hp * 512, KC, xn_lhs, xn_reg, [0, 1, 2, 3])
                        for tc in range(4):
                            S.act(lambda e, tc=tc: e.activation(out=vtokb.ap(tc * 512, (tc + 1) * 512), in_=PS(tc), func=AF.Copy), r=[PSr(tc)], w=[(vtokb, tc * 512, (tc + 1) * 512)])
                        tm_dense(wv_in, O_GB + hp * 512, KC, xn_lhs, xn_reg, [0, 1, 2, 3])
                        for tc in range(4):
                            S.act(lambda e, tc=tc: e.activation(out=gsb.ap(tc * 512, (tc + 1) * 512), in_=PS(tc), func=AF.Silu), r=[PSr(tc)], w=[(gsb, tc * 512, (tc + 1) * 512)])
                        for hh in range(2):
                            h = hp * 2 + hh
                            ssl = (h * 256, (h + 1) * 256)
                            S.act(lambda e, ssl=ssl, hh=hh: e.activation(out=Sbfb.ap(hh * 256, (hh + 1) * 256), in_=SB_.ap(*ssl), func=AF.Copy),
                                  r=[(SB_, ssl[0], ssl[1])], w=[(Sbfb, hh * 256, (hh + 1) * 256)])
                            for tc in range(4):
                                sg_ = (hh * 512 + tc * 128, hh * 512 + (tc + 1) * 128)
                                vs = (tc * 512 + hh * 256, tc * 512 + (hh + 1) * 256)
                                mm_group(PS(4, 0, 128), lambda k, sg_=sg_: ktb.ap(*sg_), lambda k, sg_=sg_: qtb.ap(*sg_), 1,
                                         r=[(ktb, sg_[0], sg_[1]), (qtb, sg_[0], sg_[1])], w=[PSr(4, 0, 128)])
                                S.dve(lambda e: e.tensor_tensor(out=PmB.ap(), in0=PS(4, 0, 128), in1=mask01.ap(), op=ALU.mult), r=[PSr(4, 0, 128), (mask01, 0, 128)], w=[(PmB, 0, 128)])
                                S.pe(lambda e, sg_=sg_: e.transpose(out=PS(7, 0, 64).bitcast(BF16), in_=ktb.ap(*sg_), identity=identb.ap()),
                                     r=[(ktb, sg_[0], sg_[1]), (identb, 0, 128)], w=[PSr(7, 0, 64)])
                                S.act(lambda e: e.activation(out=ktokb.ap(), in_=PS(7, 0, 64).bitcast(BF16), func=AF.Copy), r=[PSr(7, 0, 64)], w=[(ktokb, 0, 128)])
                                def emit_ob(e, sg_=sg_, vs=vs, hh=hh):
                                    e.matmul(PS(5, 0, 256), lhsT=PmB.ap(), rhs=vtokb.ap(*vs), start=True, stop=False)
                                    return e.matmul(PS(5, 0, 256), lhsT=qtb.ap(*sg_), rhs=Sbfb.ap(hh * 256, (hh + 1) * 256), start=False, stop=True)
                                S.pe(emit_ob, r=[(PmB, 0, 128), (vtokb, vs[0], vs[1]), (qtb, sg_[0], sg_[1]), (Sbfb, hh * 256, (hh + 1) * 256)], w=[PSr(5, 0, 256)])
                                st0 = 32 + tc * 4
                                S.act(lambda e, st0=st0: e.activation(out=junkfB.ap(0, 256), in_=PS(5, 0, 256), func=AF.Square, accum_out=stat.ap(st0, st0 + 1)),
                                      r=[PSr(5, 0, 256)], w=[(junkfB, 0, 256), (stat, st0, st0 + 1)])
                                S.act(lambda e, st0=st0: e.activation(out=stat.ap(st0 + 1, st0 + 2), in_=stat.ap(st0, st0 + 1), func=AF.Sqrt, bias=EPS, scale=1.0 / DV_B),
                                      r=[(stat, st0, st0 + 1)], w=[(stat, st0 + 1, st0 + 2)])
                                S.dve(lambda e, st0=st0: e.reciprocal(out=stat.ap(st0 + 2, st0 + 3), in_=stat.ap(st0 + 1, st0 + 2)), r=[(stat, st0 + 1, st0 + 2)], w=[(stat, st0 + 2, st0 + 3)])
                                S.dve(lambda e, st0=st0, vs=vs: e.scalar_tensor_tensor(out=otokb.ap(), in0=PS(5, 0, 256), scalar=stat.ap(st0 + 2, st0 + 3), in1=gsb.ap(*vs),
                                                                                      op0=ALU.mult, op1=ALU.mult),
                                      r=[PSr(5, 0, 256), (stat, st0 + 2, st0 + 3), (gsb, vs[0], vs[1])], w=[(otokb, 0, 256)])
                                def emit_otb(e):
                                    ins = None
                                    for j in range(2):
                                        ins = e.transpose(out=PS(7, 256, 384).bitcast(BF16)[:, j * 128:(j + 1) * 128], in_=otokb.ap(j * 128, (j + 1) * 128), identity=identb.ap())
                                    return ins
                                S.pe(emit_otb, r=[(otokb, 0, 256), (identb, 0, 128)], w=[PSr(7, 256, 384)])
                                oc0 = 16 + h * 2
                                S.act(lambda e, oc0=oc0, tc=tc: e.activation(out=oT.ap(oc0 * NT, (oc0 + 2) * NT).rearrange("p (j t) -> p j t", t=NT)[:, :, tc * 128:(tc + 1) * 128],
                                                                            in_=PS(7, 256, 384).bitcast(BF16).rearrange("p (j t) -> p j t", t=128), func=AF.Copy),
                                      r=[PSr(7, 256, 384)], w=[(oT, oc0 * NT, (oc0 + 2) * NT)])
                                mm_group(PS(6, 0, 256), lambda k: ktokb.ap(), lambda k, vs=vs: vtokb.ap(*vs), 1, r=[(ktokb, 0, 128), (vtokb, vs[0], vs[1])], w=[PSr(6, 0, 256)])
                                S.dve(lambda e, ssl=ssl, h=h: e.tensor_scalar(out=SB_.ap(*ssl), in0=SB_.ap(*ssl), scalar1=gam.ap(h, h + 1), scalar2=None, op0=ALU.mult),
                                      r=[(SB_, ssl[0], ssl[1]), (gam, h, h + 1)], w=[(SB_, ssl[0], ssl[1])])
                                S.dve(lambda e, ssl=ssl, h=h: e.scalar_tensor_tensor(out=SB_.ap(*ssl), in0=PS(6, 0, 256), scalar=gam.ap(h, h + 1), in1=SB_.ap(*ssl), op0=ALU.mult, op1=ALU.add),
                                      r=[PSr(6, 0, 256), (SB_, ssl[0], ssl[1]), (gam, h, h + 1)], w=[(SB_, ssl[0], ssl[1])])
                                S.act(lambda e, ssl=ssl, hh=hh: e.activation(out=Sbfb.ap(hh * 256, (hh + 1) * 256), in_=SB_.ap(*ssl), func=AF.Copy),
                                      r=[(SB_, ssl[0], ssl[1])], w=[(Sbfb, hh * 256, (hh + 1) * 256)])
                    if ti == n_tiles - 1:
                        S.dma("sp", ret_p[l].rearrange("h p v -> p h v"), SB_.ap().rearrange("p (h v) -> p h v", v=256), r=[(SB_, 0, 2048)], w=[("dram_ret_p", l, l + 1)])

                    ck('mixB')
                    arena.reset(tmp_base)
                    qTc = arena.alloc(BF16, 12 * 512)
                    c2_base = arena.off
                    kf = arena.alloc(F32, 1024)
                    k16 = arena.alloc(BF16, 1024)
                    kTs = arena.alloc(BF16, 1024)
                    for g in range(3):
                        W = WINDOWS[g]
                        keep_lo = SEQ - W
                        for i in range(4):
                            bk = 4 + (i % 2)
                            fm_dense(wv_in, O_HC + g * 1536 + i * 128, 128, KC, xn_fn, xn_reg, bk)
                            qs = ((g * 4 + i) * 512, (g * 4 + i + 1) * 512)
                            S.act(lambda e, qs=qs, bk=bk: e.activation(out=qTc.ap(*qs), in_=PS(bk), func=AF.Copy, scale=128 ** -0.5), r=[PSr(bk)], w=[(qTc, qs[0], qs[1])])
                        for which in (0, 1):
                            tm_dense(wv_in, O_HC + g * 1536 + 512 * (which + 1), KC, xn_lhs, xn_reg, [0, 1, 2, 3])
                            for tc in range(4):
                                tok0 = t0 + tc * 128
                                rb = (tc % 2) * 512
                                S.act(lambda e, tc=tc, rb=rb: e.activation(out=kf.ap(rb, rb + 512), in_=PS(tc), func=AF.Copy), r=[PSr(tc)], w=[(kf, rb, rb + 512)])
                                if tok0 >= keep_lo:
                                    S.dma("sp", wp[g][l, which, tok0 - keep_lo:tok0 - keep_lo + 128, :], kf.ap(rb, rb + 512), r=[(kf, rb, rb + 512)],
                                          w=[("dram_wp%d" % g, (l * 2 + which) * SEQ + tok0, (l * 2 + which) * SEQ + tok0 + 128)])
                                S.dve(lambda e, rb=rb: e.tensor_copy(out=k16.ap(rb, rb + 512), in_=kf.ap(rb, rb + 512)), r=[(kf, rb, rb + 512)], w=[(k16, rb, rb + 512)])
                                if which == 0:
                                    def emit_kT(e, rb=rb):
                                        ins = None
                                        for i in range(4):
                                            ins = e.transpose(out=PS(7, 0, 256).bitcast(BF16)[:, i * 128:(i + 1) * 128], in_=k16.ap(rb + i * 128, rb + (i + 1) * 128), identity=identb.ap())
                                        return ins
                                    S.pe(emit_kT, r=[(k16, rb, rb + 512), (identb, 0, 128)], w=[PSr(7, 0, 256)])
                                    S.act(lambda e, rb=rb: e.activation(out=kTs.ap(rb, rb + 512), in_=PS(7, 0, 256).bitcast(BF16), func=AF.Copy), r=[PSr(7, 0, 256)], w=[(kTs, rb, rb + 512)])
                                    S.dma("sp", kts[g].rearrange("p (i t) -> p i t", t=SEQ)[:, :, tok0:tok0 + 128], kTs.ap(rb, rb + 512).rearrange("p (i t) -> p i t", t=128),
                                          r=[(kTs, rb, rb + 512)], w=[("dram_kts%d" % g, tok0, tok0 + 128)])
                                else:
                                    S.dma("sp", vsc[g][tok0:tok0 + 128, :], k16.ap(rb, rb + 512), r=[(k16, rb, rb + 512)], w=[("dram_vsc%d" % g, tok0, tok0 + 128)])
                    ck('mixC1')
                    arena.reset(c2_base)
                    s_all = arena.alloc(F32, 3072)
                    p_all = arena.alloc(BF16, 3072)
                    pT = arena.alloc(BF16, 2048)
                    KTw = arena.alloc(BF16, 3712)
                    Vw = arena.alloc(BF16, 3712)
                    octok = arena.alloc(BF16, 128)
                    koff = (0, 640, 1664)
                    for i in range(4):
                        los = []
                        for g in range(3):
                            lo_g = max(0, t0 - WINDOWS[g])
                            n_g = t0 + NT - lo_g
                            los.append(lo_g)
                            S.dma("sp", KTw.ap(koff[g], koff[g] + n_g), kts[g][:, i * SEQ + lo_g:i * SEQ + t0 + NT], r=[("dram_kts%d" % g, lo_g, t0 + NT)], w=[(KTw, koff[g], koff[g] + n_g)])
                            S.dma("sp", Vw.ap(koff[g], koff[g] + n_g).rearrange("p (b d) -> p b d", d=128),
                                  vsc[g][lo_g:t0 + NT, i * 128:(i + 1) * 128].rearrange("(b s) d -> s b d", s=128),
                                  r=[("dram_vsc%d" % g, lo_g, t0 + NT)], w=[(Vw, koff[g], koff[g] + n_g)])
                        for qb in range(4):
                            q0 = t0 + qb * 128
                            col = 0
                            blocks = []
                            bankrot = 0
                            for g in range(3):
                                W = WINDOWS[g]
                                klo = max(0, q0 - W)
                                n = q0 + 128 - klo
                                tj_lo = TJ_OFF[g] + (klo - (q0 - W))
                                kbase = koff[g] + (klo - los[g])
                                qs = ((g * 4 + i) * 512 + qb * 128, (g * 4 + i) * 512 + (qb + 1) * 128)
                                coef = -alibi_slope(g, i) * DILS[g]
                                for pc in range(0, n, 512):
                                    pn = min(512, n - pc)
                                    bk = bankrot % 4
                                    bankrot += 1
                                    mm_group(PS(bk, 0, pn), lambda k, qs=qs: qTc.ap(*qs), lambda k, kbase=kbase, pc=pc, pn=pn: KTw.ap(kbase + pc, kbase + pc + pn), 1,
                                             r=[(qTc, qs[0], qs[1]), (KTw, kbase + pc, kbase + pc + pn)], w=[PSr(bk, 0, pn)])
                                    S.dve(lambda e, bk=bk, pn=pn, col=col, pc=pc, tj_lo=tj_lo, coef=coef: e.scalar_tensor_tensor(
                                        out=s_all.ap(col + pc, col + pc + pn), in0=tjb.ap(tj_lo + pc, tj_lo + pc + pn), scalar=coef, in1=PS(bk, 0, pn), op0=ALU.mult, op1=ALU.add),
                                        r=[(tjb, tj_lo + pc, tj_lo + pc + pn), PSr(bk, 0, pn)], w=[(s_all, col + pc, col + pc + pn)])
                                for kb in range(n // 128):
                                    blocks.append((col + kb * 128, kbase + kb * 128))
                                col += n
                            S.dve(lambda e, col=col: e.tensor_reduce(out=stat.ap(48, 49), in_=s_all.ap(0, col), axis=AX.X, op=ALU.max, negate=True), r=[(s_all, 0, col)], w=[(stat, 48, 49)])
                            S.act(lambda e, col=col: e.activation(out=p_all.ap(0, col), in_=s_all.ap(0, col), func=AF.Exp, bias=stat.ap(48, 49), scale=1.0, accum_out=stat.ap(49, 50)),
                                  r=[(s_all, 0, col), (stat, 48, 49)], w=[(p_all, 0, col), (stat, 49, 50)])
                            S.dve(lambda e: e.reciprocal(out=stat.ap(50, 51), in_=stat.ap(49, 50)), r=[(stat, 49, 50)], w=[(stat, 50, 51)])
                            nb = len(blocks)
                            for b8 in range(0, nb, 8):
                                bn = min(8, nb - b8)
                                half = (b8 // 8) % 2
                                def emit_pt(e, b8=b8, bn=bn, blocks=blocks):
                                    ins = None
                                    for j in range(bn):
                                        pc0 = blocks[b8 + j][0]
                                        ins = e.transpose(out=PS(7).bitcast(BF16)[:, j * 128:(j + 1) * 128], in_=p_all.ap(pc0, pc0 + 128), identity=identb.ap())
                                    return ins
                                S.pe(emit_pt, r=[(p_all, blocks[b8][0], blocks[b8 + bn - 1][0] + 128), (identb, 0, 128)], w=[PSr(7, 0, bn * 64)])
                                pts = (half * 1024, half * 1024 + bn * 128)
                                if half == 0:
                                    S.dve(lambda e, pts=pts, bn=bn: e.tensor_copy(out=pT.ap(*pts), in_=PS(7, 0, bn * 64).bitcast(BF16)), r=[PSr(7, 0, bn * 64)], w=[(pT, pts[0], pts[1])])
                                else:
                                    S.act(lambda e, pts=pts, bn=bn: e.activation(out=pT.ap(*pts), in_=PS(7, 0, bn * 64).bitcast(BF16), func=AF.Copy), r=[PSr(7, 0, bn * 64)], w=[(pT, pts[0], pts[1])])
                                def emit_pv(e, b8=b8, bn=bn, half=half, nb=nb, blocks=blocks):
                                    ins = None
                                    for j in range(bn):
                                        vb0 = blocks[b8 + j][1]
                                        ins = e.matmul(PS(6, 0, 128), lhsT=pT.ap(half * 1024 + j * 128, half * 1024 + (j + 1) * 128), rhs=Vw.ap(vb0, vb0 + 128),
                                                       start=(b8 + j == 0), stop=(b8 + j == nb - 1))
                                    return ins
                                S.pe(emit_pv, r=[(pT, pts[0], pts[1]), (Vw, 0, 3712)], w=[PSr(6, 0, 128)])
                            S.act(lambda e: e.activation(out=octok.ap(), in_=PS(6, 0, 128), func=AF.Copy, scale=stat.ap(50, 51)), r=[PSr(6, 0, 128), (stat, 50, 51)], w=[(octok, 0, 128)])
                            S.pe(lambda e: e.transpose(out=PS(5, 256, 320).bitcast(BF16), in_=octok.ap(), identity=identb.ap()), r=[(octok, 0, 128), (identb, 0, 128)], w=[PSr(5, 256, 320)])
                            od = ((32 + i) * NT + qb * 128, (32 + i) * NT + (qb + 1) * 128)
                            S.dve(lambda e, od=od: e.tensor_copy(out=oT.ap(*od), in_=PS(5, 256, 320).bitcast(BF16)), r=[PSr(5, 256, 320)], w=[(oT, od[0], od[1])])

                    ck('mixC')
                    arena.reset(tmp_base + 32 * NT * 2)
                    sg = [arena.alloc(F32, 512) for _ in range(3)]
                    acc = arena.alloc(F32, 512)
                    wv_mg = wview(w_merge[l])
                    wv_ups = [(wview(w_up_a[l]), 16, 0), (wview(w_up_b[l]), 16, 16), (wview(w_up_c[l]), 4, 32)]
                    oT_reg = (oT, 0, 36 * NT)
                    for j in range(KC):
                        for b in range(3):
                            fm_dense(wv_mg, b * D + j * 128, 128, KC, xn_fn, xn_reg, b)
                            S.act(lambda e, b=b: e.activation(out=sg[b].ap(), in_=PS(b), func=AF.Sigmoid), r=[PSr(b)], w=[(sg[b], 0, 512)])
                        for b, (wvu, kn, k0) in enumerate(wv_ups):
                            fm_dense(wvu, j * 128, 128, kn, (lambda k, k0=k0: oT.ap((k0 + k) * NT, (k0 + k + 1) * NT)), oT_reg, 3 + b)
                        S.dve(lambda e: e.tensor_tensor(out=acc.ap(), in0=PS(3), in1=sg[0].ap(), op=ALU.mult), r=[PSr(3), (sg[0], 0, 512)], w=[(acc, 0, 512)])
                        S.dve(lambda e: e.tensor_tensor(out=sg[1].ap(), in0=PS(4), in1=sg[1].ap(), op=ALU.mult), r=[PSr(4), (sg[1], 0, 512)], w=[(sg[1], 0, 512)])
                        S.dve(lambda e: e.tensor_tensor(out=sg[2].ap(), in0=PS(5), in1=sg[2].ap(), op=ALU.mult), r=[PSr(5), (sg[2], 0, 512)], w=[(sg[2], 0, 512)])
                        S.dve(lambda e: e.tensor_tensor(out=acc.ap(), in0=acc.ap(), in1=sg[1].ap(), op=ALU.add), r=[(acc, 0, 512), (sg[1], 0, 512)], w=[(acc, 0, 512)])
                        S.dve(lambda e, j=j: e.tensor_tensor(out=mT.ap(j * NT, (j + 1) * NT), in0=acc.ap(), in1=sg[2].ap(), op=ALU.add),
                              r=[(acc, 0, 512), (sg[2], 0, 512)], w=[(mT, j * NT, (j + 1) * NT)])
                    wv_o = wview(w_out[l])
                    resid_phase(wv_o, KC, (lambda k, tc: mT.ap(k * NT + tc * 128, k * NT + (tc + 1) * 128)), (mT, 0, KC * NT), [0, 1, 2, 3], x_src, xa, t0)

                    ck('merge')
                    arena.reset()
                    rowbuf = arena.alloc(F32, D)
                    xnjunk = arena.alloc(BF16, D)
                    norm_tile(xa, t0, gffn, rowbuf)
                    arena.reset()
                    hT = arena.alloc(BF16, FKC * NT)
                    wv_g, wv_u = wview(w_fg[l]), wview(w_fu[l])
                    for j in range(FKC):
                        bg, bu = (j % 2) * 2, (j % 2) * 2 + 1
                        fm_dense(wv_g, j * 128, 128, KC, xn_fn, xn_reg, bg)
                        fm_dense(wv_u, j * 128, 128, KC, xn_fn, xn_reg, bu)
                        S.act(lambda e, bg=bg: e.activation(out=sgf.ap(), in_=PS(bg), func=AF.Silu), r=[PSr(bg)], w=[(sgf, 0, 512)])
                        S.dve(lambda e, bu=bu, j=j: e.tensor_tensor(out=hT.ap(j * NT, (j + 1) * NT), in0=PS(bu), in1=sgf.ap(), op=ALU.mult),
                              r=[PSr(bu), (sgf, 0, 512)], w=[(hT, j * NT, (j + 1) * NT)])
                    resid_phase(wview(w_fd[l]), FKC, (lambda k, tc: hT.ap(k * NT + tc * 128, k * NT + (tc + 1) * 128)), (hT, 0, FKC * NT), [4, 5, 6, 7], xa, xb, t0)

            ck('ffn')
            arena.reset()
            rowbuf = arena.alloc(F32, D)
            xnjunk = arena.alloc(BF16, D)
            gfb = arena.alloc(F32, D)
            S.dma("sp", gfb.ap(), g_final_row.partition_broadcast(128), w=[(gfb, 0, D)])
            S.dma("sp", rowbuf.ap(0, D, 0, 1), xsb, r=[("dram_xsb", 0, 1)], w=[(rowbuf, 0, D)])
            S.act(lambda e: e.activation(out=xnjunk.ap(0, D, 0, 1), in_=rowbuf.ap(0, D, 0, 1), func=AF.Square, accum_out=stat.ap(0, 1, 0, 1)), r=[(rowbuf, 0, D)], w=[(xnjunk, 0, D), (stat, 0, 1)])
            S.act(lambda e: e.activation(out=stat.ap(1, 2, 0, 1), in_=stat.ap(0, 1, 0, 1), func=AF.Sqrt, bias=EPS, scale=1.0 / D), r=[(stat, 0, 1)], w=[(stat, 1, 2)])
            S.dve(lambda e: e.reciprocal(out=stat.ap(2, 3, 0, 1), in_=stat.ap(1, 2, 0, 1)), r=[(stat, 1, 2)], w=[(stat, 2, 3)])
            S.dve(lambda e: e.scalar_tensor_tensor(out=rowbuf.ap(0, D, 0, 1), in0=rowbuf.ap(0, D, 0, 1), scalar=stat.ap(2, 3, 0, 1), in1=gfb.ap(0, D, 0, 1), op0=ALU.mult, op1=ALU.mult),
                  r=[(rowbuf, 0, D), (stat, 2, 3), (gfb, 0, D)], w=[(rowbuf, 0, D)])
            S.dma("sp", ys, rowbuf.ap(0, D, 0, 1), r=[(rowbuf, 0, D)], w=[("dram_ys", 0, 1)])
            for r0 in range(0, n_tiles * NT, 128):
                S.dma("sp", rowbuf.ap(), xb[r0:r0 + 128, :], r=[("dram_xb", r0, r0 + 128)], w=[(rowbuf, 0, D)])
                S.act(lambda e: e.activation(out=xnjunk.ap(), in_=rowbuf.ap(), func=AF.Square, accum_out=stat.ap(0, 1)), r=[(rowbuf, 0, D)], w=[(xnjunk, 0, D), (stat, 0, 1)])
                S.act(lambda e: e.activation(out=stat.ap(1, 2), in_=stat.ap(0, 1), func=AF.Sqrt, bias=EPS, scale=1.0 / D), r=[(stat, 0, 1)], w=[(stat, 1, 2)])
                S.dve(lambda e: e.reciprocal(out=stat.ap(2, 3), in_=stat.ap(1, 2)), r=[(stat, 1, 2)], w=[(stat, 2, 3)])
                S.dve(lambda e: e.scalar_tensor_tensor(out=rowbuf.ap(), in0=rowbuf.ap(), scalar=stat.ap(2, 3), in1=gfb.ap(), op0=ALU.mult, op1=ALU.mult),
                      r=[(rowbuf, 0, D), (stat, 2, 3), (gfb, 0, D)], w=[(rowbuf, 0, D)])
                S.dma("sp", yp[r0:r0 + 128, :], rowbuf.ap(), r=[(rowbuf, 0, D)], w=[("dram_yp", r0, r0 + 128)])

        except _Stop:
            pass
        S.replay(block)
        nc_stats = (dict(S.cnt), list(S.sp_cnt), list(S.pool_cnt))
    build_program.stats = nc_stats
    return nc


_NC_CACHE = {}


def kernel(**inputs):
    n = 8
    if "nc" not in _NC_CACHE:
        _NC_CACHE["nc"] = build_program()
    nc = _NC_CACHE["nc"]
    consts = make_consts()
    f = lambda a: np.ascontiguousarray(np.asarray(a, dtype=np.float32))
    shared = {k: f(inputs[k]) for k in ("w_in", "w_alpha2", "g_gla", "w_merge", "w_up_a", "w_up_b", "w_up_c", "w_out",
                                       "w_ffn_gate", "w_ffn_up", "w_ffn_down")}
    shared["g_mix_t"] = f(np.asarray(inputs["g_mix"]).reshape(DEPTH, KC, 128).transpose(0, 2, 1))
    shared["g_ffn_t"] = f(np.asarray(inputs["g_ffn"]).reshape(DEPTH, KC, 128).transpose(0, 2, 1))
    shared["g_final_t"] = f(np.asarray(inputs["g_final"]).reshape(KC, 128).T)
    shared["g_final_row"] = f(inputs["g_final"])
    shared["g_mix_r"] = f(inputs["g_mix"])
    shared["g_ffn_r"] = f(inputs["g_ffn"])
    shared["b_alpha_r"] = f(inputs["b_alpha"])
    shared["b_alpha_t"] = f(np.asarray(inputs["b_alpha"]).reshape(DEPTH, 8, 128).transpose(0, 2, 1))
    for k, v in consts.items():
        shared["c_" + k] = v
    in_maps = []
    for c in range(n):
        m = dict(shared)
        m["xp"] = f(inputs["x_prompt"][c % 4])
        m["xs"] = f(inputs["x_sample"][c])
        m["sgla"] = f(inputs["state_gla"][:, c])
        m["sret"] = f(inputs["state_ret"][:, c])
        for g in range(3):
            cwg = np.asarray(inputs["cache_win%d" % g])[:, c]
            m["cw%d" % g] = f(cwg.reshape(DEPTH, 2, WINDOWS[g], 512))
        in_maps.append(m)
    res = run_bass_kernel_spmd(nc, in_maps, core_ids=list(range(n)))
    R = res.results
    y_prompt = np.stack([R[b]["yp"] for b in range(4)])
    y_sample = np.stack([R[c]["ys"] for c in range(8)])
    outs = [y_prompt, y_sample]
    for nm in ("gla", "ret"):
        outs.append(np.stack([R[b][nm + "_p"] for b in range(4)], axis=1))
        outs.append(np.stack([R[c][nm + "_s"] for c in range(8)], axis=1))
    for g in range(3):
        W = WINDOWS[g]
        outs.append(np.stack([R[b]["w%dp" % g] for b in range(4)], axis=1).reshape(DEPTH, 4, 2, W, 4, 128))
        outs.append(np.stack([R[c]["w%ds" % g] for c in range(8)], axis=1).reshape(DEPTH, 8, 2, W, 4, 128))
    return tuple(np.ascontiguousarray(o, dtype=np.float32) for o in outs)
```

```python
import numpy as np
import ml_dtypes
import concourse.bass as bass
import concourse.mybir as mybir
from concourse.bass_utils import run_bass_kernel_spmd

F32 = mybir.dt.float32
BF16 = mybir.dt.bfloat16
AF = mybir.ActivationFunctionType
ALU = mybir.AluOpType
AX = mybir.AxisListType

D = 4096
SEQ = 2048
DEPTH = 2
NT = 512
NTILES = SEQ // NT
KC = D // 128
H_A, DK_A, DV_A = 4, 256, 512
H_B, DK_B, DV_B = 8, 128, 256
WINDOWS = (128, 512, 2048)
DILS = (1, 4, 16)
N_IN = 16912
D_FF = 11008
FKC = D_FF // 128
PAST = 16384
EPS = 1e-6
O_QA, O_KA, O_VA, O_RA, O_ZA, O_QB, O_KB, O_VB, O_GB, O_HC = 0, 1024, 2048, 4096, 6144, 6160, 7184, 8208, 10256, 12304
TJ_OFF = (0, 256, 896)
NS = 4
SLOT = 4096


def _esz(dt):
    return 4 if dt == F32 else 2


class Buf:
    def __init__(self, t, name, tdt, dt, base_bytes, n, space="sb"):
        self.t, self.name, self.tdt, self.dt, self.base, self.n, self.space = t, name, tdt, dt, base_bytes, n, space
        self.esz = _esz(dt)

    def ap(self, lo=0, hi=None, p0=0, p1=128):
        hi = self.n if hi is None else hi
        b0 = self.base + lo * self.esz
        b1 = self.base + hi * self.esz
        te = _esz(self.tdt)
        if self.dt == self.tdt:
            assert b0 % te == 0 and b1 % te == 0
            return self.t[p0:p1, b0 // te:b1 // te]
        a0 = b0 // te * te
        a1 = (b1 + te - 1) // te * te
        v = self.t[p0:p1, a0 // te:a1 // te].bitcast(self.dt)
        off = (b0 - a0) // self.esz
        return v[:, off:off + (hi - lo)]

    def reg(self, lo=0, hi=None):
        hi = self.n if hi is None else hi
        return (self.name, self.base + lo * self.esz, self.base + hi * self.esz)


class Sched:
    def __init__(self, nc, sems):
        self.nc = nc
        self.ops = {e: [] for e in ("pe", "dve", "act", "pool", "sp")}
        self.sem = sems
        self.cnt = {e: 0 for e in ("pe", "dve", "act")}
        self.waited = {}
        self.track = {}
        self.sp_ch = 0
        self.sp_cnt = [0] * 8
        self.pool_cnt = [0] * NS

    def _deps(self, regs_r, regs_w):
        deps = set()
        for (name, lo, hi) in regs_r:
            for rec in self.track.get(name, []):
                if rec[2] == "w" and rec[0] < hi and lo < rec[1]:
                    deps.add(rec[3])
        for (name, lo, hi) in regs_w:
            for rec in self.track.get(name, []):
                if rec[0] < hi and lo < rec[1]:
                    deps.add(rec[3])
        return deps

    def _record(self, regs_r, regs_w, tag):
        for (name, lo, hi) in regs_w:
            lst = self.track.setdefault(name, [])
            new = []
            for rec in lst:
                if rec[0] >= lo and rec[1] <= hi:
                    continue
                new.append(rec)
            new.append([lo, hi, "w", tag])
            self.track[name] = new
        for (name, lo, hi) in regs_r:
            lst = self.track.setdefault(name, [])
            new = []
            for rec in lst:
                if rec[2] == "r" and rec[0] >= lo and rec[1] <= hi and rec[3][0] == tag[0]:
                    continue
                new.append(rec)
            new.append([lo, hi, "r", tag])
            self.track[name] = new

    def _waits(self, eng, deps):
        best = {}
        for (sn, val) in deps:
            if val > best.get(sn, 0):
                best[sn] = val
        out = []
        for sn, val in best.items():
            if self.waited.get((eng, sn), 0) < val:
                self.waited[(eng, sn)] = val
                out.append((sn, val))
        return out

    def op(self, eng, emit, r=(), w=()):
        rr = [b.reg(lo, hi) for (b, lo, hi) in r]
        ww = [b.reg(lo, hi) for (b, lo, hi) in w]
        waits = self._waits(eng, self._deps(rr, ww))
        self.cnt[eng] += 1
        tag = ("s_" + eng, self.cnt[eng])
        self.ops[eng].append((waits, emit, ("s_" + eng, 1)))
        self._record(rr, ww, tag)

    def pe(self, emit, r=(), w=()):
        self.op("pe", emit, r, w)

    def dve(self, emit, r=(), w=()):
        self.op("dve", emit, r, w)

    def act(self, emit, r=(), w=()):
        self.op("act", emit, r, w)

    def dma(self, q, out, in_, r=(), w=(), slot=None):
        rr = [x if isinstance(x[0], str) else x[0].reg(x[1], x[2]) for x in r]
        ww = [x if isinstance(x[0], str) else x[0].reg(x[1], x[2]) for x in w]
        deps = self._deps(rr, ww)
        if q == "pool":
            ch = slot
            sn = "s_w%d" % ch
            prev = self.pool_cnt[ch]
            self.pool_cnt[ch] += 1
            val = 16 * self.pool_cnt[ch]
        else:
            ch = self.sp_ch
            self.sp_ch = (self.sp_ch + 1) % 8
            sn = "s_d%d" % ch
            prev = self.sp_cnt[ch]
            self.sp_cnt[ch] += 1
            val = 16 * self.sp_cnt[ch]
        if prev:
            deps.add((sn, 16 * prev))
        waits = self._waits(q, deps)
        self.ops[q].append((waits, (lambda e, out=out, in_=in_: e.dma_start(out=out, in_=in_)), (sn, 16)))
        self._record(rr, ww, (sn, val))

    def final_waits(self):
        out = []
        for ch in range(8):
            if self.sp_cnt[ch]:
                out.append(("s_d%d" % ch, 16 * self.sp_cnt[ch]))
        for ch in range(NS):
            if self.pool_cnt[ch]:
                out.append(("s_w%d" % ch, 16 * self.pool_cnt[ch]))
        return out

    def replay(self, block):
        nc = self.nc
        sem = self.sem

        def run(e, lst, extra=None):
            for (waits, emit, inc) in lst:
                for (sn, val) in waits:
                    e.wait_ge(sem[sn], val)
                ins = emit(e)
                ins.then_inc(sem[inc[0]], inc[1])
            if extra:
                for (sn, val) in extra:
                    e.wait_ge(sem[sn], val)

        fw = self.final_waits()

        @block.tensor
        def _(e):
            run(e, self.ops["pe"])

        @block.vector
        def _(e):
            run(e, self.ops["dve"])

        @block.scalar
        def _(e):
            run(e, self.ops["act"])

        @block.gpsimd
        def _(e):
            run(e, self.ops["pool"])

        @block.sync
        def _(e):
            run(e, self.ops["sp"], extra=fw)


class Arena:
    def __init__(self, t, name, tdt, nbytes, base=0):
        self.t, self.name, self.tdt, self.nbytes, self.base = t, name, tdt, nbytes, base
        self.off = 0

    def reset(self, off=0):
        self.off = off

    def alloc(self, dt, n):
        e = _esz(dt)
        self.off = (self.off + 3) // 4 * 4
        b = Buf(self.t, self.name, self.tdt, dt, self.base + self.off, n)
        self.off += n * e
        assert self.off <= self.nbytes, (self.name, self.off, self.nbytes)
        return b


def alibi_slope(g, i):
    return 2.0 ** (-8.0 * (g * 4 + i + 1) / 12.0)


def make_consts():
    c = {}
    c["identf"] = np.eye(128, dtype=np.float32)
    s = np.arange(128)
    c["mask01"] = (s[:, None] <= s[None, :]).astype(np.float32)
    perm = np.zeros((128, 128), np.float32)
    for m in range(128):
        perm[(m + 64) % 128, m] = 1.0
    c["permf"] = perm
    half = 64
    inv = (10000.0 ** (-np.arange(half, dtype=np.float32) / half)).astype(np.float32)
    pos = np.concatenate([np.arange(SEQ), [PAST]]).astype(np.float32)
    ang = (pos[None, :] * inv[:, None]).astype(np.float32).astype(np.float64)
    cos = np.cos(ang)
    sin = np.sin(ang)
    c["cost"] = np.concatenate([cos, cos], 0).astype(np.float32)
    c["sint"] = np.concatenate([-sin, sin], 0).astype(np.float32)
    lg = np.log1p(-(2.0 ** (-5.0 - np.arange(H_B, dtype=np.float64))))
    t = np.arange(128, dtype=np.float64)
    ep = np.exp(lg[:, None] * (t[None, :] + 1))
    em = np.exp(-lg[:, None] * (t[None, :] + 1)) * (DK_B ** -0.5)
    c["rEp"] = np.broadcast_to(ep[None], (128, 8, 128)).reshape(128, 1024).astype(np.float32).copy()
    c["rEm"] = np.broadcast_to(em[None], (128, 8, 128)).reshape(128, 1024).astype(np.float32).copy()
    g128 = np.exp(lg * 128)
    g1 = np.exp(lg)
    c["gam"] = np.broadcast_to(np.concatenate([g128, g1])[None], (128, 16)).astype(np.float32).copy()
    tj = np.full((128, 3072), 1e30, np.float64)
    p = np.arange(128)[:, None]
    for g in range(3):
        W, d = WINDOWS[g], DILS[g]
        cc = np.arange(W + 128)[None, :]
        delta = p + W - cc
        valid = (delta >= 0) & (delta % d == 0) & (delta // d <= 128)
        tj[:, TJ_OFF[g]:TJ_OFF[g] + W + 128] = np.where(valid, delta // d, 1e30)
    c["tj"] = tj.astype(np.float32)
    c["onesf"] = np.ones((128, 128), np.float32)
    c["csrow"] = np.concatenate([cos[:, SEQ], sin[:, SEQ]])[None, :].astype(np.float32)
    sb_ = np.zeros((128, 12), np.float64)
    r = np.arange(128)
    for g in range(3):
        for i in range(4):
            sb_[:, g * 4 + i] = -alibi_slope(g, i) * DILS[g] * (128 - r)
    c["sbias"] = sb_.astype(np.float32)
    return c


class _Stop(Exception):
    pass


def build_program(n_layers=DEPTH, n_tiles=NTILES, stop=None, skip_sample=False):
    def ck(name):
        if stop == name:
            raise _Stop()

    nc = bass.Bass("TRN2", target_bir_lowering=False)

    def din(name, shape):
        return nc.dram_tensor(name, list(shape), F32, kind="ExternalInput").ap()

    def dout(name, shape):
        return nc.dram_tensor(name, list(shape), F32, kind="ExternalOutput").ap()

    xp = din("xp", (SEQ, D))
    xs = din("xs", (1, D))
    sgla = din("sgla", (DEPTH, H_A, DK_A, DV_A))
    sret = din("sret", (DEPTH, H_B, DK_B, DV_B))
    cw = [din("cw%d" % g, (DEPTH, 2, WINDOWS[g], 512)) for g in range(3)]
    g_mix = din("g_mix_t", (DEPTH, 128, KC))
    w_in = din("w_in", (DEPTH, D, N_IN))
    w_alpha2 = din("w_alpha2", (DEPTH, 16, 1024))
    b_alpha = din("b_alpha_t", (DEPTH, 128, 8))
    g_gla = din("g_gla", (DEPTH, 512))
    w_merge = din("w_merge", (DEPTH, D, 3 * D))
    w_up_a = din("w_up_a", (DEPTH, 2048, D))
    w_up_b = din("w_up_b", (DEPTH, 2048, D))
    w_up_c = din("w_up_c", (DEPTH, 512, D))
    w_out = din("w_out", (DEPTH, D, D))
    g_ffn = din("g_ffn_t", (DEPTH, 128, KC))
    w_fg = din("w_ffn_gate", (DEPTH, D, D_FF))
    w_fu = din("w_ffn_up", (DEPTH, D, D_FF))
    w_fd = din("w_ffn_down", (DEPTH, D_FF, D))
    g_final = din("g_final_t", (128, KC))
    g_final_row = din("g_final_row", (D,))
    g_mix_r = din("g_mix_r", (DEPTH, D))
    g_ffn_r = din("g_ffn_r", (DEPTH, D))
    b_alpha_r = din("b_alpha_r", (DEPTH, 1024))
    cn = {k: din("c_" + k, v.shape) for k, v in make_consts().items()}

    yp = dout("yp", (SEQ, D))
    ys = dout("ys", (1, D))
    gla_p = dout("gla_p", (DEPTH, H_A, DK_A, DV_A))
    gla_s = dout("gla_s", (DEPTH, H_A, DK_A, DV_A))
    ret_p = dout("ret_p", (DEPTH, H_B, DK_B, DV_B))
    ret_s = dout("ret_s", (DEPTH, H_B, DK_B, DV_B))
    wp = [dout("w%dp" % g, (DEPTH, 2, WINDOWS[g], 512)) for g in range(3)]
    ws = [dout("w%ds" % g, (DEPTH, 2, WINDOWS[g], 512)) for g in range(3)]

    xa = nc.dram_tensor("xa", [SEQ, D], F32).ap()
    xb = nc.dram_tensor("xb", [SEQ, D], F32).ap()
    kts = [nc.dram_tensor("kts%d" % g, [128, 4 * SEQ], BF16).ap() for g in range(3)]
    vsc = [nc.dram_tensor("vsc%d" % g, [SEQ, 512], BF16).ap() for g in range(3)]
    xsa = nc.dram_tensor("xsa", [1, D], F32).ap()
    WCN = 214
    wcaches = [nc.dram_tensor("wcache%d" % i, [WCN, 128, SLOT], BF16).ap() for i in range(3)]
    xsb = nc.dram_tensor("xsb", [1, D], F32).ap()

    sem_names = ["s_pe", "s_dve", "s_act"] + ["s_d%d" % i for i in range(8)] + ["s_w%d" % i for i in range(NS)]
    import contextlib
    with contextlib.ExitStack() as es:
        def sb(name, n, dt):
            return es.enter_context(nc.sbuf_tensor(name, [128, n], dt))

        t_xnT = sb("xnT", KC * NT, BF16)
        t_wsl = sb("wsl", NS * SLOT, BF16)
        ARENA_B = FKC * NT * 2
        t_arena = sb("arena", ARENA_B // 2, BF16)
        t_SA = sb("SA", 8 * 512, F32)
        t_SB = sb("SB", 8 * 256, F32)
        t_cf = sb("cf", 4864, F32)
        t_tj = sb("tj", 3072, BF16)
        t_misc = sb("misc", 1792, F32)
        t_ps = es.enter_context(nc.psum_tensor("ps", [128, 8 * 512], F32))
        sems = {n: es.enter_context(nc.semaphore(n)) for n in sem_names}
        block = es.enter_context(nc.Block())
        S = Sched(nc, sems)

        xnT = Buf(t_xnT, "xnT", BF16, BF16, 0, KC * NT)
        wsl = Buf(t_wsl, "wsl", BF16, BF16, 0, NS * SLOT)
        SA = Buf(t_SA, "SA", F32, F32, 0, 4096)
        SB_ = Buf(t_SB, "SB", F32, F32, 0, 2048)
        tjb = Buf(t_tj, "tj", BF16, BF16, 0, 3072)
        cfA = Arena(t_cf, "cf", F32, 4864 * 4)
        identf = cfA.alloc(F32, 128)
        mask01 = cfA.alloc(F32, 128)
        permf = cfA.alloc(F32, 128)
        onesf = cfA.alloc(F32, 128)
        cosb = cfA.alloc(F32, 512)
        sinb = cfA.alloc(F32, 512)
        rEp = cfA.alloc(F32, 1024)
        rEm = cfA.alloc(F32, 1024)
        gam = cfA.alloc(F32, 16)
        gmix = cfA.alloc(F32, 32)
        gffn = cfA.alloc(F32, 32)
        gfin = cfA.alloc(F32, 32)
        ggla = cfA.alloc(F32, 512)
        negb = cfA.alloc(F32, 8)
        identb = cfA.alloc(BF16, 128)
        wa2 = cfA.alloc(BF16, 1024)
        mA = Arena(t_misc, "misc", F32, 1792 * 4)
        stat = mA.alloc(F32, 64)
        arena = Arena(t_arena, "arena", BF16, ARENA_B)
        psb = [Buf(t_ps, "ps", F32, F32, i * 2048, 512, space="ps") for i in range(8)]

        wslot_ptr = [0]

        def PS(i, lo=0, hi=512, p0=0, p1=128):
            return psb[i].ap(lo, hi, p0, p1)

        def PSr(i, lo=0, hi=512):
            return (psb[i], 0, 512)

        try:
            def cload(buf, src, n=None, q="sp"):
                S.dma(q, buf.ap(0, n), src, w=[(buf, 0, n if n else buf.n)])

            cload(identf, cn["identf"])
            cload(mask01, cn["mask01"])
            cload(permf, cn["permf"])
            cload(onesf, cn["onesf"])
            cload(rEp, cn["rEp"])
            cload(rEm, cn["rEm"])
            cload(gam, cn["gam"])
            cload(gfin, g_final)
            S.act(lambda e: e.activation(out=identb.ap(), in_=identf.ap(), func=AF.Copy), r=[(identf, 0, 128)], w=[(identb, 0, 128)])
            S.dma("pool", tjb.ap().rearrange("p (a b) -> p a b", b=512), cn["tj"].rearrange("p (a b) -> p a b", b=512), w=[(tjb, 0, 3072)], slot=0)

            wseq = [0]
            wmode = ["stream"]

            def wload(src3, k, m):
                s = wslot_ptr[0]
                wslot_ptr[0] = (s + 1) % NS
                lo, hi = s * SLOT, s * SLOT + k * m
                dst = wsl.ap(lo, hi).rearrange("p (k c) -> p k c", c=m)
                if wmode[0] == "cached":
                    q = wseq[0]
                    wseq[0] += 1
                    S.dma("pool", wsl.ap(lo, hi), wcaches[q // WCN][q % WCN, :, 0:k * m], r=[("dram_wc", q, q + 1)], w=[(wsl, lo, hi)], slot=s)
                else:
                    S.dma("pool", dst, src3, w=[(wsl, lo, hi)], slot=s)
                    if wmode[0] == "fill":
                        q = wseq[0]
                        wseq[0] += 1
                        S.dma("sp", wcaches[q // WCN][q % WCN, :, 0:k * m], wsl.ap(lo, hi), r=[(wsl, lo, hi)], w=[("dram_wc", q, q + 1)])
                return dst, (wsl, lo, hi)

            def wview(wmat):
                return wmat.rearrange("(k p) c -> p k c", p=128)

            def mm_group(out_ap, lhs_fn, rhs_fn, nk, r, w, first=True, last=True):
                def emit(e):
                    ins = None
                    for k in range(nk):
                        ins = e.matmul(out_ap, lhsT=lhs_fn(k), rhs=rhs_fn(k), start=(first and k == 0), stop=(last and k == nk - 1))
                    return ins
                S.pe(emit, r=r, w=w)

            def fm_dense(wv, c0, m, kcn, act_fn, act_reg, out_bank, n=NT, mp=None, samp=None):
                sv, sreg = wload(wv[:, 0:kcn, c0:c0 + m], kcn, m)
                mm_group(PS(out_bank, 0, n, 0, m), lambda k: sv[:, k, :], act_fn, kcn, r=[sreg, act_reg], w=[PSr(out_bank, 0, n)])
                if samp is not None:
                    lcols, sbank, scol = samp
                    mm_group(PS(sbank, scol, scol + 1, 0, m), lambda k: sv[:, k, :], lambda k: lcols.ap(k, k + 1), kcn,
                             r=[sreg, (lcols, 0, kcn)], w=[PSr(sbank)])

            def xn_fn(k):
                return xnT.ap(k * NT, (k + 1) * NT)
            xn_reg = (xnT, 0, KC * NT)

            def tm_dense(wv, c0, kcn, lhs_fn, lhs_reg, banks, ncols=512, samp=None):
                nparts = (kcn + 7) // 8
                for kp in range(nparts):
                    k0 = kp * 8
                    kn = min(8, kcn - k0)
                    sv, sreg = wload(wv[:, k0:k0 + kn, c0:c0 + ncols], kn, ncols)
                    for tc in range(4):
                        mm_group(PS(banks[tc], 0, ncols), (lambda k, tc=tc, k0=k0: lhs_fn(k0 + k, tc)), (lambda k, sv=sv: sv[:, k, :]), kn,
                                 r=[sreg, lhs_reg], w=[PSr(banks[tc], 0, ncols)], first=(kp == 0), last=(kp == nparts - 1))
                    if samp is not None:
                        lcols, sbank = samp
                        mm_group(PS(sbank, 0, ncols, 0, 1), (lambda k, k0=k0: lcols.ap(k0 + k, k0 + k + 1)), (lambda k, sv=sv: sv[:, k, :]), kn,
                                 r=[sreg, (lcols, 0, kcn)], w=[PSr(sbank)], first=(kp == 0), last=(kp == nparts - 1))

            def xn_lhs(k, tc):
                return xnT.ap(k * NT + tc * 128, k * NT + (tc + 1) * 128)

            def norm_tile(src, t0, gtab, rowbuf):
                for tc in range(4):
                    r0 = t0 + tc * 128
                    S.dma("sp", rowbuf.ap(), src[r0:r0 + 128, :], r=[("dram_" + src.tensor.name, r0, r0 + 128)], w=[(rowbuf, 0, D)])
                    ssq = (stat, tc * 4, tc * 4 + 1)
                    sd = (stat, tc * 4 + 1, tc * 4 + 2)
                    rs = (stat, tc * 4 + 2, tc * 4 + 3)
                    junk = xnjunk
                    S.act(lambda e, tc=tc: e.activation(out=junk.ap(), in_=rowbuf.ap(), func=AF.Square, accum_out=stat.ap(tc * 4, tc * 4 + 1)),
                          r=[(rowbuf, 0, D)], w=[(junk, 0, D), ssq])
                    S.act(lambda e, tc=tc: e.activation(out=stat.ap(tc * 4 + 1, tc * 4 + 2), in_=stat.ap(tc * 4, tc * 4 + 1), func=AF.Sqrt, bias=EPS, scale=1.0 / D),
                          r=[ssq], w=[sd])
                    S.dve(lambda e, tc=tc: e.reciprocal(out=stat.ap(tc * 4 + 2, tc * 4 + 3), in_=stat.ap(tc * 4 + 1, tc * 4 + 2)), r=[sd], w=[rs])
                    S.dve(lambda e, tc=tc: e.tensor_scalar(out=rowbuf.ap(), in0=rowbuf.ap(), scalar1=stat.ap(tc * 4 + 2, tc * 4 + 3), scalar2=None, op0=ALU.mult),
                          r=[(rowbuf, 0, D), rs], w=[(rowbuf, 0, D)])
                    for k4 in range(8):
                        bank = k4 % 2
                        def emit(e, k4=k4, bank=bank):
                            ins = None
                            for j in range(4):
                                k = k4 * 4 + j
                                ins = e.transpose(out=PS(bank, j * 128, (j + 1) * 128), in_=rowbuf.ap(k * 128, (k + 1) * 128), identity=identf.ap())
                            return ins
                        S.pe(emit, r=[(rowbuf, k4 * 512, (k4 + 1) * 512), (identf, 0, 128)], w=[PSr(bank)])
                        for j in range(4):
                            k = k4 * 4 + j
                            eng = S.dve if (j % 2 == 0) else S.act
                            if False:
                                S.dve(lambda e, k=k, j=j, bank=bank, tc=tc: e.tensor_scalar(out=xnT.ap(k * NT + tc * 128, k * NT + (tc + 1) * 128), in0=PS(bank, j * 128, (j + 1) * 128),
                                                                                         scalar1=gtab.ap(k, k + 1), scalar2=None, op0=ALU.mult),
                                      r=[PSr(bank, j * 128, (j + 1) * 128), (gtab, k, k + 1)], w=[(xnT, k * NT + tc * 128, k * NT + (tc + 1) * 128)])
                            else:
                                S.act(lambda e, k=k, j=j, bank=bank, tc=tc: e.activation(out=xnT.ap(k * NT + tc * 128, k * NT + (tc + 1) * 128), in_=PS(bank, j * 128, (j + 1) * 128),
                                                                                      func=AF.Copy, scale=gtab.ap(k, k + 1)),
                                      r=[PSr(bank, j * 128, (j + 1) * 128), (gtab, k, k + 1)], w=[(xnT, k * NT + tc * 128, k * NT + (tc + 1) * 128)])

            sgf = mA.alloc(F32, 512)
            xpc = [mA.alloc(F32, 512) for _ in range(2)]
            xpc_i = [0]
            xsT = mA.alloc(BF16, KC)
            oTs = mA.alloc(BF16, 36)
            mTs = mA.alloc(BF16, KC)
            hTs = mA.alloc(BF16, FKC)
            sgs = mA.alloc(F32, 8)

            def resid_phase(wv, kcn, lhs_fn, lhs_reg, banks, src, dst, t0, samp=None):
                sname, dname = "dram_" + src.tensor.name, "dram_" + dst.tensor.name
                for cb in range(D // 512):
                    tm_dense(wv, cb * 512, kcn, lhs_fn, lhs_reg, banks, samp=(None if samp is None else (samp[0], samp[1])))
                    if samp is not None:
                        lcols, sbank, ssrc, sdst = samp
                        xb_ = xpc[xpc_i[0] % 2]
                        xpc_i[0] += 1
                        S.dma("sp", xb_.ap(0, 512, 0, 1), ssrc[:, cb * 512:(cb + 1) * 512], r=[("dram_" + ssrc.tensor.name, 0, 1)], w=[(xb_, 0, 512)])
                        S.dve(lambda e, xb_=xb_, sbank=sbank: e.tensor_tensor(out=xb_.ap(0, 512, 0, 1), in0=PS(sbank, 0, 512, 0, 1), in1=xb_.ap(0, 512, 0, 1), op=ALU.add),
                              r=[PSr(sbank), (xb_, 0, 512)], w=[(xb_, 0, 512)])
                        S.dma("sp", sdst[:, cb * 512:(cb + 1) * 512], xb_.ap(0, 512, 0, 1), r=[(xb_, 0, 512)], w=[("dram_" + sdst.tensor.name, 0, 1)])
                    for tc in range(4):
                        r0 = t0 + tc * 128
                        xb_ = xpc[xpc_i[0] % 2]
                        xpc_i[0] += 1
                        S.dma("sp", xb_.ap(), src[r0:r0 + 128, cb * 512:(cb + 1) * 512], r=[(sname, r0, r0 + 128)], w=[(xb_, 0, 512)])
                        S.dve(lambda e, xb_=xb_, bk=banks[tc]: e.tensor_tensor(out=xb_.ap(), in0=PS(bk), in1=xb_.ap(), op=ALU.add), r=[PSr(banks[tc]), (xb_, 0, 512)], w=[(xb_, 0, 512)])
                        S.dma("sp", dst[r0:r0 + 128, cb * 512:(cb + 1) * 512], xb_.ap(), r=[(xb_, 0, 512)], w=[(dname, r0, r0 + 128)])


            colA = Arena(t_xnT, "xnT", BF16, KC * NT * 2)

            def row_to_cols(row, off, nchunks, emit_copy):
                for c0 in range(0, nchunks, 8):
                    cn_ = min(8, nchunks - c0)
                    def emit(e, c0=c0, cn_=cn_):
                        ins = None
                        for c in range(c0, c0 + cn_):
                            ins = e.transpose(out=PS(7, c, c + 1), in_=row.ap(off + c * 128, off + (c + 1) * 128, 0, 1), identity=identf.ap(0, 1, 0, 1))
                        return ins
                    S.pe(emit, r=[(row, off + c0 * 128, off + (c0 + cn_) * 128), (identf, 0, 128)], w=[PSr(7)])
                emit_copy()

            def srow_dense(wv, c0, ncols, kcn, lcols, bank):
                nparts = (kcn + 7) // 8
                for kp in range(nparts):
                    k0 = kp * 8
                    kn = min(8, kcn - k0)
                    sv, sreg = wload(wv[:, k0:k0 + kn, c0:c0 + ncols], kn, ncols)
                    mm_group(PS(bank, 0, ncols, 0, 1), (lambda k, k0=k0: lcols.ap(k0 + k, k0 + k + 1)), (lambda k, sv=sv: sv[:, k, :]), kn,
                             r=[sreg, (lcols, 0, kcn)], w=[PSr(bank)], first=(kp == 0), last=(kp == nparts - 1))

            def rms_row(row, n, s0, inv_n, sjunk):
                S.act(lambda e: e.activation(out=sjunk.ap(0, n, 0, 1), in_=row.ap(0, n, 0, 1), func=AF.Square, accum_out=stat.ap(s0, s0 + 1, 0, 1)),
                      r=[(row, 0, n)], w=[(sjunk, 0, n), (stat, s0, s0 + 1)])
                S.act(lambda e: e.activation(out=stat.ap(s0 + 1, s0 + 2, 0, 1), in_=stat.ap(s0, s0 + 1, 0, 1), func=AF.Sqrt, bias=EPS, scale=inv_n),
                      r=[(stat, s0, s0 + 1)], w=[(stat, s0 + 1, s0 + 2)])
                S.dve(lambda e: e.reciprocal(out=stat.ap(s0 + 2, s0 + 3, 0, 1), in_=stat.ap(s0 + 1, s0 + 2, 0, 1)), r=[(stat, s0 + 1, s0 + 2)], w=[(stat, s0 + 2, s0 + 3)])

            def sample_norm_cols(src_dram, gtab, xsT):
                srow = Buf(t_arena, "arena", BF16, F32, 0, D)
                sjunk = Buf(t_arena, "arena", BF16, F32, D * 4, D)
                S.dma("sp", srow.ap(0, D, 0, 1), src_dram, r=[("dram_" + src_dram.tensor.name, 0, 1)], w=[(srow, 0, D)])
                rms_row(srow, D, 52, 1.0 / D, sjunk)
                S.dve(lambda e: e.tensor_scalar(out=srow.ap(0, D, 0, 1), in0=srow.ap(0, D, 0, 1), scalar1=stat.ap(54, 55, 0, 1), scalar2=None, op0=ALU.mult),
                      r=[(srow, 0, D), (stat, 54, 55)], w=[(srow, 0, D)])
                row_to_cols(srow, 0, KC, lambda: S.dve(lambda e: e.tensor_tensor(out=xsT.ap(), in0=PS(7, 0, KC), in1=gtab.ap(), op=ALU.mult),
                                                       r=[PSr(7), (gtab, 0, KC)], w=[(xsT, 0, KC)]))

            def sample_resid(wv, kcn, lcols, src_dram, dst_dram):
                sname, dname = "dram_" + src_dram.tensor.name, "dram_" + dst_dram.tensor.name
                for cb in range(D // 512):
                    bank = cb % 4
                    srow_dense(wv, cb * 512, 512, kcn, lcols, bank)
                    xb_ = xpc[xpc_i[0] % 2]
                    xpc_i[0] += 1
                    S.dma("sp", xb_.ap(0, 512, 0, 1), src_dram[:, cb * 512:(cb + 1) * 512], r=[(sname, 0, 1)], w=[(xb_, 0, 512)])
                    S.dve(lambda e, xb_=xb_, bank=bank: e.tensor_tensor(out=xb_.ap(0, 512, 0, 1), in0=PS(bank, 0, 512, 0, 1), in1=xb_.ap(0, 512, 0, 1), op=ALU.add),
                          r=[PSr(bank), (xb_, 0, 512)], w=[(xb_, 0, 512)])
                    S.dma("sp", dst_dram[:, cb * 512:(cb + 1) * 512], xb_.ap(0, 512, 0, 1), r=[(xb_, 0, 512)], w=[(dname, 0, 1)])

            def sample_layer(l, src_dram):
                global_names = None
                wv_in = wview(w_in[l])
                arena.reset()
                hrow = arena.alloc(F32, N_IN)
                rbase = arena.off
                colA.reset()
                qcol = colA.alloc(F32, 16)
                acol = colA.alloc(F32, 8)
                zc = colA.alloc(F32, 1)
                pcol = colA.alloc(F32, 16)
                wa2f = colA.alloc(F32, 1024)
                sTt = colA.alloc(F32, 400)
                ocs = colA.alloc(F32, 512)
                sample_norm_cols(src_dram, gmix, xsT)
                for cb in range(0, N_IN, 512):
                    ncols = min(512, N_IN - cb)
                    bank = (cb // 512) % 4
                    srow_dense(wv_in, cb, ncols, KC, xsT, bank)
                    S.act(lambda e, cb=cb, ncols=ncols, bank=bank: e.activation(out=hrow.ap(cb, cb + ncols, 0, 1), in_=PS(bank, 0, ncols, 0, 1), func=AF.Copy),
                          r=[PSr(bank)], w=[(hrow, cb, cb + ncols)])
                arena.reset(rbase)
                arow = arena.alloc(F32, 1024)
                brow = arena.alloc(F32, 1024)
                t512 = arena.alloc(F32, 512)
                u512 = arena.alloc(F32, 512)
                sj = arena.alloc(F32, 512)
                S.dma("sp", wa2f.ap(0, 1024, 0, 16), w_alpha2[l], w=[(wa2f, 0, 1024)])
                S.dma("sp", brow.ap(0, 1024, 0, 1), b_alpha_r[l:l + 1, :], w=[(brow, 0, 1024)])
                S.dma("sp", SA.ap().rearrange("p (c v) -> p c v", v=512), sgla[l].rearrange("h (c p) v -> p (h c) v", p=128), w=[(SA, 0, 4096)])
                S.pe(lambda e: e.transpose(out=PS(7, 0, 1, 0, 16), in_=hrow.ap(O_ZA, O_ZA + 16, 0, 1), identity=identf.ap(0, 1, 0, 1)), r=[(hrow, O_ZA, O_ZA + 16), (identf, 0, 128)], w=[PSr(7)])
                S.dve(lambda e: e.tensor_copy(out=zc.ap(0, 1, 0, 16), in_=PS(7, 0, 1, 0, 16)), r=[PSr(7)], w=[(zc, 0, 1)])
                for hf in range(2):
                    mm_group(PS(hf, 0, 512, 0, 1), lambda k: zc.ap(0, 1, 0, 16), lambda k, hf=hf: wa2f.ap(hf * 512, (hf + 1) * 512, 0, 16), 1,
                             r=[(zc, 0, 1), (wa2f, hf * 512, (hf + 1) * 512)], w=[PSr(hf)])
                    sl = (hf * 512, (hf + 1) * 512)
                    S.dve(lambda e, hf=hf, sl=sl: e.tensor_tensor(out=arow.ap(sl[0], sl[1], 0, 1), in0=PS(hf, 0, 512, 0, 1), in1=brow.ap(sl[0], sl[1], 0, 1), op=ALU.add),
                          r=[PSr(hf), (brow, sl[0], sl[1])], w=[(arow, sl[0], sl[1])])
                S.act(lambda e: e.activation(out=arow.ap(0, 1024, 0, 1), in_=arow.ap(0, 1024, 0, 1), func=AF.Exp, scale=-1.0), r=[(arow, 0, 1024)], w=[(arow, 0, 1024)])
                S.act(lambda e: e.activation(out=arow.ap(0, 1024, 0, 1), in_=arow.ap(0, 1024, 0, 1), func=AF.Ln, bias=1.0), r=[(arow, 0, 1024)], w=[(arow, 0, 1024)])
                S.act(lambda e: e.activation(out=arow.ap(0, 1024, 0, 1), in_=arow.ap(0, 1024, 0, 1), func=AF.Exp, scale=-1.0 / 16), r=[(arow, 0, 1024)], w=[(arow, 0, 1024)])
                row_to_cols(arow, 0, 8, lambda: S.dve(lambda e: e.tensor_copy(out=acol.ap(), in_=PS(7, 0, 8)), r=[PSr(7)], w=[(acol, 0, 8)]))
                row_to_cols(hrow, O_QA, 8, lambda: S.dve(lambda e: e.tensor_scalar(out=qcol.ap(0, 8), in0=PS(7, 0, 8), scalar1=DK_A ** -0.5, scalar2=None, op0=ALU.mult),
                                                          r=[PSr(7)], w=[(qcol, 0, 8)]))
                for h in range(H_A):
                    for dkc in range(2):
                        c8 = h * 2 + dkc
                        sl = (c8 * 512, (c8 + 1) * 512)
                        ko = O_KA + c8 * 128
                        vo = O_VA + h * 512
                        mm_group(PS(dkc), lambda k, ko=ko: hrow.ap(ko, ko + 128, 0, 1), lambda k, vo=vo: hrow.ap(vo, vo + 512, 0, 1), 1,
                                 r=[(hrow, ko, ko + 128), (hrow, vo, vo + 512)], w=[PSr(dkc)])
                        S.dve(lambda e, sl=sl, c8=c8, dkc=dkc: e.scalar_tensor_tensor(out=SA.ap(*sl), in0=SA.ap(*sl), scalar=acol.ap(c8, c8 + 1), in1=PS(dkc), op0=ALU.mult, op1=ALU.add),
                              r=[(SA, sl[0], sl[1]), (acol, c8, c8 + 1), PSr(dkc)], w=[(SA, sl[0], sl[1])])
                    def emit_os(e, h=h):
                        e.matmul(PS(2, 0, 512, 0, 1), lhsT=qcol.ap(h * 2, h * 2 + 1), rhs=SA.ap(h * 1024, h * 1024 + 512), start=True, stop=False)
                        return e.matmul(PS(2, 0, 512, 0, 1), lhsT=qcol.ap(h * 2 + 1, h * 2 + 2), rhs=SA.ap(h * 1024 + 512, h * 1024 + 1024), start=False, stop=True)
                    S.pe(emit_os, r=[(qcol, 0, 8), (SA, h * 1024, (h + 1) * 1024)], w=[PSr(2)])
                    S.act(lambda e: e.activation(out=t512.ap(0, 512, 0, 1), in_=PS(2, 0, 512, 0, 1), func=AF.Copy), r=[PSr(2)], w=[(t512, 0, 512)])
                    rms_row(t512, 512, 56, 1.0 / DV_A, sj)
                    ro = O_RA + h * 512
                    S.act(lambda e, ro=ro: e.activation(out=u512.ap(0, 512, 0, 1), in_=hrow.ap(ro, ro + 512, 0, 1), func=AF.Silu), r=[(hrow, ro, ro + 512)], w=[(u512, 0, 512)])
                    S.dve(lambda e: e.tensor_tensor(out=u512.ap(0, 512, 0, 1), in0=u512.ap(0, 512, 0, 1), in1=ggla.ap(0, 512, 0, 1), op=ALU.mult), r=[(u512, 0, 512), (ggla, 0, 512)], w=[(u512, 0, 512)])
                    S.dve(lambda e: e.scalar_tensor_tensor(out=t512.ap(0, 512, 0, 1), in0=t512.ap(0, 512, 0, 1), scalar=stat.ap(58, 59, 0, 1), in1=u512.ap(0, 512, 0, 1), op0=ALU.mult, op1=ALU.mult),
                          r=[(t512, 0, 512), (stat, 58, 59), (u512, 0, 512)], w=[(t512, 0, 512)])
                    row_to_cols(t512, 0, 4, lambda h=h: S.dve(lambda e, h=h: e.tensor_copy(out=oTs.ap(h * 4, h * 4 + 4), in_=PS(7, 0, 4)), r=[PSr(7)], w=[(oTs, h * 4, h * 4 + 4)]))
                S.dma("sp", gla_s[l].rearrange("h (c p) v -> p (h c) v", p=128), SA.ap().rearrange("p (c v) -> p c v", v=512), r=[(SA, 0, 4096)], w=[("dram_gla_s", l, l + 1)])
                arena.reset(rbase)
                qk = arena.alloc(F32, 2048)
                tq = arena.alloc(F32, 1024)
                t256 = arena.alloc(F32, 256)
                u256 = arena.alloc(F32, 256)
                sj = arena.alloc(F32, 512)
                csr = arena.alloc(F32, 128)
                S.dma("sp", csr.ap(0, 128, 0, 1), cn["csrow"], w=[(csr, 0, 128)])
                S.dma("sp", SB_.ap().rearrange("p (h v) -> p h v", v=256), sret[l].rearrange("h p v -> p h v"), w=[(SB_, 0, 2048)])
                cosr = csr.ap(0, 64, 0, 1).unsqueeze(1).to_broadcast([1, 8, 64])
                sinr = csr.ap(64, 128, 0, 1).unsqueeze(1).to_broadcast([1, 8, 64])
                for wi, (ho, scl) in enumerate(((O_QB, 1.0), (O_KB, DK_B ** -0.5))):
                    x3 = hrow.ap(ho, ho + 1024, 0, 1).rearrange("p (h two n) -> p h two n", two=2, n=64)
                    o3 = qk.ap(wi * 1024, (wi + 1) * 1024, 0, 1).rearrange("p (h two n) -> p h two n", two=2, n=64)
                    t3 = tq.ap(0, 1024, 0, 1).rearrange("p (h two n) -> p h two n", two=2, n=64)
                    rr_ = [(hrow, ho, ho + 1024), (csr, 0, 128)]
                    S.dve(lambda e, x3=x3, o3=o3: e.tensor_tensor(out=o3[:, :, 0, :], in0=x3[:, :, 0, :], in1=cosr, op=ALU.mult), r=rr_, w=[(qk, wi * 1024, (wi + 1) * 1024)])
                    S.dve(lambda e, x3=x3, t3=t3: e.tensor_tensor(out=t3[:, :, 0, :], in0=x3[:, :, 1, :], in1=sinr, op=ALU.mult), r=rr_, w=[(tq, 0, 1024)])
                    S.dve(lambda e, o3=o3, t3=t3: e.tensor_tensor(out=o3[:, :, 0, :], in0=o3[:, :, 0, :], in1=t3[:, :, 0, :], op=ALU.subtract),
                          r=[(qk, wi * 1024, (wi + 1) * 1024), (tq, 0, 1024)], w=[(qk, wi * 1024, (wi + 1) * 1024)])
                    S.dve(lambda e, x3=x3, o3=o3: e.tensor_tensor(out=o3[:, :, 1, :], in0=x3[:, :, 0, :], in1=sinr, op=ALU.mult), r=rr_, w=[(qk, wi * 1024, (wi + 1) * 1024)])
                    S.dve(lambda e, x3=x3, t3=t3: e.tensor_tensor(out=t3[:, :, 1, :], in0=x3[:, :, 1, :], in1=cosr, op=ALU.mult), r=rr_, w=[(tq, 0, 1024)])
                    S.dve(lambda e, o3=o3, t3=t3: e.tensor_tensor(out=o3[:, :, 1, :], in0=o3[:, :, 1, :], in1=t3[:, :, 1, :], op=ALU.add),
                          r=[(qk, wi * 1024, (wi + 1) * 1024), (tq, 0, 1024)], w=[(qk, wi * 1024, (wi + 1) * 1024)])
                    if scl != 1.0:
                        S.dve(lambda e, wi=wi, scl=scl: e.tensor_scalar(out=qk.ap(wi * 1024, (wi + 1) * 1024, 0, 1), in0=qk.ap(wi * 1024, (wi + 1) * 1024, 0, 1), scalar1=scl, scalar2=None, op0=ALU.mult),
                              r=[(qk, wi * 1024, (wi + 1) * 1024)], w=[(qk, wi * 1024, (wi + 1) * 1024)])
                row_to_cols(qk, 0, 8, lambda: S.dve(lambda e: e.tensor_copy(out=qcol.ap(8, 16), in_=PS(7, 0, 8)), r=[PSr(7)], w=[(qcol, 8, 16)]))
                for h in range(H_B):
                    ssl = (h * 256, (h + 1) * 256)
                    ko = 1024 + h * 128
                    vo = O_VB + h * 256
                    mm_group(PS(0, 0, 256), lambda k, ko=ko: qk.ap(ko, ko + 128, 0, 1), lambda k, vo=vo: hrow.ap(vo, vo + 256, 0, 1), 1,
                             r=[(qk, ko, ko + 128), (hrow, vo, vo + 256)], w=[PSr(0)])
                    S.dve(lambda e, ssl=ssl, h=h: e.scalar_tensor_tensor(out=SB_.ap(*ssl), in0=SB_.ap(*ssl), scalar=gam.ap(8 + h, 9 + h), in1=PS(0, 0, 256), op0=ALU.mult, op1=ALU.add),
                          r=[(SB_, ssl[0], ssl[1]), (gam, 8 + h, 9 + h), PSr(0)], w=[(SB_, ssl[0], ssl[1])])
                    mm_group(PS(2, 0, 256, 0, 1), lambda k, h=h: qcol.ap(8 + h, 9 + h), lambda k, ssl=ssl: SB_.ap(*ssl), 1, r=[(qcol, 8, 16), (SB_, ssl[0], ssl[1])], w=[PSr(2)])
                    S.act(lambda e: e.activation(out=t256.ap(0, 256, 0, 1), in_=PS(2, 0, 256, 0, 1), func=AF.Copy), r=[PSr(2)], w=[(t256, 0, 256)])
                    rms_row(t256, 256, 56, 1.0 / DV_B, sj)
                    go = O_GB + h * 256
                    S.act(lambda e, go=go: e.activation(out=u256.ap(0, 256, 0, 1), in_=hrow.ap(go, go + 256, 0, 1), func=AF.Silu), r=[(hrow, go, go + 256)], w=[(u256, 0, 256)])
                    S.dve(lambda e: e.scalar_tensor_tensor(out=t256.ap(0, 256, 0, 1), in0=t256.ap(0, 256, 0, 1), scalar=stat.ap(58, 59, 0, 1), in1=u256.ap(0, 256, 0, 1), op0=ALU.mult, op1=ALU.mult),
                          r=[(t256, 0, 256), (stat, 58, 59), (u256, 0, 256)], w=[(t256, 0, 256)])
                    row_to_cols(t256, 0, 2, lambda h=h: S.dve(lambda e, h=h: e.tensor_copy(out=oTs.ap(16 + h * 2, 18 + h * 2), in_=PS(7, 0, 2)), r=[PSr(7)], w=[(oTs, 16 + h * 2, 18 + h * 2)]))
                S.dma("sp", ret_s[l].rearrange("h p v -> p h v"), SB_.ap().rearrange("p (h v) -> p h v", v=256), r=[(SB_, 0, 2048)], w=[("dram_ret_s", l, l + 1)])
                arena.reset(rbase)
                Kc = arena.alloc(F32, 512)
                Vc = [arena.alloc(F32, 512) for _ in range(3)]
                prod = arena.alloc(F32, 512)
                sc12 = arena.alloc(F32, 16)
                s0row = arena.alloc(F32, 16)
                sbt = arena.alloc(F32, 12)
                S.dma("sp", sbt.ap(), cn["sbias"], w=[(sbt, 0, 12)])
                for g in range(3):
                    W, d = WINDOWS[g], DILS[g]
                    qo, ko, vo = O_HC + g * 1536, O_HC + g * 1536 + 512, O_HC + g * 1536 + 1024
                    S.dma("sp", Kc.ap(), cw[g][l, 0].rearrange("(j d) c -> j d c", d=d)[:, 0, :], w=[(Kc, 0, 512)])
                    S.dma("sp", Vc[g].ap(), cw[g][l, 1].rearrange("(j d) c -> j d c", d=d)[:, 0, :], w=[(Vc[g], 0, 512)])
                    for which, oo in ((0, ko), (1, vo)):
                        S.dma("sp", ws[g][l, which, 0:W - 1, :], cw[g][l, which, 1:W, :], w=[("dram_ws%d" % g, (l * 2 + which) * 4096, (l * 2 + which) * 4096 + W - 1)])
                        S.dma("sp", ws[g][l, which, W - 1:W, :], hrow.ap(oo, oo + 512, 0, 1), r=[(hrow, oo, oo + 512)], w=[("dram_ws%d" % g, (l * 2 + which) * 4096 + W - 1, (l * 2 + which) * 4096 + W)])
                    mm_group(PS(0), lambda k: onesf.ap(0, 128, 0, 1), lambda k, qo=qo: hrow.ap(qo, qo + 512, 0, 1), 1, r=[(onesf, 0, 128), (hrow, qo, qo + 512)], w=[PSr(0)])
                    S.dve(lambda e: e.tensor_tensor(out=prod.ap(), in0=Kc.ap(), in1=PS(0), op=ALU.mult), r=[(Kc, 0, 512), PSr(0)], w=[(prod, 0, 512)])
                    S.dve(lambda e, g=g: e.tensor_reduce(out=sc12.ap(g * 4, g * 4 + 4), in_=prod.ap().rearrange("p (i d) -> p i d", d=128), axis=AX.X, op=ALU.add),
                          r=[(prod, 0, 512)], w=[(sc12, g * 4, g * 4 + 4)])
                    S.dve(lambda e, g=g: e.scalar_tensor_tensor(out=sc12.ap(g * 4, g * 4 + 4), in0=sc12.ap(g * 4, g * 4 + 4), scalar=128 ** -0.5, in1=sbt.ap(g * 4, g * 4 + 4), op0=ALU.mult, op1=ALU.add),
                          r=[(sc12, g * 4, g * 4 + 4), (sbt, g * 4, g * 4 + 4)], w=[(sc12, g * 4, g * 4 + 4)])
                    S.dve(lambda e, qo=qo, ko=ko: e.tensor_tensor(out=prod.ap(0, 512, 0, 1), in0=hrow.ap(qo, qo + 512, 0, 1), in1=hrow.ap(ko, ko + 512, 0, 1), op=ALU.mult),
                          r=[(hrow, qo, qo + 512), (hrow, ko, ko + 512), (prod, 0, 512)], w=[(prod, 0, 512)])
                    S.dve(lambda e, g=g: e.tensor_reduce(out=s0row.ap(g * 4, g * 4 + 4, 0, 1), in_=prod.ap(0, 512, 0, 1).rearrange("p (i d) -> p i d", d=128), axis=AX.X, op=ALU.add),
                          r=[(prod, 0, 512)], w=[(s0row, g * 4, g * 4 + 4)])
                    S.pe(lambda e, g=g: e.transpose(out=PS(3, g * 128, (g + 1) * 128, 0, 4), in_=sc12.ap(g * 4, g * 4 + 4), identity=identf.ap()),
                         r=[(sc12, g * 4, g * 4 + 4), (identf, 0, 128)], w=[PSr(3)])
                    S.pe(lambda e, g=g: e.transpose(out=PS(3, 384 + g, 385 + g, 0, 4), in_=s0row.ap(g * 4, g * 4 + 4, 0, 1), identity=identf.ap(0, 1, 0, 1)),
                         r=[(s0row, g * 4, g * 4 + 4), (identf, 0, 128)], w=[PSr(3)])
                S.dve(lambda e: e.tensor_copy(out=sTt.ap(0, 387, 0, 4), in_=PS(3, 0, 387, 0, 4)), r=[PSr(3)], w=[(sTt, 0, 387)])
                S.dve(lambda e: e.tensor_scalar(out=sTt.ap(384, 387, 0, 4), in0=sTt.ap(384, 387, 0, 4), scalar1=128 ** -0.5, scalar2=None, op0=ALU.mult), r=[(sTt, 384, 387)], w=[(sTt, 384, 387)])
                S.dve(lambda e: e.tensor_reduce(out=stat.ap(60, 61, 0, 4), in_=sTt.ap(0, 387, 0, 4), axis=AX.X, op=ALU.max, negate=True), r=[(sTt, 0, 387)], w=[(stat, 60, 61)])
                S.act(lambda e: e.activation(out=sTt.ap(0, 387, 0, 4), in_=sTt.ap(0, 387, 0, 4), func=AF.Exp, bias=stat.ap(60, 61, 0, 4), scale=1.0, accum_out=stat.ap(61, 62, 0, 4)),
                      r=[(sTt, 0, 387), (stat, 60, 61)], w=[(sTt, 0, 387), (stat, 61, 62)])
                S.dve(lambda e: e.reciprocal(out=stat.ap(62, 63, 0, 4), in_=stat.ap(61, 62, 0, 4)), r=[(stat, 61, 62)], w=[(stat, 62, 63)])
                for g in range(3):
                    S.pe(lambda e, g=g: e.transpose(out=PS(4, g * 4, g * 4 + 4), in_=sTt.ap(g * 128, (g + 1) * 128, 0, 4), identity=identf.ap(0, 4, 0, 4)),
                         r=[(sTt, g * 128, (g + 1) * 128), (identf, 0, 128)], w=[PSr(4)])
                    S.pe(lambda e, g=g: e.transpose(out=PS(4, 16 + g * 4, 20 + g * 4, 0, 1), in_=sTt.ap(384 + g, 385 + g, 0, 4), identity=identf.ap(0, 4, 0, 4)),
                         r=[(sTt, 384 + g, 385 + g), (identf, 0, 128)], w=[PSr(4)])
                S.dve(lambda e: e.tensor_copy(out=pcol.ap(0, 12), in_=PS(4, 0, 12)), r=[PSr(4)], w=[(pcol, 0, 12)])
                S.dve(lambda e: e.tensor_copy(out=s0row.ap(0, 12, 0, 1), in_=PS(4, 16, 28, 0, 1)), r=[PSr(4)], w=[(s0row, 0, 12)])
                def emit_pvs(e):
                    ins = None
                    for g in range(3):
                        vo = O_HC + g * 1536 + 1024
                        e.matmul(PS(5, 0, 512, 0, 4), lhsT=pcol.ap(g * 4, g * 4 + 4), rhs=Vc[g].ap(), start=(g == 0), stop=False)
                        ins = e.matmul(PS(5, 0, 512, 0, 4), lhsT=s0row.ap(g * 4, g * 4 + 4, 0, 1), rhs=hrow.ap(vo, vo + 512, 0, 1), start=False, stop=(g == 2))
                    return ins
                S.pe(emit_pvs, r=[(pcol, 0, 12), (s0row, 0, 12), (Vc[0], 0, 512), (Vc[1], 0, 512), (Vc[2], 0, 512), (hrow, O_HC, N_IN)], w=[PSr(5)])
                S.act(lambda e: e.activation(out=ocs.ap(0, 512, 0, 4), in_=PS(5, 0, 512, 0, 4), func=AF.Copy, scale=stat.ap(62, 63, 0, 4)), r=[PSr(5), (stat, 62, 63)], w=[(ocs, 0, 512)])
                for i in range(4):
                    S.pe(lambda e, i=i: e.transpose(out=PS(6, i * 4, i * 4 + 4), in_=ocs.ap(i * 128, (i + 1) * 128, 0, 4), identity=identf.ap(0, 4, 0, 4)),
                         r=[(ocs, i * 128, (i + 1) * 128), (identf, 0, 128)], w=[PSr(6)])
                    S.dve(lambda e, i=i: e.tensor_copy(out=oTs.ap(32 + i, 33 + i), in_=PS(6, i * 4 + i, i * 4 + i + 1)), r=[PSr(6)], w=[(oTs, 32 + i, 33 + i)])

            for l in range(n_layers):
                x_src = xp if l == 0 else xb
                wv_in = wview(w_in[l])
                cload(gmix, g_mix[l])
                cload(gffn, g_ffn[l])
                cload(ggla, g_gla[l].partition_broadcast(128))
                cload(negb, b_alpha[l])
                S.dve(lambda e: e.tensor_scalar(out=negb.ap(), in0=negb.ap(), scalar1=-1.0, scalar2=None, op0=ALU.mult), r=[(negb, 0, 8)], w=[(negb, 0, 8)])
                S.dma("pool", wa2.ap(0, 1024, 0, 16), w_alpha2[l], w=[(wa2, 0, 1024)], slot=0)
                ck('params')
                wmode[0] = "stream"
                if not skip_sample:
                    sample_layer(l, xs if l == 0 else xsb)
                ck('sample')
                S.dve(lambda e: e.memset(SA.ap(), 0.0), w=[(SA, 0, 4096)])
                S.dve(lambda e: e.memset(SB_.ap(), 0.0), w=[(SB_, 0, 2048)])

                for ti in range(n_tiles):
                    t0 = ti * NT
                    wmode[0] = "fill" if (ti == 0 and n_tiles > 1) else ("cached" if ti > 0 else "stream")
                    wseq[0] = 0
                    do_s = (ti == 0) and not skip_sample
                    arena.reset()
                    rowbuf = arena.alloc(F32, D)
                    xnjunk = arena.alloc(BF16, D)
                    norm_tile(x_src, t0, gmix, rowbuf)
                    ck('norm')
                    cload(cosb, cn["cost"][:, t0:t0 + NT])
                    cload(sinb, cn["sint"][:, t0:t0 + NT])

                    arena.reset()
                    oT = arena.alloc(BF16, 36 * NT)
                    mT = arena.alloc(BF16, 32 * NT)
                    tmp_base = arena.off - 32 * NT * 2
                    arena.reset(tmp_base)
                    zT = arena.alloc(BF16, 512)
                    qt = arena.alloc(BF16, 1024)
                    kt = arena.alloc(BF16, 1024)
                    Ep = arena.alloc(F32, 1024)
                    Em = arena.alloc(F32, 512)
                    tmp1 = arena.alloc(F32, 512)
                    tmp2 = arena.alloc(F32, 512)
                    cum = arena.alloc(F32, 512)
                    junkf = arena.alloc(F32, 512)
                    vtok = arena.alloc(BF16, 2048)
                    gs = arena.alloc(BF16, 2048)
                    Pm = arena.alloc(BF16, 128)
                    ktok = arena.alloc(BF16, 256)
                    otok = arena.alloc(BF16, 512)
                    Sbf = arena.alloc(BF16, 1024)
                    fm_dense(wv_in, O_ZA, 16, KC, xn_fn, xn_reg, 0)
                    S.act(lambda e: e.activation(out=zT.ap(0, 512, 0, 16), in_=PS(0, 0, 512, 0, 16), func=AF.Copy), r=[PSr(0)], w=[(zT, 0, 512)])
                    for h in range(H_A):
                        for dkc in range(2):
                            c8 = h * 2 + dkc
                            seg = (dkc * 512, (dkc + 1) * 512)
                            mm_group(PS(1), lambda k, c8=c8: wa2.ap(c8 * 128, (c8 + 1) * 128, 0, 16), lambda k: zT.ap(0, 512, 0, 16), 1,
                                     r=[(wa2, c8 * 128, (c8 + 1) * 128), (zT, 0, 512)], w=[PSr(1)])
                            S.act(lambda e, c8=c8: e.activation(out=tmp1.ap(), in_=PS(1), func=AF.Exp, bias=negb.ap(c8, c8 + 1), scale=-1.0),
                                  r=[PSr(1), (negb, c8, c8 + 1)], w=[(tmp1, 0, 512)])
                            S.act(lambda e: e.activation(out=tmp2.ap(), in_=tmp1.ap(), func=AF.Ln, bias=1.0, scale=1.0), r=[(tmp1, 0, 512)], w=[(tmp2, 0, 512)])
                            for tc in range(4):
                                S.dve(lambda e, tc=tc: e.tensor_tensor_scan(out=cum.ap(tc * 128, (tc + 1) * 128), data0=onesf.ap(), data1=tmp2.ap(tc * 128, (tc + 1) * 128),
                                                                            initial=0.0, op0=ALU.mult, op1=ALU.add),
                                      r=[(tmp2, tc * 128, (tc + 1) * 128), (onesf, 0, 128)], w=[(cum, tc * 128, (tc + 1) * 128)])
                            S.act(lambda e, seg=seg: e.activation(out=Ep.ap(*seg), in_=cum.ap(), func=AF.Exp, scale=-1.0 / 16), r=[(cum, 0, 512)], w=[(Ep, seg[0], seg[1])])
                            S.act(lambda e: e.activation(out=Em.ap(), in_=cum.ap(), func=AF.Exp, scale=1.0 / 16), r=[(cum, 0, 512)], w=[(Em, 0, 512)])
                            fm_dense(wv_in, O_QA + c8 * 128, 128, KC, xn_fn, xn_reg, 2)
                            S.dve(lambda e, seg=seg: e.scalar_tensor_tensor(out=qt.ap(*seg), in0=PS(2), scalar=DK_A ** -0.5, in1=Ep.ap(*seg), op0=ALU.mult, op1=ALU.mult),
                                  r=[PSr(2), (Ep, seg[0], seg[1])], w=[(qt, seg[0], seg[1])])
                            fm_dense(wv_in, O_KA + c8 * 128, 128, KC, xn_fn, xn_reg, 3)
                            S.dve(lambda e, seg=seg: e.tensor_tensor(out=kt.ap(*seg), in0=PS(3), in1=Em.ap(), op=ALU.mult),
                                  r=[PSr(3), (Em, 0, 512)], w=[(kt, seg[0], seg[1])])
                        tm_dense(wv_in, O_VA + h * 512, KC, xn_lhs, xn_reg, [0, 1, 2, 3])
                        for tc in range(4):
                            S.act(lambda e, tc=tc: e.activation(out=vtok.ap(tc * 512, (tc + 1) * 512), in_=PS(tc), func=AF.Copy), r=[PSr(tc)], w=[(vtok, tc * 512, (tc + 1) * 512)])
                        tm_dense(wv_in, O_RA + h * 512, KC, xn_lhs, xn_reg, [0, 1, 2, 3])
                        for tc in range(4):
                            S.act(lambda e, tc=tc: e.activation(out=junkf.ap(), in_=PS(tc), func=AF.Silu), r=[PSr(tc)], w=[(junkf, 0, 512)])
                            S.dve(lambda e, tc=tc: e.tensor_tensor(out=gs.ap(tc * 512, (tc + 1) * 512), in0=junkf.ap(), in1=ggla.ap(), op=ALU.mult),
                                  r=[(junkf, 0, 512), (ggla, 0, 512)], w=[(gs, tc * 512, (tc + 1) * 512)])
                        S.act(lambda e, h=h: e.activation(out=Sbf.ap(), in_=SA.ap(h * 1024, (h + 1) * 1024), func=AF.Copy), r=[(SA, h * 1024, (h + 1) * 1024)], w=[(Sbf, 0, 1024)])
                        for tc in range(4):
                            def ts(dkc, tc=tc):
                                return (dkc * 512 + tc * 128, dkc * 512 + (tc + 1) * 128)
                            mm_group(PS(4, 0, 128), lambda k, tc=tc: kt.ap(*ts(k, tc)), lambda k, tc=tc: qt.ap(*ts(k, tc)), 2,
                                     r=[(kt, 0, 1024), (qt, 0, 1024)], w=[PSr(4, 0, 128)])
                            S.dve(lambda e: e.tensor_tensor(out=Pm.ap(), in0=PS(4, 0, 128), in1=mask01.ap(), op=ALU.mult), r=[PSr(4, 0, 128), (mask01, 0, 128)], w=[(Pm, 0, 128)])
                            def emit_kt(e, tc=tc):
                                ins = None
                                for dkc in range(2):
                                    ins = e.transpose(out=PS(7, 0, 256).bitcast(BF16)[:, dkc * 128:(dkc + 1) * 128], in_=kt.ap(*ts(dkc, tc)), identity=identb.ap())
                                return ins
                            S.pe(emit_kt, r=[(kt, 0, 1024), (identb, 0, 128)], w=[PSr(7, 0, 256)])
                            S.act(lambda e: e.activation(out=ktok.ap(), in_=PS(7, 0, 256).bitcast(BF16)[:, 0:256], func=AF.Copy), r=[PSr(7, 0, 256)], w=[(ktok, 0, 256)])
                            def emit_o(e, tc=tc):
                                e.matmul(PS(5), lhsT=Pm.ap(), rhs=vtok.ap(tc * 512, (tc + 1) * 512), start=True, stop=False)
                                e.matmul(PS(5), lhsT=qt.ap(*ts(0, tc)), rhs=Sbf.ap(0, 512), start=False, stop=False)
                                return e.matmul(PS(5), lhsT=qt.ap(*ts(1, tc)), rhs=Sbf.ap(512, 1024), start=False, stop=True)
                            S.pe(emit_o, r=[(Pm, 0, 128), (vtok, tc * 512, (tc + 1) * 512), (qt, 0, 1024), (Sbf, 0, 1024)], w=[PSr(5)])
                            st0 = 16 + tc * 4
                            S.act(lambda e, st0=st0: e.activation(out=junkf.ap(), in_=PS(5), func=AF.Square, accum_out=stat.ap(st0, st0 + 1)),
                                  r=[PSr(5)], w=[(junkf, 0, 512), (stat, st0, st0 + 1)])
                            S.act(lambda e, st0=st0: e.activation(out=stat.ap(st0 + 1, st0 + 2), in_=stat.ap(st0, st0 + 1), func=AF.Sqrt, bias=EPS, scale=1.0 / DV_A),
                                  r=[(stat, st0, st0 + 1)], w=[(stat, st0 + 1, st0 + 2)])
                            S.dve(lambda e, st0=st0: e.reciprocal(out=stat.ap(st0 + 2, st0 + 3), in_=stat.ap(st0 + 1, st0 + 2)), r=[(stat, st0 + 1, st0 + 2)], w=[(stat, st0 + 2, st0 + 3)])
                            S.dve(lambda e, st0=st0, tc=tc: e.scalar_tensor_tensor(out=otok.ap(), in0=PS(5), scalar=stat.ap(st0 + 2, st0 + 3), in1=gs.ap(tc * 512, (tc + 1) * 512),
                                                                                  op0=ALU.mult, op1=ALU.mult),
                                  r=[PSr(5), (stat, st0 + 2, st0 + 3), (gs, tc * 512, (tc + 1) * 512)], w=[(otok, 0, 512)])
                            def emit_ot(e):
                                ins = None
                                for j in range(4):
                                    ins = e.transpose(out=PS(7, 256, 512).bitcast(BF16)[:, j * 128:(j + 1) * 128], in_=otok.ap(j * 128, (j + 1) * 128), identity=identb.ap())
                                return ins
                            S.pe(emit_ot, r=[(otok, 0, 512), (identb, 0, 128)], w=[PSr(7, 256, 512)])
                            oc0 = h * 4
                            S.act(lambda e, oc0=oc0, tc=tc: e.activation(out=oT.ap(oc0 * NT, (oc0 + 4) * NT).rearrange("p (j t) -> p j t", t=NT)[:, :, tc * 128:(tc + 1) * 128],
                                                                        in_=PS(7, 256, 512).bitcast(BF16).rearrange("p (j t) -> p j t", t=128), func=AF.Copy),
                                  r=[PSr(7, 256, 512)], w=[(oT, oc0 * NT, (oc0 + 4) * NT)])
                            for dkc in range(2):
                                c8 = h * 2 + dkc
                                ecol = dkc * 512 + tc * 128 + 127
                                mm_group(PS(6), lambda k, dkc=dkc: ktok.ap(dkc * 128, (dkc + 1) * 128), lambda k, tc=tc: vtok.ap(tc * 512, (tc + 1) * 512), 1,
                                         r=[(ktok, 0, 256), (vtok, tc * 512, (tc + 1) * 512)], w=[PSr(6)])
                                sl = (c8 * 512, (c8 + 1) * 512)
                                S.dve(lambda e, sl=sl, ecol=ecol: e.tensor_scalar(out=SA.ap(*sl), in0=SA.ap(*sl), scalar1=Ep.ap(ecol, ecol + 1), scalar2=None, op0=ALU.mult),
                                      r=[(SA, sl[0], sl[1]), (Ep, ecol, ecol + 1)], w=[(SA, sl[0], sl[1])])
                                S.dve(lambda e, sl=sl, ecol=ecol: e.scalar_tensor_tensor(out=SA.ap(*sl), in0=PS(6), scalar=Ep.ap(ecol, ecol + 1), in1=SA.ap(*sl), op0=ALU.mult, op1=ALU.add),
                                      r=[PSr(6), (SA, sl[0], sl[1]), (Ep, ecol, ecol + 1)], w=[(SA, sl[0], sl[1])])
                                S.act(lambda e, sl=sl, dkc=dkc: e.activation(out=Sbf.ap(dkc * 512, (dkc + 1) * 512), in_=SA.ap(*sl), func=AF.Copy),
                                      r=[(SA, sl[0], sl[1])], w=[(Sbf, dkc * 512, (dkc + 1) * 512)])
                    if ti == n_tiles - 1:
                        S.dma("sp", gla_p[l].rearrange("h (c p) v -> p (h c) v", p=128), SA.ap().rearrange("p (c v) -> p c v", v=512), r=[(SA, 0, 4096)], w=[("dram_gla_p", l, l + 1)])


                    ck('mixA')
                    arena.reset(tmp_base)
                    qf = arena.alloc(F32, 512)
                    t1 = arena.alloc(F32, 512)
                    t2 = arena.alloc(F32, 512)
                    junkfB = arena.alloc(F32, 512)
                    qtb = arena.alloc(BF16, 1024)
                    ktb = arena.alloc(BF16, 1024)
                    vtokb = arena.alloc(BF16, 2048)
                    gsb = arena.alloc(BF16, 2048)
                    PmB = arena.alloc(BF16, 128)
                    ktokb = arena.alloc(BF16, 128)
                    otokb = arena.alloc(BF16, 256)
                    Sbfb = arena.alloc(BF16, 512)
                    for hp in range(4):
                        for hh in range(2):
                            h = hp * 2 + hh
                            for (coff, dst, etab) in ((O_QB, qtb, rEp), (O_KB, ktb, rEm)):
                                fm_dense(wv_in, coff + h * 128, 128, KC, xn_fn, xn_reg, 2)
                                S.act(lambda e: e.activation(out=qf.ap(), in_=PS(2), func=AF.Copy), r=[PSr(2)], w=[(qf, 0, 512)])
                                mm_group(PS(3), lambda k: permf.ap(), lambda k: qf.ap(), 1, r=[(permf, 0, 128), (qf, 0, 512)], w=[PSr(3)])
                                S.dve(lambda e: e.tensor_tensor(out=t1.ap(), in0=qf.ap(), in1=cosb.ap(), op=ALU.mult), r=[(qf, 0, 512), (cosb, 0, 512)], w=[(t1, 0, 512)])
                                S.dve(lambda e: e.tensor_tensor(out=t2.ap(), in0=PS(3), in1=sinb.ap(), op=ALU.mult), r=[PSr(3), (sinb, 0, 512)], w=[(t2, 0, 512)])
                                S.dve(lambda e: e.tensor_tensor(out=t1.ap(), in0=t1.ap(), in1=t2.ap(), op=ALU.add), r=[(t1, 0, 512), (t2, 0, 512)], w=[(t1, 0, 512)])
                                S.dve(lambda e, dst=dst, etab=etab, h=h, hh=hh: e.tensor_tensor(
                                    out=dst.ap(hh * 512, (hh + 1) * 512).rearrange("p (c t) -> p c t", t=128),
                                    in0=t1.ap().rearrange("p (c t) -> p c t", t=128),
                                    in1=etab.ap(h * 128, (h + 1) * 128).unsqueeze(1).to_broadcast([128, 4, 128]), op=ALU.mult),
                                    r=[(t1, 0, 512), (etab, h * 128, (h + 1) * 128)], w=[(dst, hh * 512, (hh + 1) * 512)])
                        tm_dense(wv_in, O_VB + hp * 512, KC, xn_lhs, xn_reg, [0, 1, 2, 3])
                        for tc in range(4):
                            S.act(lambda e, tc=tc: e.activation(out=vtokb.ap(tc * 512, (tc + 1) * 512), in_=PS(tc), func=AF.Copy), r=[PSr(tc)], w=[(vtokb, tc * 512, (tc + 1) * 512)])
                        tm_dense(wv_in, O_GB + hp * 512, KC, xn_lhs, xn_reg, [0, 1, 2, 3])
                        for tc in range(4):
                            S.act(lambda e, tc=tc: e.activation(out=gsb.ap(tc * 512, (tc + 1) * 512), in_=PS(tc), func=AF.Silu), r=[PSr(tc)], w=[(gsb, tc * 512, (tc + 1) * 512)])
                        for hh in range(2):
                            h = hp * 2 + hh
                            ssl = (h * 256, (h + 1) * 256)
                            S.act(lambda e, ssl=ssl, hh=hh: e.activation(out=Sbfb.ap(hh * 256, (hh + 1) * 256), in_=SB_.ap(*ssl), func=AF.Copy),
                                  r=[(SB_, ssl[0], ssl[1])], w=[(Sbfb, hh * 256, (hh + 1) * 256)])
                            for tc in range(4):
                                sg_ = (hh * 512 + tc * 128, hh * 512 + (tc + 1) * 128)
                                vs = (tc * 512 + hh * 256, tc * 512 + (hh + 1) * 256)
                                mm_group(PS(4, 0, 128), lambda k, sg_=sg_: ktb.ap(*sg_), lambda k, sg_=sg_: qtb.ap(*sg_), 1,
                                         r=[(ktb, sg_[0], sg_[1]), (qtb, sg_[0], sg_[1])], w=[PSr(4, 0, 128)])
                                S.dve(lambda e: e.tensor_tensor(out=PmB.ap(), in0=PS(4, 0, 128), in1=mask01.ap(), op=ALU.mult), r=[PSr(4, 0, 128), (mask01, 0, 128)], w=[(PmB, 0, 128)])
                                S.pe(lambda e, sg_=sg_: e.transpose(out=PS(7, 0, 64).bitcast(BF16), in_=ktb.ap(*sg_), identity=identb.ap()),
                                     r=[(ktb, sg_[0], sg_[1]), (identb, 0, 128)], w=[PSr(7, 0, 64)])
                                S.act(lambda e: e.activation(out=ktokb.ap(), in_=PS(7, 0, 64).bitcast(BF16), func=AF.Copy), r=[PSr(7, 0, 64)], w=[(ktokb, 0, 128)])
                                def emit_ob(e, sg_=sg_, vs=vs, hh=hh):
                                    e.matmul(PS(5, 0, 256), lhsT=PmB.ap(), rhs=vtokb.ap(*vs), start=True, stop=False)
                                    return e.matmul(PS(5, 0, 256), lhsT=qtb.ap(*sg_), rhs=Sbfb.ap(hh * 256, (hh + 1) * 256), start=False, stop=True)
                                S.pe(emit_ob, r=[(PmB, 0, 128), (vtokb, vs[0], vs[1]), (qtb, sg_[0], sg_[1]), (Sbfb, hh * 256, (hh + 1) * 256)], w=[PSr(5, 0, 256)])
                                st0 = 32 + tc * 4
                                S.act(lambda e, st0=st0: e.activation(out=junkfB.ap(0, 256), in_=PS(5, 0, 256), func=AF.Square, accum_out=stat.ap(st0, st0 + 1)),
                                      r=[PSr(5, 0, 256)], w=[(junkfB, 0, 256), (stat, st0, st0 + 1)])
                                S.act(lambda e, st0=st0: e.activation(out=stat.ap(st0 + 1, st0 + 2), in_=stat.ap(st0, st0 + 1), func=AF.Sqrt, bias=EPS, scale=1.0 / DV_B),
                                      r=[(stat, st0, st0 + 1)], w=[(stat, st0 + 1, st0 + 2)])
                                S.dve(lambda e, st0=st0: e.reciprocal(out=stat.ap(st0 + 2, st0 + 3), in_=stat.ap(st0 + 1, st0 + 2)), r=[(stat, st0 + 1, st0 + 2)], w=[(stat, st0 + 2, st0 + 3)])
                                S.dve(lambda e, st0=st0, vs=vs: e.scalar_tensor_tensor(out=otokb.ap(), in0=PS(5, 0, 256), scalar=stat.ap(st0 + 2, st0 + 3), in1=gsb.ap(*vs),
                                                                                      op0=ALU.mult, op1=ALU.mult),
                                      r=[PSr(5, 0, 256), (stat, st0 + 2, st0 + 3), (gsb, vs[0], vs[1])], w=[(otokb, 0, 256)])
                                def emit_otb(e):
                                    ins = None
                                    for j in range(2):
                                        ins = e.transpose(out=PS(7, 256, 384).bitcast(BF16)[:, j * 128:(j + 1) * 128], in_=otokb.ap(j * 128, (j + 1) * 128), identity=identb.ap())
                                    return ins
                                S.pe(emit_otb, r=[(otokb, 0, 256), (identb, 0, 128)], w=[PSr(7, 256, 384)])
                                oc0 = 16 + h * 2
                                S.act(lambda e, oc0=oc0, tc=tc: e.activation(out=oT.ap(oc0 * NT, (oc0 + 2) * NT).rearrange("p (j t) -> p j t", t=NT)[:, :, tc * 128:(tc + 1) * 128],
                                                                            in_=PS(7, 256, 384).bitcast(BF16).rearrange("p (j t) -> p j t", t=128), func=AF.Copy),
                                      r=[PSr(7, 256, 384)], w=[(oT, oc0 * NT, (oc0 + 2) * NT)])
                                mm_group(PS(6, 0, 256), lambda k: ktokb.ap(), lambda k, vs=vs: vtokb.ap(*vs), 1, r=[(ktokb, 0, 128), (vtokb, vs[0], vs[1])], w=[PSr(6, 0, 256)])
                                S.dve(lambda e, ssl=ssl, h=h: e.tensor_scalar(out=SB_.ap(*ssl), in0=SB_.ap(*ssl), scalar1=gam.ap(h, h + 1), scalar2=None, op0=ALU.mult),
                                      r=[(SB_, ssl[0], ssl[1]), (gam, h, h + 1)], w=[(SB_, ssl[0], ssl[1])])
                                S.dve(lambda e, ssl=ssl, h=h: e.scalar_tensor_tensor(out=SB_.ap(*ssl), in0=PS(6, 0, 256), scalar=gam.ap(h, h + 1), in1=SB_.ap(*ssl), op0=ALU.mult, op1=ALU.add),
                                      r=[PSr(6, 0, 256), (SB_, ssl[0], ssl[1]), (gam, h, h + 1)], w=[(SB_, ssl[0], ssl[1])])
                                S.act(lambda e, ssl=ssl, hh=hh: e.activation(out=Sbfb.ap(hh * 256, (hh + 1) * 256), in_=SB_.ap(*ssl), func=AF.Copy),
                                      r=[(SB_, ssl[0], ssl[1])], w=[(Sbfb, hh * 256, (hh + 1) * 256)])
                    if ti == n_tiles - 1:
                        S.dma("sp", ret_p[l].rearrange("h p v -> p h v"), SB_.ap().rearrange("p (h v) -> p h v", v=256), r=[(SB_, 0, 2048)], w=[("dram_ret_p", l, l + 1)])

                    ck('mixB')
                    arena.reset(tmp_base)
                    qTc = arena.alloc(BF16, 12 * 512)
                    c2_base = arena.off
                    kf = arena.alloc(F32, 1024)
                    k16 = arena.alloc(BF16, 1024)
                    kTs = arena.alloc(BF16, 1024)
                    for g in range(3):
                        W = WINDOWS[g]
                        keep_lo = SEQ - W
                        for i in range(4):
                            bk = 4 + (i % 2)
                            fm_dense(wv_in, O_HC + g * 1536 + i * 128, 128, KC, xn_fn, xn_reg, bk)
                            qs = ((g * 4 + i) * 512, (g * 4 + i + 1) * 512)
                            S.act(lambda e, qs=qs, bk=bk: e.activation(out=qTc.ap(*qs), in_=PS(bk), func=AF.Copy, scale=128 ** -0.5), r=[PSr(bk)], w=[(qTc, qs[0], qs[1])])
                        for which in (0, 1):
                            tm_dense(wv_in, O_HC + g * 1536 + 512 * (which + 1), KC, xn_lhs, xn_reg, [0, 1, 2, 3])
                            for tc in range(4):
                                tok0 = t0 + tc * 128
                                rb = (tc % 2) * 512
                                S.act(lambda e, tc=tc, rb=rb: e.activation(out=kf.ap(rb, rb + 512), in_=PS(tc), func=AF.Copy), r=[PSr(tc)], w=[(kf, rb, rb + 512)])
                                if tok0 >= keep_lo:
                                    S.dma("sp", wp[g][l, which, tok0 - keep_lo:tok0 - keep_lo + 128, :], kf.ap(rb, rb + 512), r=[(kf, rb, rb + 512)],
                                          w=[("dram_wp%d" % g, (l * 2 + which) * SEQ + tok0, (l * 2 + which) * SEQ + tok0 + 128)])
                                S.dve(lambda e, rb=rb: e.tensor_copy(out=k16.ap(rb, rb + 512), in_=kf.ap(rb, rb + 512)), r=[(kf, rb, rb + 512)], w=[(k16, rb, rb + 512)])
                                if which == 0:
                                    def emit_kT(e, rb=rb):
                                        ins = None
                                        for i in range(4):
                                            ins = e.transpose(out=PS(7, 0, 256).bitcast(BF16)[:, i * 128:(i + 1) * 128], in_=k16.ap(rb + i * 128, rb + (i + 1) * 128), identity=identb.ap())
                                        return ins
                                    S.pe(emit_kT, r=[(k16, rb, rb + 512), (identb, 0, 128)], w=[PSr(7, 0, 256)])
                                    S.act(lambda e, rb=rb: e.activation(out=kTs.ap(rb, rb + 512), in_=PS(7, 0, 256).bitcast(BF16), func=AF.Copy), r=[PSr(7, 0, 256)], w=[(kTs, rb, rb + 512)])
                                    S.dma("sp", kts[g].rearrange("p (i t) -> p i t", t=SEQ)[:, :, tok0:tok0 + 128], kTs.ap(rb, rb + 512).rearrange("p (i t) -> p i t", t=128),
                                          r=[(kTs, rb, rb + 512)], w=[("dram_kts%d" % g, tok0, tok0 + 128)])
                                else:
                                    S.dma("sp", vsc[g][tok0:tok0 + 128, :], k16.ap(rb, rb + 512), r=[(k16, rb, rb + 512)], w=[("dram_vsc%d" % g, tok0, tok0 + 128)])
                    ck('mixC1')
                    arena.reset(c2_base)
                    s_all = arena.alloc(F32, 3072)
                    p_all = arena.alloc(BF16, 3072)
                    pT = arena.alloc(BF16, 2048)
                    KTw = arena.alloc(BF16, 3712)
                    Vw = arena.alloc(BF16, 3712)
                    octok = arena.alloc(BF16, 128)
                    koff = (0, 640, 1664)
                    for i in range(4):
                        los = []
                        for g in range(3):
                            lo_g = max(0, t0 - WINDOWS[g])
                            n_g = t0 + NT - lo_g
                            los.append(lo_g)
                            S.dma("sp", KTw.ap(koff[g], koff[g] + n_g), kts[g][:, i * SEQ + lo_g:i * SEQ + t0 + NT], r=[("dram_kts%d" % g, lo_g, t0 + NT)], w=[(KTw, koff[g], koff[g] + n_g)])
                            S.dma("sp", Vw.ap(koff[g], koff[g] + n_g).rearrange("p (b d) -> p b d", d=128),
                                  vsc[g][lo_g:t0 + NT, i * 128:(i + 1) * 128].rearrange("(b s) d -> s b d", s=128),
                                  r=[("dram_vsc%d" % g, lo_g, t0 + NT)], w=[(Vw, koff[g], koff[g] + n_g)])
                        for qb in range(4):
                            q0 = t0 + qb * 128
                            col = 0
                            blocks = []
                            bankrot = 0
                            for g in range(3):
                                W = WINDOWS[g]
                                klo = max(0, q0 - W)
                                n = q0 + 128 - klo
                                tj_lo = TJ_OFF[g] + (klo - (q0 - W))
                                kbase = koff[g] + (klo - los[g])
                                qs = ((g * 4 + i) * 512 + qb * 128, (g * 4 + i) * 512 + (qb + 1) * 128)
                                coef = -alibi_slope(g, i) * DILS[g]
                                for pc in range(0, n, 512):
                                    pn = min(512, n - pc)
                                    bk = bankrot % 4
                                    bankrot += 1
                                    mm_group(PS(bk, 0, pn), lambda k, qs=qs: qTc.ap(*qs), lambda k, kbase=kbase, pc=pc, pn=pn: KTw.ap(kbase + pc, kbase + pc + pn), 1,
                                             r=[(qTc, qs[0], qs[1]), (KTw, kbase + pc, kbase + pc + pn)], w=[PSr(bk, 0, pn)])
                                    S.dve(lambda e, bk=bk, pn=pn, col=col, pc=pc, tj_lo=tj_lo, coef=coef: e.scalar_tensor_tensor(
                                        out=s_all.ap(col + pc, col + pc + pn), in0=tjb.ap(tj_lo + pc, tj_lo + pc + pn), scalar=coef, in1=PS(bk, 0, pn), op0=ALU.mult, op1=ALU.add),
                                        r=[(tjb, tj_lo + pc, tj_lo + pc + pn), PSr(bk, 0, pn)], w=[(s_all, col + pc, col + pc + pn)])
                                for kb in range(n // 128):
                                    blocks.append((col + kb * 128, kbase + kb * 128))
                                col += n
                            S.dve(lambda e, col=col: e.tensor_reduce(out=stat.ap(48, 49), in_=s_all.ap(0, col), axis=AX.X, op=ALU.max, negate=True), r=[(s_all, 0, col)], w=[(stat, 48, 49)])
                            S.act(lambda e, col=col: e.activation(out=p_all.ap(0, col), in_=s_all.ap(0, col), func=AF.Exp, bias=stat.ap(48, 49), scale=1.0, accum_out=stat.ap(49, 50)),
                                  r=[(s_all, 0, col), (stat, 48, 49)], w=[(p_all, 0, col), (stat, 49, 50)])
                            S.dve(lambda e: e.reciprocal(out=stat.ap(50, 51), in_=stat.ap(49, 50)), r=[(stat, 49, 50)], w=[(stat, 50, 51)])
                            nb = len(blocks)
                            for b8 in range(0, nb, 8):
                                bn = min(8, nb - b8)
                                half = (b8 // 8) % 2
                                def emit_pt(e, b8=b8, bn=bn, blocks=blocks):
                                    ins = None
                                    for j in range(bn):
                                        pc0 = blocks[b8 + j][0]
                                        ins = e.transpose(out=PS(7).bitcast(BF16)[:, j * 128:(j + 1) * 128], in_=p_all.ap(pc0, pc0 + 128), identity=identb.ap())
                                    return ins
                                S.pe(emit_pt, r=[(p_all, blocks[b8][0], blocks[b8 + bn - 1][0] + 128), (identb, 0, 128)], w=[PSr(7, 0, bn * 64)])
                                pts = (half * 1024, half * 1024 + bn * 128)
                                if half == 0:
                                    S.dve(lambda e, pts=pts, bn=bn: e.tensor_copy(out=pT.ap(*pts), in_=PS(7, 0, bn * 64).bitcast(BF16)), r=[PSr(7, 0, bn * 64)], w=[(pT, pts[0], pts[1])])
                                else:
                                    S.act(lambda e, pts=pts, bn=bn: e.activation(out=pT.ap(*pts), in_=PS(7, 0, bn * 64).bitcast(BF16), func=AF.Copy), r=[PSr(7, 0, bn * 64)], w=[(pT, pts[0], pts[1])])
                                def emit_pv(e, b8=b8, bn=bn, half=half, nb=nb, blocks=blocks):
                                    ins = None
                                    for j in range(bn):
                                        vb0 = blocks[b8 + j][1]
                                        ins = e.matmul(PS(6, 0, 128), lhsT=pT.ap(half * 1024 + j * 128, half * 1024 + (j + 1) * 128), rhs=Vw.ap(vb0, vb0 + 128),
                                                       start=(b8 + j == 0), stop=(b8 + j == nb - 1))
                                    return ins
                                S.pe(emit_pv, r=[(pT, pts[0], pts[1]), (Vw, 0, 3712)], w=[PSr(6, 0, 128)])
                            S.act(lambda e: e.activation(out=octok.ap(), in_=PS(6, 0, 128), func=AF.Copy, scale=stat.ap(50, 51)), r=[PSr(6, 0, 128), (stat, 50, 51)], w=[(octok, 0, 128)])
                            S.pe(lambda e: e.transpose(out=PS(5, 256, 320).bitcast(BF16), in_=octok.ap(), identity=identb.ap()), r=[(octok, 0, 128), (identb, 0, 128)], w=[PSr(5, 256, 320)])
                            od = ((32 + i) * NT + qb * 128, (32 + i) * NT + (qb + 1) * 128)
                            S.dve(lambda e, od=od: e.tensor_copy(out=oT.ap(*od), in_=PS(5, 256, 320).bitcast(BF16)), r=[PSr(5, 256, 320)], w=[(oT, od[0], od[1])])

                    ck('mixC')
                    arena.reset(tmp_base + 32 * NT * 2)
                    sg = [arena.alloc(F32, 512) for _ in range(3)]
                    acc = arena.alloc(F32, 512)
                    wv_mg = wview(w_merge[l])
                    wv_ups = [(wview(w_up_a[l]), 16, 0), (wview(w_up_b[l]), 16, 16), (wview(w_up_c[l]), 4, 32)]
                    oT_reg = (oT, 0, 36 * NT)
                    for j in range(KC):
                        for b in range(3):
                            fm_dense(wv_mg, b * D + j * 128, 128, KC, xn_fn, xn_reg, b, samp=((xsT, 6, b) if do_s else None))
                            S.act(lambda e, b=b: e.activation(out=sg[b].ap(), in_=PS(b), func=AF.Sigmoid), r=[PSr(b)], w=[(sg[b], 0, 512)])
                        for b, (wvu, kn, k0) in enumerate(wv_ups):
                            ocols = Buf(oTs.t, oTs.name, oTs.tdt, BF16, oTs.base + k0 * 2, kn)
                            fm_dense(wvu, j * 128, 128, kn, (lambda k, k0=k0: oT.ap((k0 + k) * NT, (k0 + k + 1) * NT)), oT_reg, 3 + b, samp=((ocols, 6, 3 + b) if do_s else None))
                        S.dve(lambda e: e.tensor_tensor(out=acc.ap(), in0=PS(3), in1=sg[0].ap(), op=ALU.mult), r=[PSr(3), (sg[0], 0, 512)], w=[(acc, 0, 512)])
                        S.dve(lambda e: e.tensor_tensor(out=sg[1].ap(), in0=PS(4), in1=sg[1].ap(), op=ALU.mult), r=[PSr(4), (sg[1], 0, 512)], w=[(sg[1], 0, 512)])
                        S.dve(lambda e: e.tensor_tensor(out=sg[2].ap(), in0=PS(5), in1=sg[2].ap(), op=ALU.mult), r=[PSr(5), (sg[2], 0, 512)], w=[(sg[2], 0, 512)])
                        S.dve(lambda e: e.tensor_tensor(out=acc.ap(), in0=acc.ap(), in1=sg[1].ap(), op=ALU.add), r=[(acc, 0, 512), (sg[1], 0, 512)], w=[(acc, 0, 512)])
                        S.dve(lambda e, j=j: e.tensor_tensor(out=mT.ap(j * NT, (j + 1) * NT), in0=acc.ap(), in1=sg[2].ap(), op=ALU.add),
                              r=[(acc, 0, 512), (sg[2], 0, 512)], w=[(mT, j * NT, (j + 1) * NT)])
                        if do_s:
                            S.act(lambda e: e.activation(out=sgs.ap(0, 3), in_=PS(6, 0, 3), func=AF.Sigmoid), r=[PSr(6)], w=[(sgs, 0, 3)])
                            S.dve(lambda e: e.tensor_tensor(out=sgs.ap(0, 3), in0=PS(6, 3, 6), in1=sgs.ap(0, 3), op=ALU.mult), r=[PSr(6), (sgs, 0, 3)], w=[(sgs, 0, 3)])
                            S.dve(lambda e: e.tensor_tensor(out=sgs.ap(0, 1), in0=sgs.ap(0, 1), in1=sgs.ap(1, 2), op=ALU.add), r=[(sgs, 0, 3)], w=[(sgs, 0, 1)])
                            S.dve(lambda e, j=j: e.tensor_tensor(out=mTs.ap(j, j + 1), in0=sgs.ap(0, 1), in1=sgs.ap(2, 3), op=ALU.add), r=[(sgs, 0, 3)], w=[(mTs, j, j + 1)])
                    wv_o = wview(w_out[l])
                    resid_phase(wv_o, KC, (lambda k, tc: mT.ap(k * NT + tc * 128, k * NT + (tc + 1) * 128)), (mT, 0, KC * NT), [0, 1, 2, 3], x_src, xa, t0,
                                samp=((mTs, 6, (xs if l == 0 else xsb), xsa) if do_s else None))
                    if do_s:
                        sample_norm_cols(xsa, gffn, xsT)

                    ck('merge')
                    arena.reset()
                    rowbuf = arena.alloc(F32, D)
                    xnjunk = arena.alloc(BF16, D)
                    norm_tile(xa, t0, gffn, rowbuf)
                    arena.reset()
                    hT = arena.alloc(BF16, FKC * NT)
                    wv_g, wv_u = wview(w_fg[l]), wview(w_fu[l])
                    for j in range(FKC):
                        bg, bu = (j % 2) * 2, (j % 2) * 2 + 1
                        fm_dense(wv_g, j * 128, 128, KC, xn_fn, xn_reg, bg, samp=((xsT, 4, 0) if do_s else None))
                        fm_dense(wv_u, j * 128, 128, KC, xn_fn, xn_reg, bu, samp=((xsT, 4, 1) if do_s else None))
                        if do_s:
                            S.act(lambda e: e.activation(out=sgs.ap(4, 5), in_=PS(4, 0, 1), func=AF.Silu), r=[PSr(4)], w=[(sgs, 4, 5)])
                            S.dve(lambda e, j=j: e.tensor_tensor(out=hTs.ap(j, j + 1), in0=PS(4, 1, 2), in1=sgs.ap(4, 5), op=ALU.mult), r=[PSr(4), (sgs, 4, 5)], w=[(hTs, j, j + 1)])
                        S.act(lambda e, bg=bg: e.activation(out=sgf.ap(), in_=PS(bg), func=AF.Silu), r=[PSr(bg)], w=[(sgf, 0, 512)])
                        S.dve(lambda e, bu=bu, j=j: e.tensor_tensor(out=hT.ap(j * NT, (j + 1) * NT), in0=PS(bu), in1=sgf.ap(), op=ALU.mult),
                              r=[PSr(bu), (sgf, 0, 512)], w=[(hT, j * NT, (j + 1) * NT)])
                    resid_phase(wview(w_fd[l]), FKC, (lambda k, tc: hT.ap(k * NT + tc * 128, k * NT + (tc + 1) * 128)), (hT, 0, FKC * NT), [4, 5, 6, 7], xa, xb, t0,
                                samp=((hTs, 0, xsa, xsb) if do_s else None))

            ck('ffn')
            arena.reset()
            rowbuf = arena.alloc(F32, D)
            xnjunk = arena.alloc(BF16, D)
            gfb = arena.alloc(F32, D)
            S.dma("sp", gfb.ap(), g_final_row.partition_broadcast(128), w=[(gfb, 0, D)])
            S.dma("sp", rowbuf.ap(0, D, 0, 1), xsb, r=[("dram_xsb", 0, 1)], w=[(rowbuf, 0, D)])
            S.act(lambda e: e.activation(out=xnjunk.ap(0, D, 0, 1), in_=rowbuf.ap(0, D, 0, 1), func=AF.Square, accum_out=stat.ap(0, 1, 0, 1)), r=[(rowbuf, 0, D)], w=[(xnjunk, 0, D), (stat, 0, 1)])
            S.act(lambda e: e.activation(out=stat.ap(1, 2, 0, 1), in_=stat.ap(0, 1, 0, 1), func=AF.Sqrt, bias=EPS, scale=1.0 / D), r=[(stat, 0, 1)], w=[(stat, 1, 2)])
            S.dve(lambda e: e.reciprocal(out=stat.ap(2, 3, 0, 1), in_=stat.ap(1, 2, 0, 1)), r=[(stat, 1, 2)], w=[(stat, 2, 3)])
            S.dve(lambda e: e.scalar_tensor_tensor(out=rowbuf.ap(0, D, 0, 1), in0=rowbuf.ap(0, D, 0, 1), scalar=stat.ap(2, 3, 0, 1), in1=gfb.ap(0, D, 0, 1), op0=ALU.mult, op1=ALU.mult),
                  r=[(rowbuf, 0, D), (stat, 2, 3), (gfb, 0, D)], w=[(rowbuf, 0, D)])
            S.dma("sp", ys, rowbuf.ap(0, D, 0, 1), r=[(rowbuf, 0, D)], w=[("dram_ys", 0, 1)])
            for r0 in range(0, n_tiles * NT, 128):
                S.dma("sp", rowbuf.ap(), xb[r0:r0 + 128, :], r=[("dram_xb", r0, r0 + 128)], w=[(rowbuf, 0, D)])
                S.act(lambda e: e.activation(out=xnjunk.ap(), in_=rowbuf.ap(), func=AF.Square, accum_out=stat.ap(0, 1)), r=[(rowbuf, 0, D)], w=[(xnjunk, 0, D), (stat, 0, 1)])
                S.act(lambda e: e.activation(out=stat.ap(1, 2), in_=stat.ap(0, 1), func=AF.Sqrt, bias=EPS, scale=1.0 / D), r=[(stat, 0, 1)], w=[(stat, 1, 2)])
                S.dve(lambda e: e.reciprocal(out=stat.ap(2, 3), in_=stat.ap(1, 2)), r=[(stat, 1, 2)], w=[(stat, 2, 3)])
                S.dve(lambda e: e.scalar_tensor_tensor(out=rowbuf.ap(), in0=rowbuf.ap(), scalar=stat.ap(2, 3), in1=gfb.ap(), op0=ALU.mult, op1=ALU.mult),
                      r=[(rowbuf, 0, D), (stat, 2, 3), (gfb, 0, D)], w=[(rowbuf, 0, D)])
                S.dma("sp", yp[r0:r0 + 128, :], rowbuf.ap(), r=[(rowbuf, 0, D)], w=[("dram_yp", r0, r0 + 128)])

        except _Stop:
            pass
        S.replay(block)
        nc_stats = (dict(S.cnt), list(S.sp_cnt), list(S.pool_cnt))
    build_program.stats = nc_stats
    return nc


_NC_CACHE = {}


def kernel(**inputs):
    n = 8
    if "nc" not in _NC_CACHE:
        _NC_CACHE["nc"] = build_program()
    nc = _NC_CACHE["nc"]
    consts = make_consts()
    f = lambda a: np.ascontiguousarray(np.asarray(a, dtype=np.float32))
    shared = {k: f(inputs[k]) for k in ("w_in", "w_alpha2", "g_gla", "w_merge", "w_up_a", "w_up_b", "w_up_c", "w_out",
                                       "w_ffn_gate", "w_ffn_up", "w_ffn_down")}
    shared["g_mix_t"] = f(np.asarray(inputs["g_mix"]).reshape(DEPTH, KC, 128).transpose(0, 2, 1))
    shared["g_ffn_t"] = f(np.asarray(inputs["g_ffn"]).reshape(DEPTH, KC, 128).transpose(0, 2, 1))
    shared["g_final_t"] = f(np.asarray(inputs["g_final"]).reshape(KC, 128).T)
    shared["g_final_row"] = f(inputs["g_final"])
    shared["g_mix_r"] = f(inputs["g_mix"])
    shared["g_ffn_r"] = f(inputs["g_ffn"])
    shared["b_alpha_r"] = f(inputs["b_alpha"])
    shared["b_alpha_t"] = f(np.asarray(inputs["b_alpha"]).reshape(DEPTH, 8, 128).transpose(0, 2, 1))
    for k, v in consts.items():
        shared["c_" + k] = v
    in_maps = []
    for c in range(n):
        m = dict(shared)
        m["xp"] = f(inputs["x_prompt"][c % 4])
        m["xs"] = f(inputs["x_sample"][c])
        m["sgla"] = f(inputs["state_gla"][:, c])
        m["sret"] = f(inputs["state_ret"][:, c])
        for g in range(3):
            cwg = np.asarray(inputs["cache_win%d" % g])[:, c]
            m["cw%d" % g] = f(cwg.reshape(DEPTH, 2, WINDOWS[g], 512))
        in_maps.append(m)
    res = run_bass_kernel_spmd(nc, in_maps, core_ids=list(range(n)))
    R = res.results
    y_prompt = np.stack([R[b]["yp"] for b in range(4)])
    y_sample = np.stack([R[c]["ys"] for c in range(8)])
    outs = [y_prompt, y_sample]
    for nm in ("gla", "ret"):
        outs.append(np.stack([R[b][nm + "_p"] for b in range(4)], axis=1))
        outs.append(np.stack([R[c][nm + "_s"] for c in range(8)], axis=1))
    for g in range(3):
        W = WINDOWS[g]
        outs.append(np.stack([R[b]["w%dp" % g] for b in range(4)], axis=1).reshape(DEPTH, 4, 2, W, 4, 128))
        outs.append(np.stack([R[c]["w%ds" % g] for c in range(8)], axis=1).reshape(DEPTH, 8, 2, W, 4, 128))
    return tuple(np.ascontiguousarray(o, dtype=np.float32) for o in outs)
```
